# Optimizing a Trainium2 kernel written in Bass

```python
import jax, jax.numpy as jnp
from jax import lax
import numpy as np

D_MODEL = 4096
BATCH = 32
SEQ = 256
DEPTH = 2
DEC_BATCH = 4
DEC_SEQ = 4096
PAST_LEN = 256

GRID_W = 64
HEAD_DIM = 128
N_HEAD_SLOTS = D_MODEL // HEAD_DIM
RET_HEADS = N_HEAD_SLOTS // 4
HG_HEADS = N_HEAD_SLOTS // 4
ATTN_HEADS = N_HEAD_SLOTS - RET_HEADS - HG_HEADS
KV_HEADS = ATTN_HEADS // 4
RET_W = RET_HEADS * HEAD_DIM
HG_W = HG_HEADS * HEAD_DIM
ATTN_W = ATTN_HEADS * HEAD_DIM
KV_W = KV_HEADS * HEAD_DIM
N_IN = 4 * RET_W + 5 * HG_W + ATTN_W + 2 * KV_W
FFN_DIM = 256 * (-(-8 * D_MODEL // (3 * 256)))
CHUNK = 64
SUB_CHUNK = 16
Q_BLOCK = 128
ROPE_THETA = 10000.0
EPS = 1e-6

kernel_name = "hybrid_retention_hgrn2_gqa_diffusion_step"

F32 = jnp.float32


def rms_norm(x, g):
    x32 = x.astype(F32)
    y = x32 * lax.rsqrt(jnp.mean(x32 * x32, axis=-1, keepdims=True) + EPS)
    return (y * g.astype(F32)).astype(x.dtype)


def adaln(cond, w_mod, b_mod):
    m = jax.nn.silu(cond) @ w_mod + b_mod
    return jnp.split(m[:, None, :], 6, axis=-1)


def axial_rope(n_tok):
    rows = n_tok // GRID_W
    row = jnp.repeat(jnp.arange(rows), GRID_W).astype(F32)
    col = jnp.tile(jnp.arange(GRID_W), rows).astype(F32)
    half = HEAD_DIM // 2
    inv = ROPE_THETA ** (-jnp.arange(0, half, 2, dtype=F32) / half)
    ang_r = row[:, None] * inv
    ang_c = col[:, None] * inv
    return (jnp.cos(ang_r), jnp.sin(ang_r), jnp.cos(ang_c), jnp.sin(ang_c))


def _rot(x, cos, sin):
    x1, x2 = jnp.split(x, 2, axis=-1)
    c = cos[None, :, None, :]
    s = sin[None, :, None, :]
    return jnp.concatenate([x1 * c - x2 * s, x2 * c + x1 * s], axis=-1)


def apply_axial_rope(x, rope):
    cr, sr, cc, sc = rope
    xr, xc = jnp.split(x.astype(F32), 2, axis=-1)
    return jnp.concatenate([_rot(xr, cr, sr), _rot(xc, cc, sc)], axis=-1).astype(x.dtype)


def to_chunks(x):
    b, t, h, d = x.shape
    return x.reshape(b, t // CHUNK, CHUNK, h, d).transpose(1, 0, 3, 2, 4)


def from_chunks(x):
    n, b, h, c, d = x.shape
    return x.transpose(1, 0, 3, 2, 4).reshape(b, n * c, h, d)


def retention_scan(q, k, v, log_gamma, s0):
    idx = jnp.arange(CHUNK, dtype=F32)
    diff = idx[:, None] - idx[None, :]
    lg = log_gamma[:, None, None]
    decay_intra = jnp.where(diff >= 0, jnp.exp(lg * jnp.maximum(diff, 0.0)), 0.0)
    q_decay = jnp.exp(log_gamma[:, None] * (idx + 1.0))
    k_decay = jnp.exp(log_gamma[:, None] * (CHUNK - 1.0 - idx))
    chunk_decay = jnp.exp(log_gamma * CHUNK)

    def body(S, xs):
        qc, kc, vc = xs
        a = jnp.einsum('bhid,bhjd->bhij', qc, kc) * decay_intra
        o = jnp.einsum('bhij,bhje->bhie', a, vc)
        o = o + jnp.einsum('bhid,bhde->bhie', qc * q_decay[None, :, :, None], S)
        S = chunk_decay[None, :, None, None] * S + jnp.einsum('bhjd,bhje->bhde', kc * k_decay[None, :, :, None], vc)
        return S, o

    s_last, o = lax.scan(body, s0.astype(F32), (to_chunks(q), to_chunks(k), to_chunks(v)))
    return from_chunks(o), s_last


def gla_scan(q, k, v, g, s0):
    n_sub = CHUNK // SUB_CHUNK
    si = jnp.arange(SUB_CHUNK)
    tril = si[:, None] >= si[None, :]
    pi = jnp.arange(n_sub)
    off_mask = pi[:, None] > pi[None, :]
    eye = jnp.eye(n_sub, dtype=F32)

    def body(S, xs):
        qc, kc, vc, gc = xs
        b = jnp.cumsum(gc, axis=2)
        o = jnp.einsum('bhck,bhkv->bhcv', qc * jnp.exp(b), S)
        bsz, nh = qc.shape[0], qc.shape[1]
        sh = (bsz, nh, n_sub, SUB_CHUNK, -1)
        qs, ks, vs, bs, gs = [z.reshape(sh) for z in (qc, kc, vc, b, gc)]
        r_ref = bs[:, :, :, 0, :] - gs[:, :, :, 0, :]
        q_off = qs * jnp.exp(bs - r_ref[:, :, :, None, :])
        k_off = ks[:, :, None] * jnp.exp(jnp.minimum(r_ref[:, :, :, None, None, :] - bs[:, :, None], 0.0))
        a_off = jnp.einsum('bhpik,bhprjk->bhpirj', q_off, k_off)
        m_ref = bs[:, :, :, SUB_CHUNK // 2, :]
        q_d = qs * jnp.exp(bs - m_ref[:, :, :, None, :])
        k_d = ks * jnp.exp(m_ref[:, :, :, None, :] - bs)
        a_d = jnp.where(tril, jnp.einsum('bhpik,bhpjk->bhpij', q_d, k_d), 0.0)
        a = jnp.where(off_mask[:, None, :, None], a_off, 0.0) + jnp.einsum('bhpij,pr->bhpirj', a_d, eye)
        o = o + jnp.einsum('bhpirj,bhrjv->bhpiv', a, vs).reshape(o.shape)
        b_last = b[:, :, -1:, :]
        S = jnp.exp(b_last[:, :, 0, :, None]) * S + jnp.einsum('bhck,bhcv->bhkv', kc * jnp.exp(b_last - b), vc)
        return S, o

    s_last, o = lax.scan(body, s0.astype(F32), (to_chunks(q), to_chunks(k), to_chunks(v), to_chunks(g)))
    return from_chunks(o), s_last


def _flip(x):
    return jnp.flip(x, axis=1)


def retention_mixer(q, k, v, gate, g_norm, rope, s0_f, s0_b):
    b, t, _ = q.shape
    shp = (b, t, RET_HEADS, HEAD_DIM)
    q, k, v = [z.reshape(shp) for z in (q, k, v)]
    if rope is not None:
        q = apply_axial_rope(q, rope)
        k = apply_axial_rope(k, rope)
    q = q.astype(F32)
    k = k.astype(F32) * (HEAD_DIM ** -0.5)
    v = v.astype(F32)
    lg_f = jnp.log1p(-jnp.exp2(-5.0 - jnp.arange(RET_HEADS, dtype=F32)))
    lg_b = lg_f[::-1]
    o_f, s_f = retention_scan(q, k, v, lg_f, s0_f)
    o_b, s_b = retention_scan(_flip(q), _flip(k), _flip(v), lg_b, s0_b)
    o = o_f + _flip(o_b)
    mu = jnp.mean(o, axis=-1, keepdims=True)
    oc = o - mu
    o = oc * lax.rsqrt(jnp.mean(oc * oc, axis=-1, keepdims=True) + EPS)
    o = o * g_norm.astype(F32).reshape(RET_HEADS, HEAD_DIM)
    o = o.reshape(b, t, RET_W) * jax.nn.silu(gate.astype(F32))
    return o.astype(gate.dtype), s_f, s_b


def hgrn2_mixer(q, z_f, z_b, inp, gate, lb, g_norm, s0_f, s0_b):
    b, t, _ = q.shape
    shp = (b, t, HG_HEADS, HEAD_DIM)
    qh = jax.nn.silu(q.astype(F32)).reshape(shp)
    vh = inp.astype(F32).reshape(shp)

    def forget(z, lbd):
        z = z.astype(F32)
        log_f = jnp.logaddexp(jnp.log(lbd), jnp.log1p(-lbd) + jax.nn.log_sigmoid(z))
        key = (1.0 - lbd) * jax.nn.sigmoid(-z)
        return log_f.reshape(shp), key.reshape(shp)

    lf_f, k_f = forget(z_f, lb[0])
    lf_b, k_b = forget(z_b, lb[1])
    o_f, s_f = gla_scan(qh, k_f, vh, lf_f, s0_f)
    o_b, s_b = gla_scan(_flip(qh), _flip(k_b), _flip(vh), _flip(lf_b), s0_b)
    o = o_f + _flip(o_b)
    o = o * lax.rsqrt(jnp.mean(o * o, axis=-1, keepdims=True) + EPS)
    o = o * g_norm.astype(F32).reshape(HG_HEADS, HEAD_DIM)
    o = o.reshape(b, t, HG_W) * jax.nn.sigmoid(gate.astype(F32))
    return o.astype(gate.dtype), s_f, s_b


def block_attention(q, k, v):
    b, n, h, d = q.shape
    kvh = k.shape[2]
    grp = h // kvh
    qb = q.reshape(b, n // Q_BLOCK, Q_BLOCK, kvh, grp, d).transpose(1, 0, 2, 3, 4, 5)
    scale = d ** -0.5

    def attend(qblk):
        s = jnp.einsum('bqkgd,bskd->bkgqs', qblk, k).astype(F32) * scale
        p = jax.nn.softmax(s, axis=-1).astype(v.dtype)
        return jnp.einsum('bkgqs,bskd->bqkgd', p, v)

    o = lax.map(attend, qb)
    return o.transpose(1, 0, 2, 3, 4, 5).reshape(b, n, h * d)


def attention_mixer(q, k, v, g_q, g_k, rope, ctx_k, ctx_v):
    b, t, _ = q.shape
    q = rms_norm(q.reshape(b, t, ATTN_HEADS, HEAD_DIM), g_q)
    k = rms_norm(k.reshape(b, t, KV_HEADS, HEAD_DIM), g_k)
    v = v.reshape(b, t, KV_HEADS, HEAD_DIM)
    if rope is not None:
        q = apply_axial_rope(q, rope)
        k = apply_axial_rope(k, rope)
    if ctx_k is None:
        k_all, v_all = k, v
    else:
        k_all = jnp.concatenate([ctx_k.astype(k.dtype), k], axis=1)
        v_all = jnp.concatenate([ctx_v.astype(v.dtype), v], axis=1)
    return block_attention(q, k_all, v_all), k, v


def trunk_layer(x, cond, rope, ctx, w_mod, b_mod, g_pre_mix, g_post_mix, w_in, g_q, g_k,
                g_ret, g_hgrn, lb, w_out, g_pre_ffn, g_post_ffn, w_gu, w_down):
    shift_m, scale_m, gate_m, shift_f, scale_f, gate_f = adaln(cond, w_mod, b_mod)
    h = rms_norm(x, g_pre_mix) * (1.0 + scale_m) + shift_m
    proj = h @ w_in
    sizes = [RET_W] * 4 + [HG_W] * 5 + [ATTN_W, KV_W, KV_W]
    (rq, rk, rv, rg, hq, hz_f, hz_b, hi, hg, aq, ak, av) = jnp.split(proj, np.cumsum(sizes)[:-1].tolist(), axis=-1)
    b = x.shape[0]
    if ctx is None:
        zr = jnp.zeros((b, RET_HEADS, HEAD_DIM, HEAD_DIM), F32)
        zh = jnp.zeros((b, HG_HEADS, HEAD_DIM, HEAD_DIM), F32)
        sr_f0, sr_b0, sh_f0, sh_b0, ctx_k, ctx_v = zr, zr, zh, zh, None, None
    else:
        ctx_k, ctx_v, s_ret, s_hg = ctx
        sr_f0, sr_b0, sh_f0, sh_b0 = s_ret[:, 0], s_ret[:, 1], s_hg[:, 0], s_hg[:, 1]
    o_ret, sr_f, sr_b = retention_mixer(rq, rk, rv, rg, g_ret, rope, sr_f0, sr_b0)
    o_hg, sh_f, sh_b = hgrn2_mixer(hq, hz_f, hz_b, hi, hg, lb, g_hgrn, sh_f0, sh_b0)
    o_att, k_new, v_new = attention_mixer(aq, ak, av, g_q, g_k, rope, ctx_k, ctx_v)
    mix = jnp.concatenate([o_ret, o_hg, o_att.astype(o_ret.dtype)], axis=-1) @ w_out
    x = x + gate_m * rms_norm(mix, g_post_mix)
    h = rms_norm(x, g_pre_ffn) * (1.0 + scale_f) + shift_f
    a, u = jnp.split(h @ w_gu, 2, axis=-1)
    y = (jax.nn.silu(a) * u) @ w_down
    x = x + gate_f * rms_norm(y, g_post_ffn)
    if ctx is None:
        return x, (k_new, v_new, jnp.stack([sr_f, sr_b], axis=1), jnp.stack([sh_f, sh_b], axis=1))
    return x, None


def setup_inputs(seed: int = 0) -> dict:
    key = jax.random.key(seed)
    ks = jax.random.split(key, 24)

    def nrm(k, shape, scale):
        return jax.random.normal(k, shape, F32) * scale

    def gain(k, shape):
        return 1.0 + 0.02 * jax.random.normal(k, shape, F32)

    d = D_MODEL
    return {
        "x_prompt": nrm(ks[0], (BATCH, SEQ, d), 1.0),
        "x_sample": nrm(ks[1], (DEC_BATCH, DEC_SEQ, d), 1.0),
        "cache_k": nrm(ks[2], (DEC_BATCH, DEPTH, PAST_LEN, KV_HEADS, HEAD_DIM), 1.0),
        "cache_v": nrm(ks[3], (DEC_BATCH, DEPTH, PAST_LEN, KV_HEADS, HEAD_DIM), 1.0),
        "state_ret": nrm(ks[4], (DEC_BATCH, DEPTH, 2, RET_HEADS, HEAD_DIM, HEAD_DIM), 1.0),
        "state_hgrn": nrm(ks[5], (DEC_BATCH, DEPTH, 2, HG_HEADS, HEAD_DIM, HEAD_DIM), 0.5),
        "c": nrm(ks[6], (DEC_BATCH, d), 1.0),
        "c_ctx": nrm(ks[7], (d,), 1.0),
        "w_mod": nrm(ks[8], (DEPTH, d, 6 * d), d ** -0.5),
        "b_mod": nrm(ks[9], (DEPTH, 6 * d), 0.02),
        "g_pre_mix": gain(ks[10], (DEPTH, d)),
        "g_post_mix": gain(ks[11], (DEPTH, d)),
        "w_in": nrm(ks[12], (DEPTH, d, N_IN), d ** -0.5),
        "g_q": gain(ks[13], (DEPTH, HEAD_DIM)),
        "g_k": gain(ks[14], (DEPTH, HEAD_DIM)),
        "g_ret": gain(ks[15], (DEPTH, RET_W)),
        "g_hgrn": gain(ks[16], (DEPTH, HG_W)),
        "hg_lb": nrm(ks[17], (DEPTH, 2, HG_W), 0.5),
        "w_out": nrm(ks[18], (DEPTH, d, d), d ** -0.5),
        "g_pre_ffn": gain(ks[19], (DEPTH, d)),
        "g_post_ffn": gain(ks[20], (DEPTH, d)),
        "w_gu": nrm(ks[21], (DEPTH, d, 2 * FFN_DIM), d ** -0.5),
        "w_down": nrm(ks[22], (DEPTH, FFN_DIM, d), FFN_DIM ** -0.5),
    }


def reference(x_prompt, x_sample, cache_k, cache_v, state_ret, state_hgrn, c, c_ctx,
              w_mod, b_mod, g_pre_mix, g_post_mix, w_in, g_q, g_k, g_ret, g_hgrn, hg_lb,
              w_out, g_pre_ffn, g_post_ffn, w_gu, w_down):
    lb_all = jnp.cumsum(jax.nn.softmax(hg_lb.astype(F32), axis=0), axis=0)
    lb_all = lb_all - lb_all[0:1]

    cond_ctx = c_ctx[None, :]
    y_prompt = x_prompt
    ks_l, vs_l, sr_l, sh_l = [], [], [], []
    for l in range(DEPTH):
        y_prompt, (k_l, v_l, s_r, s_h) = trunk_layer(
            y_prompt, cond_ctx, None, None, w_mod[l], b_mod[l], g_pre_mix[l], g_post_mix[l], w_in[l],
            g_q[l], g_k[l], g_ret[l], g_hgrn[l], lb_all[l], w_out[l], g_pre_ffn[l], g_post_ffn[l],
            w_gu[l], w_down[l])
        ks_l.append(k_l)
        vs_l.append(v_l)
        sr_l.append(s_r)
        sh_l.append(s_h)

    rope = axial_rope(x_sample.shape[1])
    y_sample = x_sample
    for l in range(DEPTH):
        y_sample, _ = trunk_layer(
            y_sample, c, rope, (cache_k[:, l], cache_v[:, l], state_ret[:, l], state_hgrn[:, l]),
            w_mod[l], b_mod[l], g_pre_mix[l], g_post_mix[l], w_in[l], g_q[l], g_k[l], g_ret[l],
            g_hgrn[l], lb_all[l], w_out[l], g_pre_ffn[l], g_post_ffn[l], w_gu[l], w_down[l])

    new_cache_k = jnp.stack(ks_l, axis=1)
    new_cache_v = jnp.stack(vs_l, axis=1)
    new_state_ret = jnp.stack(sr_l, axis=1)
    new_state_hgrn = jnp.stack(sh_l, axis=1)
    return (y_prompt, y_sample, new_cache_k, new_cache_v, new_state_ret, new_state_hgrn)
```

```python
import math
from contextlib import ExitStack
import numpy as np
import concourse.bass as bass
import concourse.mybir as mybir
from concourse.bass_utils import run_bass_kernel_spmd

F32 = mybir.dt.float32
BF16 = mybir.dt.bfloat16
ALU = mybir.AluOpType
AF = mybir.ActivationFunctionType
EPS = 1e-6
NCORES = 8
DEPTH = 2
SEG = 256
PAST = 256


class Cfg:
    def __init__(self, D=4096, T=4096, stop=None):
        self.D, self.T, self.stop = D, T, stop
        self.KC = D // 128
        self.NS = D // 128
        self.RH = self.HH = self.NS // 4
        self.AH = self.NS // 2
        self.KVH = self.AH // 4
        self.RW, self.HW, self.AW, self.KW = self.RH * 128, self.HH * 128, self.AH * 128, self.KVH * 128
        self.NIN = 4 * self.RW + 5 * self.HW + self.AW + 2 * self.KW
        self.FF = 256 * (-(-8 * D // (3 * 256)))
        self.FC = self.FF // 128
        self.NSEG = T // SEG
        self.NB = T // 128
        o = 0
        self.col = {}
        for nm, w in [("rq", self.RW), ("rk", self.RW), ("rv", self.RW), ("rg", self.RW), ("hq", self.HW),
                      ("hzf", self.HW), ("hzb", self.HW), ("hi", self.HW), ("hg", self.HW), ("aq", self.AW),
                      ("ak", self.KW), ("av", self.KW)]:
            self.col[nm] = (o, w)
            o += w
        self.fm = {}
        i = 0
        for nm in ["rq", "rk", "rg", "hq", "hzf", "hzb", "hg", "aq", "ak"]:
            self.fm[nm] = i
            i += self.col[nm][1] // 128
        self.NFM = i
        self.tm = {}
        o = 0
        for nm in ["rv", "hi", "av", "ak"]:
            self.tm[nm] = o
            o += self.col[nm][1]
        self.NTM = o


class _Op:
    __slots__ = ("eng", "fn", "waits", "signal", "val", "epoch", "dsem")

    def __init__(self, eng, fn, epoch):
        self.eng, self.fn, self.waits, self.signal, self.val, self.epoch, self.dsem = eng, fn, [], False, 0, epoch, None


class Sched:
    CE = ("pe", "act", "dve", "pool")
    ALL = ("pe", "act", "dve", "pool", "sp")
    NDS = 84

    def __init__(self, nc, es):
        self.nc = nc
        self.sem = {e: es.enter_context(nc.semaphore("se_" + e)) for e in self.CE}
        self.dsem = [es.enter_context(nc.semaphore("sd%d" % i)) for i in range(self.NDS)]
        self.dcnt = [0] * self.NDS
        self.dpersist = set()
        self.cnt = {e: 0 for e in self.CE}
        self.ops = {e: [] for e in self.ALL}
        self.lastw, self.rd = {}, {}
        self.k2s = {}
        self.epoch = 0
        self.waited = {e: {} for e in self.ALL}
        self.nins = 0

    def _dep(self, o, ev, is_dma):
        if ev is None:
            return
        if ev[0] == "op":
            p = ev[1]
            if p.epoch < self.epoch:
                return
            if p.eng == o.eng and p.eng == "pe" and not is_dma:
                return
            p.signal = True
            o.waits.append(ev)
        else:
            if ev[3] < self.epoch and ev[1] not in self.dpersist:
                return
            o.waits.append(ev)

    def _track(self, o, reads, writes, ev, is_dma):
        for k in reads:
            self._dep(o, self.lastw.get(k), is_dma)
        for k in writes:
            self._dep(o, self.lastw.get(k), is_dma)
            for e2 in self.rd.get(k, {}).values():
                self._dep(o, e2, is_dma)
        rk = ("e", o.eng) if ev[0] == "op" else ("d", ev[1])
        for k in reads:
            self.rd.setdefault(k, {})[rk] = ev
        for k in writes:
            self.lastw[k] = ev
            self.rd[k] = {}

    def op(self, eng, fn, reads=(), writes=()):
        o = _Op(eng, fn, self.epoch)
        self._track(o, reads, writes, ("op", o), False)
        self.ops[eng].append(o)
        return o

    def dma(self, q, fn, reads=(), writes=(), skey=None, persist=False):
        o = _Op(q, fn, self.epoch)
        if skey not in self.k2s:
            used = set(self.k2s.values()) | self.dpersist
            rng = range(0, 50) if q == "sp" else range(50, self.NDS)
            si = next(i for i in rng if i not in used)
            self.k2s[skey] = si
            if persist:
                self.dpersist.add(si)
        si = self.k2s[skey]
        if self.dcnt[si]:
            o.waits.append(("dma", si, self.dcnt[si], self.epoch))
        self.dcnt[si] += 16
        o.dsem = si
        self._track(o, reads, writes, ("dma", si, self.dcnt[si], self.epoch), True)
        self.ops[q].append(o)
        return o

    def fence(self, final=False):
        lasts = {}
        for e in self.CE:
            if self.ops[e]:
                cands = [o for o in self.ops[e] if o.fn is not None and o.dsem is None]
                if cands:
                    cands[-1].signal = True
                    lasts[e] = cands[-1]
        for e in self.ALL:
            o = _Op(e, None, self.epoch)
            for f, p in lasts.items():
                if f != e:
                    o.waits.append(("op", p))
            for si in range(self.NDS):
                if self.dcnt[si] and (final or si not in self.dpersist):
                    o.waits.append(("dma", si, self.dcnt[si], self.epoch))
            self.ops[e].append(o)
        self.emit()
        self.epoch += 1
        self.lastw = {k: v for k, v in self.lastw.items() if v[0] == "dma" and v[1] in self.dpersist}
        self.rd = {}
        self.k2s = {k: v for k, v in self.k2s.items() if v in self.dpersist}

    def emit(self):
        nc = self.nc
        for e in self.CE:
            for o in self.ops[e]:
                if o.signal:
                    self.cnt[e] += 1
                    o.val = self.cnt[e]

        def run(ename, eng):
            wd = self.waited[ename]
            for o in self.ops[ename]:
                for ev in o.waits:
                    if ev[0] == "op":
                        s, v = self.sem[ev[1].eng], ev[1].val
                        key = ev[1].eng
                    else:
                        s, v = self.dsem[ev[1]], ev[2]
                        key = ev[1]
                    if wd.get(key, 0) >= v:
                        continue
                    wd[key] = v
                    eng.wait_ge(s, v)
                if o.fn is None:
                    continue
                ins = o.fn(eng)
                self.nins += 1
                if o.dsem is not None:
                    ins.then_inc(self.dsem[o.dsem], 16)
                elif o.signal:
                    ins.then_inc(self.sem[ename], 1)

        with nc.Block() as block:
            @block.tensor
            def _(t):
                run("pe", t)

            @block.scalar
            def _(a):
                run("act", a)

            @block.vector
            def _(v):
                run("dve", v)

            @block.gpsimd
            def _(g):
                run("pool", g)

            @block.sync
            def _(s):
                run("sp", s)
        self.ops = {e: [] for e in self.ALL}


class Ring:
    def __init__(self, name, tiles):
        self.name, self.tiles, self.i = name, tiles, 0

    def next(self):
        i = self.i % len(self.tiles)
        self.i += 1
        return self.tiles[i], (self.name, i)


def build(cfg):
    D, T, KC, NS, RH, HH, AH, KVH = cfg.D, cfg.T, cfg.KC, cfg.NS, cfg.RH, cfg.HH, cfg.AH, cfg.KVH
    FF, FC, NIN, KW, NB, NSEG = cfg.FF, cfg.FC, cfg.NIN, cfg.KW, cfg.NB, cfg.NSEG
    nc = bass.Bass("TRN2", target_bir_lowering=False)
    es = ExitStack()

    def din(name, shape, dt=F32):
        return nc.dram_tensor(name, list(shape), dt, kind="ExternalInput").ap()

    def dout(name, shape):
        return nc.dram_tensor(name, list(shape), F32, kind="ExternalOutput").ap()

    def dint(name, shape, dt=F32):
        return nc.dram_tensor(name, list(shape), dt, kind="Internal").ap()

    x_in = din("x", [T, D])
    cond_in = din("cond", [1, D])
    ctxk_in = din("ctxk", [DEPTH, PAST, KW])
    ctxv_in = din("ctxv", [DEPTH, PAST, KW])
    s0_in = din("s0", [DEPTH, 2, 2 * RH, 128, 128])
    keep_in = din("keep", [128, 1])
    maskb_in = din("maskb", [128, NSEG * (2 + NB)])
    cos_in = din("cosT", [128, T])
    sin_in = din("sinT", [128, T])
    cst_in = din("cst", [128, 7 * 128])
    lgam_in = din("lgam", [128, 2 * RH])
    wsh = {
        "mod": din("w_mod", [DEPTH, D, 6 * D]), "in": din("w_in", [DEPTH, D, NIN]),
        "out": din("w_out", [DEPTH, D, D]), "gu": din("w_gu", [DEPTH, D, 2 * FF]),
        "down": din("w_down", [DEPTH, FF, D]),
    }
    wshape = {"mod": (D, 6 * D), "in": (D, NIN), "out": (D, D), "gu": (D, 2 * FF), "down": (FF, D)}
    bmod_in = din("b_mod", [DEPTH, 6 * D])
    gvec_in = din("gvec", [DEPTH, 4, D])
    gqk_in = din("gqk", [DEPTH, 2, 128])
    ghead_in = din("ghead", [DEPTH, 2 * RH * 128])
    hglb_in = din("hg_lb", [DEPTH, 2, HH * 128])

    y_out = dout("y", [T, D])
    ck_out = dout("ck", [NSEG, DEPTH, SEG, KW])
    cv_out = dout("cv", [NSEG, DEPTH, SEG, KW])
    st_out = dout("st", [NSEG, DEPTH, 2, 2 * RH, 128, 128])

    wb = {k: [dint("wb_%s%d" % (k, l), list(wshape[k]), BF16) for l in range(DEPTH)] for k in wsh}
    PFT = [dint("pft%d" % l, [cfg.NFM, 128, T]) for l in range(DEPTH)]
    PTM = [dint("ptm%d" % l, [T, cfg.NTM]) for l in range(DEPTH)]
    OFW = [dint("ofw%d" % l, [2 * RH, 128, T]) for l in range(DEPTH)]
    MIXT = [dint("mixt%d" % l, [NS, 128, T], BF16) for l in range(DEPTH)]
    XMID = dint("xmid", [T, D])
    GROW = dint("grow", [DEPTH, 2, D])

    S = Sched(nc, es)

    names = []

    def sb(name, shape, dt=F32):
        return es2.enter_context(nc.sbuf_tensor("sb_" + name + "_%d" % len(names), list(shape), dt)) if not names.append(name) else None

    es2 = es
    cst = sb("cst", [128, 7 * 128])
    ident_f = cst[:, 0:128]
    ones_f = cst[:, 512:640]
    cstb = sb("cstb", [128, 7 * 128], BF16)
    ident_b, perm_b, maskf_b, maskb_b, ones_b = (cstb[:, i * 128:(i + 1) * 128] for i in range(5))
    maskf_f, maskbk_f = cst[:, 256:384], cst[:, 384:512]
    maskxf_f, maskxb_f = cst[:, 640:768], cst[:, 768:896]
    lgam = sb("lgam", [128, 2 * RH])
    keep = sb("keep", [128, 1])
    epsc = sb("epsc", [128, 1])
    maskb = sb("maskb", [128, NSEG * (2 + NB)])
    modc = sb("modc", [128, DEPTH, 4, KC])
    gqkc = sb("gqkc", [128, DEPTH, 2])
    gheadc = sb("gheadc", [128, DEPTH, 2 * RH])
    lbc = sb("lbc", [128, DEPTH, 2, HH])
    omlbc = sb("omlbc", [128, DEPTH, 2, HH])
    nomlbc = sb("nomlbc", [128, DEPTH, 2, HH])
    ps = [es.enter_context(nc.psum_tensor("ps%d" % i, [128, 512], F32)) for i in range(8)]
    PK = lambda i: ("ps", i)

    def mm(out, lhsT, rhs, start, stop):
        return lambda e: e.matmul(out, lhsT, rhs, start=start, stop=stop)

    def tcopy(out, in_):
        return lambda e: e.tensor_copy(out=out, in_=in_)

    def dmaf(out, in_, **kw):
        return lambda e: e.dma_start(out=out, in_=in_, **kw)

    def act(out, in_, func, bias=None, scale=None, accum_out=None):
        kw = {}
        if bias is not None:
            kw["bias"] = bias
        if scale is not None:
            kw["scale"] = scale
        if accum_out is not None:
            kw["accum_out"] = accum_out
        return lambda e: e.activation(out=out, in_=in_, func=func, **kw)

    def tscal(out, in0, s1, s2, op0, op1=None):
        if op1 is None:
            return lambda e: e.tensor_scalar(out=out, in0=in0, scalar1=s1, scalar2=None, op0=op0)
        return lambda e: e.tensor_scalar(out=out, in0=in0, scalar1=s1, scalar2=s2, op0=op0, op1=op1)

    def tt(out, in0, in1, op):
        return lambda e: e.tensor_tensor(out=out, in0=in0, in1=in1, op=op)

    def stt(out, in0, scalar, in1, op0, op1):
        return lambda e: e.scalar_tensor_tensor(out=out, in0=in0, scalar=scalar, in1=in1, op0=op0, op1=op1)

    def memset(ap, v):
        return lambda e: e.memset(ap, v)

    def rstd_ops(dst, src, n, rk, wk):
        S.op("act", act(dst, src, AF.Ln, bias=epsc[:, 0:1], scale=1.0 / n), reads=list(rk) + ["epsc"], writes=wk)
        S.op("act", act(dst, dst, AF.Exp, scale=-0.5), reads=wk, writes=wk)

    with ExitStack() as es2:
        order = [(k, l) for l in range(DEPTH) for k in ("mod", "in", "out", "gu", "down")]
        for k, l in order:
            rows, cols = wshape[k]
            for r in range(0, rows, 128):
                n = min(128, rows - r)
                for c0_ in range(0, cols, 8192):
                    cn = min(8192, cols - c0_)
                    S.dma("pool", dmaf(wb[k][l][r:r + n, c0_:c0_ + cn], wsh[k][l, r:r + n, c0_:c0_ + cn]),
                          writes=[("W", k, l)], skey="cast", persist=True)

        S.dma("sp", dmaf(cst[:], cst_in[:, :]), writes=["cst"], skey="c0")
        S.dma("sp", dmaf(lgam[:], lgam_in[:, :]), writes=["lgam"], skey="c1")
        S.dma("sp", dmaf(keep[:], keep_in[:, :]), writes=["keep"], skey="c2")
        S.dma("sp", dmaf(maskb[:], maskb_in[:, :]), writes=["maskb"], skey="c3")
        S.op("dve", tcopy(cstb[:], cst[:]), reads=["cst"], writes=["cstb"])
        S.op("dve", memset(epsc[:], EPS), writes=["epsc"])

        rows = sb("rows", [1, D])
        one11 = cst[0:1, 512:513]

        def row_to_cols(src_ap, n, dst, tag):
            S.dma("sp", dmaf(rows[0:1, 0:n * 128], src_ap), writes=["rows"], skey="rows")
            for j in range(n):
                S.op("pe", mm(ps[0][:, j:j + 1], rows[0:1, j * 128:(j + 1) * 128], one11, True, True),
                     reads=["rows", "cst"], writes=[PK(0)])
            S.op("dve", tcopy(dst, ps[0][:, 0:n]), reads=[PK(0)], writes=[tag])

        condc = sb("condc", [128, KC])
        row_to_cols(cond_in[0:1, :], KC, condc[:], "condc")
        scb = sb("scb", [128, KC], BF16)
        S.op("act", act(scb[:], condc[:], AF.Silu), reads=["condc"], writes=["scb"])
        screp = sb("screp", [128, KC, 128], BF16)
        for c in range(KC):
            S.op("dve", tcopy(screp[:, c, :], scb[:, c:c + 1].to_broadcast([128, 128])), reads=["scb"], writes=["screp"])
        gcol = sb("gcol", [128, 4, KC])
        tmpc = sb("tmpc", [128, 2 * max(KC, 2 * RH)])
        bmodb = sb("bmodb", [1, D], BF16)
        onesrow_b = cstb[0:1, 512:640]
        one11b = cstb[0:1, 512:513]
        wmod_ring = Ring("wmod", [sb("wmod%d" % i, [128, KC, 512], BF16) for i in range(2)])
        grep = sb("grep", [128, D])
        growst = sb("growst", [128, 512])
        for l in range(DEPTH):
            for i in range(4):
                row_to_cols(gvec_in[l, i:i + 1, :], KC, gcol[:, i, :], ("gcol", i))
            row_to_cols(gqk_in[l, 0:1, :], 1, gqkc[:, l, 0:1], "gqkc")
            row_to_cols(gqk_in[l, 1:2, :], 1, gqkc[:, l, 1:2], "gqkc")
            row_to_cols(ghead_in[l:l + 1, :], 2 * RH, gheadc[:, l, :], "gheadc")
            for part in range(6):
                S.dma("sp", dmaf(rows[0:1, :], bmod_in[l:l + 1, part * D:(part + 1) * D]), writes=["rows"], skey="rows")
                S.op("dve", tcopy(bmodb[:], rows[:]), reads=["rows"], writes=["bmodb"])
                for cg in range(D // 512 if D >= 512 else 1):
                    c0 = part * D + cg * 512
                    b0_ = cg * 512
                    wt, wk = wmod_ring.next()
                    S.dma("sp", dmaf(wt[:], wb["mod"][l][:, c0:c0 + 512].rearrange("(kc p) n -> p kc n", p=128)),
                          reads=[("W", "mod", l)], writes=[wk], skey=wk)
                    if part in (2, 5):
                        for kc in range(KC):
                            S.op("pe", mm(ps[1][:, :], screp[:, kc, :], wt[:, kc, :], kc == 0, False),
                                 reads=[wk, "screp"], writes=[PK(1)])
                        S.op("pe", mm(ps[1][:, :], onesrow_b, bmodb[0:1, b0_:b0_ + 512], False, True),
                             reads=["bmodb", "cstb"], writes=[PK(1)])
                        gi = 1 if part == 2 else 3
                        if cg == 0:
                            S.dma("sp", dmaf(grep[:], gvec_in[l, gi:gi + 1, :].partition_broadcast(128)),
                                  writes=["grep"], skey="grep")
                        S.op("dve", tt(growst[:], ps[1][:, :], grep[:, cg * 512:(cg + 1) * 512], ALU.mult),
                             reads=[PK(1), "grep"], writes=["growst"])
                        S.dma("pool", dmaf(GROW[l, (0 if part == 2 else 1):(1 if part == 2 else 2), cg * 512:(cg + 1) * 512],
                                           growst[0:1, :]), reads=["growst"], writes=[("grow", l)], skey="growst")
                    else:
                        for j in range(4):
                            for kc in range(KC):
                                S.op("pe", mm(ps[2][:, j:j + 1], wt[:, kc, j * 128:(j + 1) * 128], scb[:, kc:kc + 1], kc == 0, False),
                                     reads=[wk, "scb"], writes=[PK(2)])
                            S.op("pe", mm(ps[2][:, j:j + 1], bmodb[0:1, b0_ + j * 128:b0_ + (j + 1) * 128], one11b, False, True),
                                 reads=["bmodb", "cstb"], writes=[PK(2)])
                        cs = slice(cg * 4, cg * 4 + 4)
                        if part in (0, 3):
                            S.op("dve", tcopy(modc[:, l, 1 if part == 0 else 3, cs], ps[2][:, 0:4]), reads=[PK(2)], writes=["modc"])
                        else:
                            gi = 0 if part == 1 else 2
                            S.op("dve", stt(modc[:, l, 0 if part == 1 else 2, cs], ps[2][:, 0:4], 1.0, gcol[:, gi, cs], ALU.add, ALU.mult),
                                 reads=[PK(2), ("gcol", gi)], writes=["modc"])
        for d in range(2):
            row_to_cols(hglb_in[0, d:d + 1, :], HH, tmpc[:, 0:HH], "tmpc0")
            row_to_cols(hglb_in[1, d:d + 1, :], HH, tmpc[:, HH:2 * HH], "tmpc1")
            S.op("dve", memset(lbc[:, 0, d, :], 0.0), writes=["lbc"])
            S.op("dve", tt(tmpc[:, 0:HH], tmpc[:, HH:2 * HH], tmpc[:, 0:HH], ALU.subtract), reads=["tmpc0", "tmpc1"], writes=["tmpc0"])
            S.op("act", act(lbc[:, 1, d, :], tmpc[:, 0:HH], AF.Sigmoid), reads=["tmpc0", "lbc"], writes=["lbc"])
        S.op("dve", tscal(omlbc[:], lbc[:], -1.0, 1.0, ALU.mult, ALU.add), reads=["lbc"], writes=["omlbc"])
        S.op("dve", tscal(nomlbc[:], omlbc[:], -1.0, None, ALU.mult), reads=["omlbc"], writes=["nomlbc"])
        S.fence(final=(cfg.stop == 0))
    if cfg.stop == 0:
        es.close()
        return nc, S

    def prenorm_tile(x_src, r0, nblk, xring, hT, A, B, ssq, rstd):
        for b in range(nblk):
            xt, xk = xring.next()
            S.dma("sp", dmaf(xt[:], x_src[r0 + b * 128:r0 + (b + 1) * 128, :]), writes=[xk], skey=xk)
            norm_transpose(xt, xk, b, hT, A, B, ssq, rstd)

    jk = {"i": 0}

    def norm_transpose(xt, xk, b, hT, A, B, ssq, rstd, dst=None, dk=None):
        dst = xt if dst is None else dst
        dk = xk if dk is None else dk
        S.op("dve", memset(ssq[:], 0.0), writes=["ssq"])
        S.op("act", act(junk_t[:, 0:D], xt[:], AF.Square, accum_out=ssq[:, 0:1]), reads=[xk, "ssq"], writes=["junk", "ssq"])
        rstd_ops(rstd[:, 0:1], ssq[:, 0:1], D, ["ssq"], ["rstd"])
        S.op("dve", tscal(dst[:], xt[:], rstd[:, 0:1], None, ALU.mult), reads=[xk, "rstd"], writes=[dk])
        for c in range(KC):
            s = jk["i"] % 32
            jk["i"] += 1
            pt = ps[s // 4][:, (s % 4) * 128:(s % 4 + 1) * 128]
            S.op("pe", lambda e, pt=pt, c=c: e.transpose(pt, dst[:, c * 128:(c + 1) * 128], ident_f),
                 reads=[dk, "cst"], writes=[PK(s // 4)])
            S.op("act", act(hT[:, c, b * 128:(b + 1) * 128], pt, AF.Identity, bias=B[:, c:c + 1], scale=A[:, c:c + 1]),
                 reads=[PK(s // 4), "modc"], writes=[("hT", b)])

    def load_w(wring, name, l, k0, kn, c0, ncol):
        wt, wk = wring.next()
        src = wb[name][l][k0 * 128:(k0 + kn) * 128, c0:c0 + ncol].rearrange("(kc p) n -> p kc n", p=128)
        S.dma("sp", dmaf(wt[:, 0:kn, 0:ncol], src), reads=[("W", name, l)], writes=[wk], skey=wk)
        return wt, wk

    KT = 16

    for l in range(DEPTH):
        x_src = x_in if l == 0 else XMID
        x_dst = XMID if l == 0 else y_out
        with ExitStack() as es2:
            TT = 512 if T >= 512 else T
            nblk = TT // 128
            hT = sb("hT", [128, KC, TT], BF16)
            xring = Ring("xin", [sb("xin%d" % i, [128, D]) for i in range(2)])
            junk_t = sb("junk", [128, D], BF16)
            wring = Ring("w", [sb("w%d" % i, [128, KT, 512], BF16) for i in range(3)])
            stg = Ring("stg", [sb("stg%d" % i, [128, 512]) for i in range(6)])
            ssq = sb("ssq", [128, 1])
            rstd = sb("rstd", [128, 1])
            pieces = []
            for nm in ["rq", "rk", "rv", "rg", "hq", "hzf", "hzb", "hi", "hg", "aq", "ak", "av", "ak_tm"]:
                base = nm[:2] if nm.endswith("_tm") else nm
                c0, w = cfg.col[base]
                kind = "tm" if nm in ("rv", "hi", "av", "ak_tm") else "fm"
                o = 0
                while o < w:
                    n = min(512, w - o)
                    pieces.append((kind, base, c0 + o, n, o))
                    o += n
            pset = 0
            for ti in range(T // TT):
                r0 = ti * TT
                prenorm_tile(x_src, r0, nblk, xring, hT, modc[:, l, 0, :], modc[:, l, 1, :], ssq, rstd)
                for kind, base, c0, n, o in pieces:
                    banks = [pset * 4 + i for i in range(4)]
                    pset ^= 1
                    nch = n // 128
                    for k0 in range(0, KC, KT):
                        kn = min(KT, KC - k0)
                        wt, wk = load_w(wring, "in", l, k0, kn, c0, n)
                        for kk in range(kn):
                            kc = k0 + kk
                            st, sp_ = kc == 0, kc == KC - 1
                            if kind == "fm":
                                for j in range(nch):
                                    S.op("pe", mm(ps[banks[j]][:, 0:TT], wt[:, kk, j * 128:(j + 1) * 128], hT[:, kc, :], st, sp_),
                                         reads=[wk] + [("hT", b) for b in range(nblk)], writes=[PK(banks[j])])
                            else:
                                for b in range(nblk):
                                    S.op("pe", mm(ps[banks[b]][:, 0:n], hT[:, kc, b * 128:(b + 1) * 128], wt[:, kk, 0:n], st, sp_),
                                         reads=[wk, ("hT", b)], writes=[PK(banks[b])])
                    nout = nch if kind == "fm" else nblk
                    for j in range(nout):
                        sg, sk = stg.next()
                        eng = "act" if j % 2 == 0 else "dve"
                        if kind == "fm":
                            src = ps[banks[j]][:, 0:TT]
                            fn = act(sg[:, 0:TT], src, AF.Copy) if eng == "act" else tcopy(sg[:, 0:TT], src)
                            S.op(eng, fn, reads=[PK(banks[j])], writes=[sk])
                            fi = cfg.fm[base] + (o // 128) + j
                            S.dma("pool", dmaf(PFT[l][fi, :, r0:r0 + TT], sg[:, 0:TT]), reads=[sk],
                                  writes=[("pft", fi, ti)], skey=sk)
                        else:
                            src = ps[banks[j]][:, 0:n]
                            fn = act(sg[:, 0:n], src, AF.Copy) if eng == "act" else tcopy(sg[:, 0:n], src)
                            S.op(eng, fn, reads=[PK(banks[j])], writes=[sk])
                            tc0 = cfg.tm[base] + o
                            S.dma("pool", dmaf(PTM[l][r0 + j * 128:r0 + (j + 1) * 128, tc0:tc0 + n], sg[:, 0:n]), reads=[sk],
                                  writes=[("ptm", base, ti)], skey=sk)
            S.fence(final=(cfg.stop == 1))
        if cfg.stop == 1:
            es.close()
            return nc, S

        with ExitStack() as es2:
            NK = 2 + NB
            TQ = 512 if T >= 512 else T
            KTall = sb("KTall", [128, KVH, PAST + T], BF16)
            Vall = sb("Vall", [128, NK, KW], BF16)
            ldr = Ring("ld", [sb("ld%d" % i, [128, 512]) for i in range(3)])
            tb_r = Ring("tb", [sb("tb%d" % i, [128, 512], BF16) for i in range(3)])
            f1 = Ring("f1", [sb("f1_%d" % i, [128, 512]) for i in range(3)])
            f2 = Ring("f2", [sb("f2_%d" % i, [128, 512]) for i in range(3)])
            cosr = Ring("cos", [sb("cos%d" % i, [128, 512]) for i in range(2)])
            sinr = Ring("sin", [sb("sin%d" % i, [128, 512]) for i in range(2)])
            qr_r = Ring("qr", [sb("qr%d" % i, [128, 512], BF16) for i in range(2)])
            pT_r = Ring("pT", [sb("pT%d" % i, [128, 512], BF16) for i in range(4)])
            ost = Ring("ost", [sb("ost%d" % i, [128, 512], BF16) for i in range(2)])
            kvld = Ring("kvld", [sb("kvld%d" % i, [128, KW]) for i in range(2)])
            kvo = Ring("kvo", [sb("kvo%d" % i, [128, KW]) for i in range(2)])
            gkrep = sb("gkrep", [128, 128])
            ssqh = sb("ssqh", [128, KVH])
            junk2 = sb("junk2", [128, 128], BF16)
            psi = {"i": 0}

            def psn(lo, hi):
                i = lo + psi["i"] % (hi - lo)
                psi["i"] += 1
                return i

            def norm_rope(src_dram, t0, n, gcol_ap, dst, dk, cs_t, cs_k, sn_t, sn_k):
                ld, lk = ldr.next()
                S.dma("sp", dmaf(ld[:, 0:n], src_dram), reads=[], writes=[lk], skey=lk)
                a1, k1 = f1.next()
                S.op("act", act(a1[:, 0:n], ld[:, 0:n], AF.Square), reads=[lk], writes=[k1])
                b0 = psn(0, 2)
                S.op("pe", mm(ps[b0][:, 0:n], ones_f, a1[:, 0:n], True, True), reads=[k1, "cst"], writes=[PK(b0)])
                a2, k2 = f2.next()
                rstd_ops(a2[:, 0:n], ps[b0][:, 0:n], 128, [PK(b0)], [k2])
                S.op("dve", stt(a1[:, 0:n], ld[:, 0:n], gcol_ap, a2[:, 0:n], ALU.mult, ALU.mult), reads=[lk, k2, "gqkc"], writes=[k1])
                tb, tk = tb_r.next()
                S.op("act", act(tb[:, 0:n], a1[:, 0:n], AF.Copy), reads=[k1], writes=[tk])
                S.op("pe", mm(ps[b0][:, 0:n], perm_b, tb[:, 0:n], True, True), reads=[tk, "cstb"], writes=[PK(b0)])
                S.op("dve", tt(a2[:, 0:n], ps[b0][:, 0:n], sn_t[:, 0:n], ALU.mult), reads=[PK(b0), sn_k], writes=[k2])
                S.op("pool", tt(a1[:, 0:n], a1[:, 0:n], cs_t[:, 0:n], ALU.mult), reads=[k1, cs_k], writes=[k1])
                S.op("dve", tt(dst, a1[:, 0:n], a2[:, 0:n], ALU.add), reads=[k1, k2], writes=[dk])

            for blk in range(2):
                kt_, kk_ = kvld.next()
                S.dma("sp", dmaf(kt_[:], ctxk_in[l, blk * 128:(blk + 1) * 128, :]), writes=[kk_], skey=kk_)
                for h in range(KVH):
                    b0 = psn(0, 2)
                    S.op("pe", lambda e, b0=b0, kt_=kt_, h=h: e.transpose(ps[b0][:, 0:128], kt_[:, h * 128:(h + 1) * 128], ident_f),
                         reads=[kk_, "cst"], writes=[PK(b0)])
                    S.op("dve", tcopy(KTall[:, h, blk * 128:(blk + 1) * 128], ps[b0][:, 0:128]), reads=[PK(b0)], writes=["KT"])
                vt_, vk_ = kvld.next()
                S.dma("sp", dmaf(vt_[:], ctxv_in[l, blk * 128:(blk + 1) * 128, :]), writes=[vk_], skey=vk_)
                S.op("act", act(Vall[:, blk, :], vt_[:], AF.Copy), reads=[vk_], writes=["V"])
            S.dma("sp", dmaf(gkrep[:], gqk_in[l, 1:2, :].partition_broadcast(128)), writes=["gkrep"], skey="gkrep")
            tmo = cfg.tm
            for b in range(NB):
                vt_, vk_ = kvld.next()
                S.dma("sp", dmaf(vt_[:], PTM[l][b * 128:(b + 1) * 128, tmo["av"]:tmo["av"] + KW]), writes=[vk_], skey=vk_)
                S.op("act", act(Vall[:, 2 + b, :], vt_[:], AF.Copy), reads=[vk_], writes=["V"])
                seg, ro = (b * 128) // SEG, (b * 128) % SEG
                S.dma("pool", dmaf(cv_out[seg, l, ro:ro + 128, :], vt_[:]), reads=[vk_], writes=[("cvo", b)], skey=("cvs", vk_))
                kt_, kk_ = kvld.next()
                S.dma("sp", dmaf(kt_[:], PTM[l][b * 128:(b + 1) * 128, tmo["ak"]:tmo["ak"] + KW]), writes=[kk_], skey=kk_)
                S.op("dve", memset(ssqh[:], 0.0), writes=["ssqh"])
                for h in range(KVH):
                    S.op("act", act(junk2[:], kt_[:, h * 128:(h + 1) * 128], AF.Square, accum_out=ssqh[:, h:h + 1]),
                         reads=[kk_, "ssqh"], writes=["ssqh", "junk2"])
                rstd_ops(ssqh[:], ssqh[:], 128, ["ssqh"], ["ssqh"])
                ko, kok = kvo.next()
                for h in range(KVH):
                    S.op("dve", stt(ko[:, h * 128:(h + 1) * 128], kt_[:, h * 128:(h + 1) * 128], ssqh[:, h:h + 1], gkrep[:], ALU.mult, ALU.mult),
                         reads=[kk_, "ssqh", "gkrep"], writes=[kok])
                S.dma("pool", dmaf(ck_out[seg, l, ro:ro + 128, :], ko[:]), reads=[kok], writes=[("cko", b)], skey=kok)
            for ti in range(T // TQ):
                t0 = ti * TQ
                ct, ck_ = cosr.next()
                S.dma("sp", dmaf(ct[:, 0:TQ], cos_in[:, t0:t0 + TQ]), writes=[ck_], skey=ck_)
                st_, sk_ = sinr.next()
                S.dma("sp", dmaf(st_[:, 0:TQ], sin_in[:, t0:t0 + TQ]), writes=[sk_], skey=sk_)
                for h in range(KVH):
                    norm_rope(PFT[l][cfg.fm["ak"] + h, :, t0:t0 + TQ], t0, TQ, gqkc[:, l, 1:2],
                              KTall[:, h, PAST + t0:PAST + t0 + TQ], "KT", ct, ck_, st_, sk_)
            scale = 1.0 / math.sqrt(128.0)
            for ti in range(T // TQ):
                t0 = ti * TQ
                ct, ck_ = cosr.next()
                S.dma("sp", dmaf(ct[:, 0:TQ], cos_in[:, t0:t0 + TQ]), writes=[ck_], skey=ck_)
                st_, sk_ = sinr.next()
                S.dma("sp", dmaf(st_[:, 0:TQ], sin_in[:, t0:t0 + TQ]), writes=[sk_], skey=sk_)
                for h in range(AH):
                    kvh = h // 4
                    qr, qk = qr_r.next()
                    norm_rope(PFT[l][cfg.fm["aq"] + h, :, t0:t0 + TQ], t0, TQ, gqkc[:, l, 0:1], qr[:, 0:TQ], qk, ct, ck_, st_, sk_)
                    bo, bd = psn(2, 4), psn(4, 6)
                    for kb in range(NK):
                        bs_ = psn(6, 8)
                        S.op("pe", mm(ps[bs_][:, 0:TQ], KTall[:, kvh, kb * 128:(kb + 1) * 128], qr[:, 0:TQ], True, True),
                             reads=["KT", qk], writes=[PK(bs_)])
                        pT, pk = pT_r.next()
                        for qh in range(TQ // SEG):
                            gq = (t0 // SEG) + qh
                            S.op("act", act(pT[:, qh * SEG:(qh + 1) * SEG], ps[bs_][:, qh * SEG:(qh + 1) * SEG], AF.Exp,
                                            bias=maskb[:, gq * NK + kb:gq * NK + kb + 1], scale=scale),
                                 reads=[PK(bs_), "maskb"], writes=[pk])
                        S.op("pe", mm(ps[bo][:, 0:TQ], Vall[:, kb, kvh * 128:(kvh + 1) * 128], pT[:, 0:TQ], kb == 0, kb == NK - 1),
                             reads=["V", pk], writes=[PK(bo)])
                        S.op("pe", mm(ps[bd][:, 0:TQ], ones_b, pT[:, 0:TQ], kb == 0, kb == NK - 1),
                             reads=["cstb", pk], writes=[PK(bd)])
                    a2, k2 = f2.next()
                    S.op("dve", lambda e, a2=a2, bd=bd: e.reciprocal(out=a2[:, 0:TQ], in_=ps[bd][:, 0:TQ]), reads=[PK(bd)], writes=[k2])
                    og, ok_ = ost.next()
                    S.op("dve", tt(og[:, 0:TQ], ps[bo][:, 0:TQ], a2[:, 0:TQ], ALU.mult), reads=[PK(bo), k2], writes=[ok_])
                    S.dma("pool", dmaf(MIXT[l][2 * RH + h, :, t0:t0 + TQ], og[:, 0:TQ]), reads=[ok_], writes=[("mixt", 2 * RH + h, ti)], skey=ok_)
            S.fence(final=(cfg.stop == 2))
        if cfg.stop == 2:
            es.close()
            return nc, S

        with ExitStack() as es2:
            NMH = 2 * RH
            Sst = sb("Sst", [128, 2, NMH, 128])
            Sbf = sb("Sbf", [128, 2, NMH, 128], BF16)
            ldq = Ring("ldq", [sb("ldq%d" % i, [128, 128]) for i in range(3)])
            ldk = Ring("ldk", [sb("ldk%d" % i, [128, 128]) for i in range(3)])
            ldz = Ring("ldz", [sb("ldz%d" % i, [128, 128]) for i in range(3)])
            ldg = Ring("ldg", [sb("ldg%d" % i, [128, 128]) for i in range(2)])
            ldv = Ring("ldv", [sb("ldv%d" % i, [128, 128]) for i in range(3)])
            ldo = Ring("ldo", [sb("ldo%d" % i, [128, 128]) for i in range(2)])
            csr = Ring("csr", [sb("csr%d" % i, [128, 256]) for i in range(2)])
            qf_r = Ring("qf", [sb("qf%d" % i, [128, 128]) for i in range(3)])
            kf_r = Ring("kf", [sb("kf%d" % i, [128, 128]) for i in range(3)])
            gf_r = Ring("gf", [sb("gf%d" % i, [128, 128]) for i in range(3)])
            Bc_r = Ring("Bc", [sb("Bc%d" % i, [128, 132]) for i in range(3)])
            ar_r = Ring("ar", [sb("ar%d" % i, [128, 128]) for i in range(6)])
            ex_r = Ring("ex", [sb("ex%d" % i, [128, 128]) for i in range(6)])
            dec_r = Ring("dec", [sb("dec%d" % i, [128, 4]) for i in range(3)])
            qd_r = Ring("qd", [sb("qd%d" % i, [128, 128], BF16) for i in range(3)])
            kd_r = Ring("kd", [sb("kd%d" % i, [128, 128], BF16) for i in range(3)])
            qs_r = Ring("qs", [sb("qs%d" % i, [128, 128], BF16) for i in range(3)])
            qx_r = Ring("qx", [sb("qx%d" % i, [128, 128], BF16) for i in range(3)])
            kx_r = Ring("kx", [sb("kx%d" % i, [128, 128], BF16) for i in range(3)])
            am2_r = Ring("am2", [sb("am2_%d" % i, [128, 128], BF16) for i in range(3)])
            ku_r = Ring("ku", [sb("ku%d" % i, [128, 128]) for i in range(3)])
            kut_r = Ring("kut", [sb("kut%d" % i, [128, 128], BF16) for i in range(3)])
            vb_r = Ring("vb", [sb("vb%d" % i, [128, 128], BF16) for i in range(3)])
            am_r = Ring("am", [sb("am%d" % i, [128, 128], BF16) for i in range(3)])
            tb2 = Ring("tb2", [sb("tb2_%d" % i, [128, 128], BF16) for i in range(3)])
            of_r = Ring("of", [sb("of%d" % i, [128, 128]) for i in range(3)])
            o2_r = Ring("o2", [sb("o2_%d" % i, [128, 128]) for i in range(3)])
            o3_r = Ring("o3", [sb("o3_%d" % i, [128, 128]) for i in range(3)])
            mx_r = Ring("mx", [sb("mx%d" % i, [128, 128], BF16) for i in range(3)])
            zer = sb("zer", [128, 128])
            one_t = sb("one_t", [128, 128])
            S.op("dve", memset(zer[:], 0.0), writes=["zer"])
            S.op("dve", memset(one_t[:], 1.0), writes=["one_t"])
            psj = {"i": 0}

            def pq(lo, hi):
                i = lo + psj["i"] % (hi - lo)
                psj["i"] += 1
                return i

            for d in range(2):
                for mh in range(NMH):
                    S.dma("sp", dmaf(Sst[:, d, mh, :], s0_in[l, d, mh, :, :]), writes=[("S", d, mh)], skey=("sld", d, mh % 4))
                    S.op("act", act(Sbf[:, d, mh, :], Sst[:, d, mh, :], AF.Copy), reads=[("S", d, mh)], writes=[("Sb", d, mh)])

            def gla_block(d, mh, blk):
                isret = mh < RH
                h = mh if isret else mh - RH
                t0 = blk * 128
                fmq = cfg.fm["rq" if isret else "hq"] + h
                ql, qlk = ldq.next()
                S.dma("sp", dmaf(ql[:], PFT[l][fmq, :, t0:t0 + 128]), writes=[qlk], skey=qlk)
                vl, vlk = ldv.next()
                vc0 = cfg.tm["rv" if isret else "hi"] + h * 128
                S.dma("sp", dmaf(vl[:], PTM[l][t0:t0 + 128, vc0:vc0 + 128]), writes=[vlk], skey=vlk)
                vb, vbk = vb_r.next()
                S.op("act", act(vb[:], vl[:], AF.Copy), reads=[vlk], writes=[vbk])
                qf, qfk = qf_r.next()
                kf, kfk = kf_r.next()
                gf, gfk = gf_r.next()
                if isret:
                    kl, klk = ldk.next()
                    S.dma("sp", dmaf(kl[:], PFT[l][cfg.fm["rk"] + h, :, t0:t0 + 128]), writes=[klk], skey=klk)
                    cs, csk = csr.next()
                    S.dma("sp", dmaf(cs[:, 0:128], cos_in[:, t0:t0 + 128]), writes=[csk], skey=csk)
                    S.dma("sp", dmaf(cs[:, 128:256], sin_in[:, t0:t0 + 128]), writes=[csk], skey=csk)
                    for (src, sk_, dst, dk_, sc) in ((ql, qlk, qf, qfk, 1.0), (kl, klk, kf, kfk, 128.0 ** -0.5)):
                        tb, tbk = tb2.next()
                        S.op("act", act(tb[:], src[:], AF.Copy), reads=[sk_], writes=[tbk])
                        b0 = pq(0, 2)
                        S.op("pe", mm(ps[b0][:, 0:128], perm_b, tb[:], True, True), reads=[tbk, "cstb"], writes=[PK(b0)])
                        ar, ark = ar_r.next()
                        S.op("dve", tt(ar[:], ps[b0][:, 0:128], cs[:, 128:256], ALU.mult), reads=[PK(b0), csk], writes=[ark])
                        S.op("pool", tt(dst[:], src[:], cs[:, 0:128], ALU.mult), reads=[sk_, csk], writes=[dk_])
                        S.op("dve", tt(dst[:], dst[:], ar[:], ALU.add), reads=[dk_, ark], writes=[dk_])
                        if sc != 1.0:
                            S.op("dve", tscal(dst[:], dst[:], sc, None, ALU.mult), reads=[dk_], writes=[dk_])
                    S.op("dve", tscal(gf[:], one_t[:], lgam[:, d * RH + h:d * RH + h + 1], None, ALU.mult),
                         reads=["one_t", "lgam"], writes=[gfk])
                else:
                    S.op("act", act(qf[:], ql[:], AF.Silu), reads=[qlk], writes=[qfk])
                    zl, zlk = ldz.next()
                    S.dma("sp", dmaf(zl[:], PFT[l][cfg.fm["hzf" if d == 0 else "hzb"] + h, :, t0:t0 + 128]), writes=[zlk], skey=zlk)
                    ar, ark = ar_r.next()
                    S.op("act", act(ar[:], zl[:], AF.Sigmoid), reads=[zlk], writes=[ark])
                    S.op("act", act(gf[:], ar[:], AF.Ln, bias=lbc[:, l, d, h:h + 1], scale=omlbc[:, l, d, h:h + 1]),
                         reads=[ark, "lbc", "omlbc"], writes=[gfk])
                    S.op("act", act(kf[:], zl[:], AF.Sigmoid, scale=-1.0), reads=[zlk], writes=[kfk])
                    S.op("dve", tscal(kf[:], kf[:], omlbc[:, l, d, h:h + 1], None, ALU.mult), reads=[kfk, "omlbc"], writes=[kfk])
                Bc, Bck = Bc_r.next()
                S.op("dve", memset(Bc[:, 0:1], 0.0), writes=[Bck])
                S.op("dve", lambda e, Bc=Bc, gf=gf: e.tensor_tensor_scan(out=Bc[:, 1:129], data0=one_t[:], data1=gf[:], initial=0.0,
                                                                        op0=ALU.mult, op1=ALU.add),
                     reads=[gfk, "one_t", Bck], writes=[Bck])
                Binc, Eexc = Bc[:, 1:129], Bc[:, 0:128]

                def bc4(col_ap):
                    return col_ap.unsqueeze(2).to_broadcast([128, 4, 32])

                def bc8(col_ap):
                    return col_ap.unsqueeze(2).to_broadcast([128, 8, 16])

                def v3(ap):
                    return ap.rearrange("p (a b) -> p a b", b=32)

                def v16(ap):
                    return ap.rearrange("p (a b) -> p a b", b=16)
                if d == 0:
                    X = Binc
                    m16 = v16(Binc)[:, :, 8]
                    rho = v3(Eexc)[:, :, 0]
                    rho_n = v3(Binc)[:, :, 31]
                    rhox = v3(Binc)[:, :, 15]
                    specs = [("qd", qf, qfk, X, m16, 1.0, True, False), ("kd", kf, kfk, X, m16, -1.0, True, False),
                             ("qx", qf, qfk, X, rhox, 1.0, False, True), ("kx", kf, kfk, X, rhox, -1.0, False, True),
                             ("qs", qf, qfk, X, rho, 1.0, False, True), ("ku", kf, kfk, X, rho_n, -1.0, False, True)]
                else:
                    X = Eexc
                    m16 = v16(Eexc)[:, :, 8]
                    rho_e = v3(Binc)[:, :, 31]
                    rho_s = v3(Eexc)[:, :, 0]
                    rhox = v3(Eexc)[:, :, 16]
                    specs = [("qd", qf, qfk, X, m16, -1.0, True, False), ("kd", kf, kfk, X, m16, 1.0, True, False),
                             ("qx", qf, qfk, X, rhox, -1.0, False, True), ("kx", kf, kfk, X, rhox, 1.0, False, True),
                             ("qs", qf, qfk, X, rho_e, -1.0, False, True), ("ku", kf, kfk, X, rho_s, 1.0, False, True)]
                outs = {}
                for nm, base, bk, Xa, ref, sgn, six, clamp in specs:
                    ar, ark = ar_r.next()
                    if six:
                        S.op("dve", lambda e, ar=ar, Xa=Xa, ref=ref: e.tensor_tensor(out=v16(ar[:]), in0=v16(Xa), in1=bc8(ref), op=ALU.subtract),
                             reads=[Bck], writes=[ark])
                    else:
                        S.op("dve", lambda e, ar=ar, Xa=Xa, ref=ref: e.tensor_tensor(out=v3(ar[:]), in0=v3(Xa), in1=bc4(ref), op=ALU.subtract),
                             reads=[Bck], writes=[ark])
                    ex, exk = ex_r.next()
                    if clamp:
                        S.op("pool", tscal(ar[:], ar[:], sgn, 0.0, ALU.mult, ALU.min), reads=[ark], writes=[ark])
                        S.op("act", act(ex[:], ar[:], AF.Exp), reads=[ark], writes=[exk])
                    else:
                        S.op("act", act(ex[:], ar[:], AF.Exp, scale=sgn), reads=[ark], writes=[exk])
                    ring = {"qd": qd_r, "kd": kd_r, "qx": qx_r, "kx": kx_r, "qs": qs_r, "ku": ku_r}[nm]
                    ot, otk = ring.next()
                    S.op("pool" if nm in ("qs", "ku", "qx") else "dve", tt(ot[:], base[:], ex[:], ALU.mult), reads=[bk, exk], writes=[otk])
                    outs[nm] = (ot, otk)
                dec, deck = dec_r.next()
                S.op("dve", tt(dec[:], v3(Binc)[:, :, 31], v3(Eexc)[:, :, 0], ALU.subtract), reads=[Bck], writes=[deck])
                S.op("act", act(dec[:], dec[:], AF.Exp), reads=[deck], writes=[deck])
                ku, kuk = outs["ku"]
                bt = pq(0, 2)
                S.op("pe", lambda e, bt=bt, ku=ku: e.transpose(ps[bt][:, 0:128], ku[:], ident_f), reads=[kuk, "cst"], writes=[PK(bt)])
                kut, kutk = kut_r.next()
                S.op("dve", tcopy(kut[:], ps[bt][:, 0:128]), reads=[PK(bt)], writes=[kutk])
                qd, qdk = outs["qd"]
                kd, kdk = outs["kd"]
                qs, qsk = outs["qs"]
                ba = pq(2, 4)
                S.op("pe", mm(ps[ba][:, 0:128], kd[:], qd[:], True, True), reads=[kdk, qdk], writes=[PK(ba)])
                am, amk = am_r.next()
                S.op("dve", tt(am[:], ps[ba][:, 0:128], maskf_f if d == 0 else maskbk_f, ALU.mult), reads=[PK(ba), "cst"], writes=[amk])
                qx, qxk = outs["qx"]
                kx, kxk = outs["kx"]
                ba2 = pq(2, 4)
                S.op("pe", mm(ps[ba2][:, 0:128], kx[:], qx[:], True, True), reads=[kxk, qxk], writes=[PK(ba2)])
                am2, am2k = am2_r.next()
                S.op("dve", tt(am2[:], ps[ba2][:, 0:128], maskxf_f if d == 0 else maskxb_f, ALU.mult), reads=[PK(ba2), "cst"], writes=[am2k])
                bo = pq(4, 6)
                S.op("pe", mm(ps[bo][:, 0:128], vb[:], am[:], True, False), reads=[vbk, amk], writes=[PK(bo)])
                S.op("pe", mm(ps[bo][:, 0:128], vb[:], am2[:], False, False), reads=[vbk, am2k], writes=[PK(bo)])
                subs = range(4) if d == 0 else range(3, -1, -1)
                for n_i, I in enumerate(subs):
                    last = n_i == 3
                    S.op("pe", mm(ps[bo][:, I * 32:(I + 1) * 32], Sbf[:, d, mh, :], qs[:, I * 32:(I + 1) * 32], False, last),
                         reads=[("Sb", d, mh), qsk], writes=[PK(bo)])
                    bs_ = pq(6, 8)
                    S.op("pe", lambda e, bs_=bs_, I=I, kut=kut, vb=vb: e.matmul(ps[bs_][:, 0:128], kut[I * 32:(I + 1) * 32, :], vb[I * 32:(I + 1) * 32, :],
                                                                           start=True, stop=True, tile_position=((I * 32), 0)),
                         reads=[kutk, vbk], writes=[PK(bs_)])
                    S.op("dve", stt(Sst[:, d, mh, :], Sst[:, d, mh, :], dec[:, I:I + 1], ps[bs_][:, 0:128], ALU.mult, ALU.add),
                         reads=[("S", d, mh), deck, PK(bs_)], writes=[("S", d, mh)])
                    S.op("act", act(Sbf[:, d, mh, :], Sst[:, d, mh, :], AF.Copy), reads=[("S", d, mh)], writes=[("Sb", d, mh)])
                endseg = (d == 0 and (t0 + 128) % SEG == 0) or (d == 1 and t0 % SEG == 0)
                if endseg:
                    seg = t0 // SEG
                    S.dma("pool", dmaf(st_out[seg, l, d, mh, :, :], Sst[:, d, mh, :]), reads=[("S", d, mh)], writes=[("sto", seg, d, mh)],
                          skey=("sst", d, mh % 4))
                    S.op("dve", tscal(Sst[:, d, mh, :], Sst[:, d, mh, :], keep[:, 0:1], None, ALU.mult), reads=[("S", d, mh), "keep"], writes=[("S", d, mh)])
                    S.op("act", act(Sbf[:, d, mh, :], Sst[:, d, mh, :], AF.Copy), reads=[("S", d, mh)], writes=[("Sb", d, mh)])
                if d == 0:
                    of, ofk = of_r.next()
                    S.op("act", act(of[:], ps[bo][:, 0:128], AF.Copy), reads=[PK(bo)], writes=[ofk])
                    S.dma("pool", dmaf(OFW[l][mh, :, t0:t0 + 128], of[:]), reads=[ofk], writes=[("ofw", mh, blk)], skey=ofk)
                else:
                    lo, lok = ldo.next()
                    S.dma("sp", dmaf(lo[:], OFW[l][mh, :, t0:t0 + 128]), reads=[("ofw", mh, blk)], writes=[lok], skey=lok)
                    o2, o2k = o2_r.next()
                    S.op("dve", tt(o2[:], ps[bo][:, 0:128], lo[:], ALU.add), reads=[PK(bo), lok], writes=[o2k])
                    o3, o3k = o3_r.next()
                    bn = pq(0, 2)
                    if isret:
                        S.op("pe", mm(ps[bn][:, 0:128], ones_f, o2[:], True, True), reads=[o2k, "cst"], writes=[PK(bn)])
                        S.op("dve", stt(o2[:], ps[bn][:, 0:128], -1.0 / 128.0, o2[:], ALU.mult, ALU.add), reads=[PK(bn), o2k], writes=[o2k])
                    S.op("act", act(o3[:], o2[:], AF.Square), reads=[o2k], writes=[o3k])
                    S.op("pe", mm(ps[bn][:, 0:128], ones_f, o3[:], True, True), reads=[o3k, "cst"], writes=[PK(bn)])
                    rstd_ops(o3[:], ps[bn][:, 0:128], 128, [PK(bn)], [o3k])
                    S.op("dve", stt(o2[:], o2[:], gheadc[:, l, mh:mh + 1], o3[:], ALU.mult, ALU.mult), reads=[o2k, o3k, "gheadc"], writes=[o2k])
                    gl, glk = ldg.next()
                    S.dma("sp", dmaf(gl[:], PFT[l][cfg.fm["rg" if isret else "hg"] + h, :, t0:t0 + 128]), writes=[glk], skey=glk)
                    S.op("act", act(gl[:], gl[:], AF.Silu if isret else AF.Sigmoid), reads=[glk], writes=[glk])
                    mx, mxk = mx_r.next()
                    S.op("dve", tt(mx[:], o2[:], gl[:], ALU.mult), reads=[o2k, glk], writes=[mxk])
                    S.dma("pool", dmaf(MIXT[l][mh, :, t0:t0 + 128], mx[:]), reads=[mxk], writes=[("mixt", mh, blk)], skey=mxk)

            for blk in range(NB):
                for mh in range(NMH):
                    gla_block(0, mh, blk)
            for blk in range(NB - 1, -1, -1):
                for mh in range(NMH):
                    gla_block(1, mh, blk)
            S.fence(final=(cfg.stop == 3))
        if cfg.stop == 3:
            es.close()
            return nc, S

        with ExitStack() as es2:
            TT = 256
            nblk = 2
            hT = sb("hT3", [128, max(KC, NS), TT], BF16)
            actT = sb("actT", [128, FC, TT], BF16)
            wring = Ring("w3", [sb("w3_%d" % i, [128, KT, 512], BF16) for i in range(3)])
            xb = [sb("xb%d" % i, [128, D]) for i in range(2)]
            zb = [sb("zb%d" % i, [128, D]) for i in range(2)]
            junk_t = sb("junk3", [128, D], BF16)
            gr_r = Ring("gr", [sb("gr%d" % i, [128, 512]) for i in range(2)])
            sl_r = Ring("sl", [sb("sl%d" % i, [128, 256]) for i in range(3)])
            ssq = sb("ssq3", [128, 1])
            rstd = sb("rstd3", [128, 1])
            ssqp = sb("ssqp", [128, 2, 16])
            NCG = D // 512 if D >= 512 else 1
            CW = min(512, D)

            def epilogue(which, cg, b, bank):
                gr, grk = gr_r.next()
                S.dma("sp", dmaf(gr[:, 0:CW], GROW[l, which:which + 1, cg * CW:(cg + 1) * CW].partition_broadcast(128)),
                      reads=[("grow", l)], writes=[grk], skey=grk)
                S.op("act", act(junk_t[:, 0:CW], ps[bank][:, 0:CW], AF.Square, accum_out=ssqp[:, b, cg:cg + 1]),
                     reads=[PK(bank), ("ssqp", b)], writes=[("ssqp", b), "junk"])
                S.op("dve", tt(zb[b][:, cg * CW:(cg + 1) * CW], ps[bank][:, 0:CW], gr[:, 0:CW], ALU.mult),
                     reads=[PK(bank), grk, ("ssqp", b)], writes=[("zb", b)])

            def finish(b):
                S.op("dve", tcopy(ssq[:, 0:1], ssqp[:, b, 0:1]), reads=[("ssqp", b)], writes=["ssq"])
                for cg_ in range(1, NCG):
                    S.op("dve", tt(ssq[:, 0:1], ssq[:, 0:1], ssqp[:, b, cg_:cg_ + 1], ALU.add), reads=[("ssqp", b), "ssq"], writes=["ssq"])
                rstd_ops(rstd[:, 0:1], ssq[:, 0:1], D, ["ssq"], ["rstd"])
                S.op("dve", stt(xb[b][:], zb[b][:], rstd[:, 0:1], xb[b][:], ALU.mult, ALU.add), reads=[("zb", b), "rstd", ("xb", b)], writes=[("xb", b)])

            pset = 0
            for ti in range(T // TT):
                r0 = ti * TT
                for s_ in range(NS):
                    S.dma("sp", dmaf(hT[:, s_, :], MIXT[l][s_, :, r0:r0 + TT]),
                          reads=[], writes=[("hT", 0), ("hT", 1)], skey=("mixld", s_ % 4))
                for b in range(nblk):
                    S.dma("sp", dmaf(xb[b][:], x_src[r0 + b * 128:r0 + (b + 1) * 128, :]), writes=[("xb", b)], skey=("xb", b))
                    S.op("dve", memset(ssqp[:, b, :], 0.0), writes=[("ssqp", b)])
                for cg in range(NCG):
                    banks = [pset * 4 + i for i in range(4)]
                    pset ^= 1
                    for k0 in range(0, NS, KT):
                        kn = min(KT, NS - k0)
                        wt, wk = load_w(wring, "out", l, k0, kn, cg * CW, CW)
                        for kk in range(kn):
                            kc = k0 + kk
                            for b in range(nblk):
                                S.op("pe", mm(ps[banks[b]][:, 0:CW], hT[:, kc, b * 128:(b + 1) * 128], wt[:, kk, 0:CW], kc == 0, kc == NS - 1),
                                     reads=[wk, ("hT", b)], writes=[PK(banks[b])])
                    for b in range(nblk):
                        epilogue(0, cg, b, banks[b])
                for b in range(nblk):
                    finish(b)
                    S.op("dve", memset(ssqp[:, b, :], 0.0), writes=[("ssqp", b)])
                    norm_transpose(xb[b], ("xb", b), b, hT, modc[:, l, 2, :], modc[:, l, 3, :], ssq, rstd, dst=zb[b], dk=("zb", b))
                if cfg.stop == 5:
                    continue
                for f0 in range(0, FC, 4):
                    nf = min(4, FC - f0)
                    banks = [pset * 4 + i for i in range(4)]
                    pset ^= 1
                    for half, coff in ((0, 0), (1, FF)):
                        for k0 in range(0, KC, KT):
                            kn = min(KT, KC - k0)
                            wt, wk = load_w(wring, "gu", l, k0, kn, coff + f0 * 128, nf * 128)
                            for kk in range(kn):
                                kc = k0 + kk
                                for j in range(nf):
                                    S.op("pe", mm(ps[banks[j]][:, half * TT:(half + 1) * TT], wt[:, kk, j * 128:(j + 1) * 128], hT[:, kc, :],
                                                  kc == 0, kc == KC - 1),
                                         reads=[wk, ("hT", 0), ("hT", 1)], writes=[PK(banks[j])])
                    for j in range(nf):
                        sl, slk = sl_r.next()
                        S.op("act", act(sl[:, 0:TT], ps[banks[j]][:, 0:TT], AF.Silu), reads=[PK(banks[j])], writes=[slk])
                        S.op("dve", tt(actT[:, f0 + j, :], sl[:, 0:TT], ps[banks[j]][:, TT:2 * TT], ALU.mult),
                             reads=[slk, PK(banks[j])], writes=[("actT", f0 + j)])
                if cfg.stop == 6:
                    continue
                for cg in range(NCG):
                    banks = [pset * 4 + i for i in range(4)]
                    pset ^= 1
                    for k0 in range(0, FC, KT):
                        kn = min(KT, FC - k0)
                        wt, wk = load_w(wring, "down", l, k0, kn, cg * CW, CW)
                        for kk in range(kn):
                            kc = k0 + kk
                            for b in range(nblk):
                                S.op("pe", mm(ps[banks[b]][:, 0:CW], actT[:, kc, b * 128:(b + 1) * 128], wt[:, kk, 0:CW], kc == 0, kc == FC - 1),
                                     reads=[wk, ("actT", kc)], writes=[PK(banks[b])])
                    for b in range(nblk):
                        epilogue(1, cg, b, banks[b])
                for b in range(nblk):
                    finish(b)
                    S.dma("pool", dmaf(x_dst[r0 + b * 128:r0 + (b + 1) * 128, :], xb[b][:]), reads=[("xb", b)],
                          writes=[("xdst", ti, b)], skey=("xst", b))
            S.fence(final=(l == DEPTH - 1 or cfg.stop in (5, 6, 9)))
        if cfg.stop in (5, 6, 9):
            break
    es.close()
    return nc, S


def _consts(cfg, is_sample):
    T, NB, NSEG, RH = cfg.T, cfg.NB, cfg.NSEG, cfg.RH
    ident = np.eye(128, dtype=np.float32)
    perm = np.zeros((128, 128), np.float32)
    for m in range(128):
        blk = m // 32
        partner = m + 32 if blk % 2 == 0 else m - 32
        perm[partner, m] = 1.0
    j = np.arange(128)[:, None]
    i = np.arange(128)[None, :]
    same = (j // 32) == (i // 32)
    same16 = (j // 16) == (i // 16)
    mf = (same16 & (j <= i)).astype(np.float32)
    mb = (same16 & (j >= i)).astype(np.float32)
    mxf = (same & ((j % 32) < 16) & ((i % 32) >= 16)).astype(np.float32)
    mxb = (same & ((j % 32) >= 16) & ((i % 32) < 16)).astype(np.float32)
    ones = np.ones((128, 128), np.float32)
    cst = np.concatenate([ident, perm, mf, mb, ones, mxf, mxb], axis=1)
    lg_f = np.log1p(-np.exp2(-5.0 - np.arange(RH, dtype=np.float32))).astype(np.float32)
    lg = np.concatenate([lg_f, lg_f[::-1]])
    lgam = np.broadcast_to(lg[None, :], (128, 2 * RH)).astype(np.float32).copy()
    NK = 2 + NB
    maskb = np.zeros((NSEG, NK), np.float32)
    NEG = -30000.0
    if not is_sample:
        maskb[:] = NEG
        for q in range(NSEG):
            maskb[q, 2 + 2 * q] = 0.0
            maskb[q, 2 + 2 * q + 1] = 0.0
    maskb = np.broadcast_to(maskb.reshape(1, -1), (128, NSEG * NK)).astype(np.float32).copy()
    cosT = np.ones((128, T), np.float32)
    sinT = np.zeros((128, T), np.float32)
    if is_sample:
        t = np.arange(T)
        row = (t // 64).astype(np.float32)
        colp = (t % 64).astype(np.float32)
        half = 64
        inv = (10000.0 ** (-np.arange(0, half, 2, dtype=np.float32) / half)).astype(np.float32)
        for p in range(128):
            f = p % 32
            ang = (row if p < 64 else colp) * inv[f]
            cosT[p] = np.cos(ang.astype(np.float32))
            s = np.sin(ang.astype(np.float32))
            sinT[p] = -s if (p // 32) % 2 == 0 else s
    return cst, lgam, maskb, cosT, sinT


_CACHE = {}


def _run(cfg, inputs, n_sample, n_prompt_cores):
    D, T, RH = cfg.D, cfg.T, cfg.RH
    key = (D, T)
    if key not in _CACHE:
        _CACHE[key] = build(cfg)
    nc, _ = _CACHE[key]
    print("built: instructions", _CACHE[key][1].nins, flush=True)
    f = lambda a: np.ascontiguousarray(np.asarray(a), dtype=np.float32)
    w = {k: f(inputs[k]) for k in ("w_mod", "w_in", "w_out", "w_gu", "w_down")}
    cs, cp = _consts(cfg, True), _consts(cfg, False)
    gvec = np.stack([f(inputs["g_pre_mix"]), f(inputs["g_post_mix"]), f(inputs["g_pre_ffn"]), f(inputs["g_post_ffn"])], axis=1)
    gqk = np.stack([f(inputs["g_q"]), f(inputs["g_k"])], axis=1)
    ghead = np.concatenate([f(inputs["g_ret"]), f(inputs["g_hgrn"])], axis=1)
    in_maps = []
    ncores = n_sample + n_prompt_cores
    for c in range(ncores):
        m = {}
        for k in ("w_mod", "w_in", "w_out", "w_gu", "w_down"):
            m[k] = w[k]
        m["b_mod"] = f(inputs["b_mod"])
        m["gvec"], m["gqk"], m["ghead"], m["hg_lb"] = gvec, gqk, ghead, f(inputs["hg_lb"])
        if c < n_sample:
            cst, lgam, maskb, cosT, sinT = cs
            m["x"] = f(inputs["x_sample"][c])
            m["cond"] = f(inputs["c"][c]).reshape(1, D)
            m["ctxk"] = f(inputs["cache_k"][c]).reshape(DEPTH, PAST, cfg.KW)
            m["ctxv"] = f(inputs["cache_v"][c]).reshape(DEPTH, PAST, cfg.KW)
            sr, sh = f(inputs["state_ret"][c]), f(inputs["state_hgrn"][c])
            m["s0"] = np.ascontiguousarray(np.concatenate([sr, sh], axis=2))
            m["keep"] = np.ones((128, 1), np.float32)
        else:
            cst, lgam, maskb, cosT, sinT = cp
            pc = c - n_sample
            if pc < n_prompt_cores:
                m["x"] = f(inputs["x_prompt"][pc * cfg.NSEG:(pc + 1) * cfg.NSEG]).reshape(T, D)
            else:
                m["x"] = np.zeros((T, D), np.float32)
            m["cond"] = f(inputs["c_ctx"]).reshape(1, D)
            m["ctxk"] = np.zeros((DEPTH, PAST, cfg.KW), np.float32)
            m["ctxv"] = np.zeros((DEPTH, PAST, cfg.KW), np.float32)
            m["s0"] = np.zeros((DEPTH, 2, 2 * RH, 128, 128), np.float32)
            m["keep"] = np.zeros((128, 1), np.float32)
        m["cst"], m["lgam"], m["maskb"], m["cosT"], m["sinT"] = cst, lgam, maskb, cosT, sinT
        in_maps.append(m)
    res = run_bass_kernel_spmd(nc, in_maps, core_ids=list(range(ncores)))
    R = res.results
    y_sample = np.stack([R[c]["y"] for c in range(n_sample)], axis=0)
    pcs = range(n_sample, n_sample + n_prompt_cores)
    y_prompt = np.concatenate([R[c]["y"].reshape(cfg.NSEG, SEG, D) for c in pcs], axis=0)
    ck = np.concatenate([R[c]["ck"] for c in pcs], axis=0).reshape(-1, DEPTH, SEG, cfg.KVH, 128)
    cv = np.concatenate([R[c]["cv"] for c in pcs], axis=0).reshape(-1, DEPTH, SEG, cfg.KVH, 128)
    st = np.concatenate([R[c]["st"] for c in pcs], axis=0)
    return (y_prompt, y_sample, ck, cv, np.ascontiguousarray(st[:, :, :, :RH]), np.ascontiguousarray(st[:, :, :, RH:]))


def kernel(**inputs):
    cfg = Cfg(4096, 4096)
    return _run(cfg, inputs, 4, 2)
```

```python
import math
from contextlib import ExitStack
import numpy as np
import concourse.bass as bass
import concourse.mybir as mybir
from concourse.bass_utils import run_bass_kernel_spmd

F32 = mybir.dt.float32
BF16 = mybir.dt.bfloat16
ALU = mybir.AluOpType
AF = mybir.ActivationFunctionType
EPS = 1e-6
NCORES = 8
DEPTH = 2
SEG = 256
PAST = 256


class Cfg:
    def __init__(self, D=4096, T=4096, stop=None):
        self.D, self.T, self.stop = D, T, stop
        self.KC = D // 128
        self.NS = D // 128
        self.RH = self.HH = self.NS // 4
        self.AH = self.NS // 2
        self.KVH = self.AH // 4
        self.RW, self.HW, self.AW, self.KW = self.RH * 128, self.HH * 128, self.AH * 128, self.KVH * 128
        self.NIN = 4 * self.RW + 5 * self.HW + self.AW + 2 * self.KW
        self.FF = 256 * (-(-8 * D // (3 * 256)))
        self.FC = self.FF // 128
        self.NSEG = T // SEG
        self.NB = T // 128
        o = 0
        self.col = {}
        for nm, w in [("rq", self.RW), ("rk", self.RW), ("rv", self.RW), ("rg", self.RW), ("hq", self.HW),
                      ("hzf", self.HW), ("hzb", self.HW), ("hi", self.HW), ("hg", self.HW), ("aq", self.AW),
                      ("ak", self.KW), ("av", self.KW)]:
            self.col[nm] = (o, w)
            o += w
        self.fm = {}
        i = 0
        for nm in ["rq", "rk", "rg", "hq", "hzf", "hzb", "hg", "aq", "ak"]:
            self.fm[nm] = i
            i += self.col[nm][1] // 128
        self.NFM = i
        self.tm = {}
        o = 0
        for nm in ["rv", "hi", "av", "ak"]:
            self.tm[nm] = o
            o += self.col[nm][1]
        self.NTM = o


class _Op:
    __slots__ = ("eng", "fn", "waits", "signal", "val", "epoch", "dsem")

    def __init__(self, eng, fn, epoch):
        self.eng, self.fn, self.waits, self.signal, self.val, self.epoch, self.dsem = eng, fn, [], False, 0, epoch, None


class Sched:
    CE = ("pe", "act", "dve", "pool")
    ALL = ("pe", "act", "dve", "pool", "sp")
    NDS = 96

    def __init__(self, nc, es):
        self.nc = nc
        self.sem = {e: es.enter_context(nc.semaphore("se_" + e)) for e in self.CE}
        self.dsem = [es.enter_context(nc.semaphore("sd%d" % i)) for i in range(self.NDS)]
        self.dcnt = [0] * self.NDS
        self.dpersist = set()
        self.cnt = {e: 0 for e in self.CE}
        self.ops = {e: [] for e in self.ALL}
        self.lastw, self.rd = {}, {}
        self.k2s = {}
        self.epoch = 0
        self.waited = {e: {} for e in self.ALL}
        self.nins = 0

    def _dep(self, o, ev, is_dma):
        if ev is None:
            return
        if ev[0] == "op":
            p = ev[1]
            if p.epoch < self.epoch:
                return
            if p.eng == o.eng and p.eng == "pe" and not is_dma:
                return
            p.signal = True
            o.waits.append(ev)
        else:
            if ev[3] < self.epoch and ev[1] not in self.dpersist:
                return
            o.waits.append(ev)

    def _track(self, o, reads, writes, ev, is_dma):
        for k in reads:
            self._dep(o, self.lastw.get(k), is_dma)
        for k in writes:
            self._dep(o, self.lastw.get(k), is_dma)
            for e2 in self.rd.get(k, {}).values():
                self._dep(o, e2, is_dma)
        rk = ("e", o.eng) if ev[0] == "op" else ("d", ev[1])
        for k in reads:
            self.rd.setdefault(k, {})[rk] = ev
        for k in writes:
            self.lastw[k] = ev
            self.rd[k] = {}

    cap = None

    def op(self, eng, fn, reads=(), writes=()):
        if self.cap is not None:
            self.cap.append((0, eng, fn, tuple(reads), tuple(writes), None))
            return None
        o = _Op(eng, fn, self.epoch)
        self._track(o, reads, writes, ("op", o), False)
        self.ops[eng].append(o)
        return o

    def replay_interleaved(self, lists):
        n = max(len(L) for L in lists)
        for i in range(n):
            for L in lists:
                if i < len(L):
                    k, eng, fn, rd, wr, sk = L[i]
                    if k == 0:
                        self.op(eng, fn, rd, wr)
                    else:
                        self.dma(eng, fn, rd, wr, skey=sk)

    def dma(self, q, fn, reads=(), writes=(), skey=None, persist=False):
        if self.cap is not None:
            self.cap.append((1, q, fn, tuple(reads), tuple(writes), skey))
            return None
        o = _Op(q, fn, self.epoch)
        if skey not in self.k2s:
            used = set(self.k2s.values()) | self.dpersist
            rng = range(0, 60) if q == "sp" else range(60, self.NDS)
            si = next(i for i in rng if i not in used)
            self.k2s[skey] = si
            if persist:
                self.dpersist.add(si)
        si = self.k2s[skey]
        if self.dcnt[si]:
            o.waits.append(("dma", si, self.dcnt[si], self.epoch))
        self.dcnt[si] += 16
        o.dsem = si
        self._track(o, reads, writes, ("dma", si, self.dcnt[si], self.epoch), True)
        self.ops[q].append(o)
        return o

    def fence(self, final=False):
        lasts = {}
        for e in self.CE:
            if self.ops[e]:
                cands = [o for o in self.ops[e] if o.fn is not None and o.dsem is None]
                if cands:
                    cands[-1].signal = True
                    lasts[e] = cands[-1]
        for e in self.ALL:
            o = _Op(e, None, self.epoch)
            for f, p in lasts.items():
                if f != e:
                    o.waits.append(("op", p))
            for si in range(self.NDS):
                if self.dcnt[si] and (final or si not in self.dpersist):
                    o.waits.append(("dma", si, self.dcnt[si], self.epoch))
            self.ops[e].append(o)
        self.emit()
        self.epoch += 1
        self.lastw = {k: v for k, v in self.lastw.items() if v[0] == "dma" and v[1] in self.dpersist}
        self.rd = {}
        self.k2s = {k: v for k, v in self.k2s.items() if v in self.dpersist}

    def emit(self):
        nc = self.nc
        for e in self.CE:
            for o in self.ops[e]:
                if o.signal:
                    self.cnt[e] += 1
                    o.val = self.cnt[e]

        def run(ename, eng):
            wd = self.waited[ename]
            for o in self.ops[ename]:
                for ev in o.waits:
                    if ev[0] == "op":
                        s, v = self.sem[ev[1].eng], ev[1].val
                        key = ev[1].eng
                    else:
                        s, v = self.dsem[ev[1]], ev[2]
                        key = ev[1]
                    if wd.get(key, 0) >= v:
                        continue
                    wd[key] = v
                    eng.wait_ge(s, v)
                if o.fn is None:
                    continue
                ins = o.fn(eng)
                self.nins += 1
                if o.dsem is not None:
                    ins.then_inc(self.dsem[o.dsem], 16)
                elif o.signal:
                    ins.then_inc(self.sem[ename], 1)

        with nc.Block() as block:
            @block.tensor
            def _(t):
                run("pe", t)

            @block.scalar
            def _(a):
                run("act", a)

            @block.vector
            def _(v):
                run("dve", v)

            @block.gpsimd
            def _(g):
                run("pool", g)

            @block.sync
            def _(s):
                run("sp", s)
        self.ops = {e: [] for e in self.ALL}


class Ring:
    def __init__(self, name, tiles):
        self.name, self.tiles, self.i = name, tiles, 0

    def next(self):
        i = self.i % len(self.tiles)
        self.i += 1
        return self.tiles[i], (self.name, i)


def build(cfg):
    D, T, KC, NS, RH, HH, AH, KVH = cfg.D, cfg.T, cfg.KC, cfg.NS, cfg.RH, cfg.HH, cfg.AH, cfg.KVH
    FF, FC, NIN, KW, NB, NSEG = cfg.FF, cfg.FC, cfg.NIN, cfg.KW, cfg.NB, cfg.NSEG
    nc = bass.Bass("TRN2", target_bir_lowering=False)
    es = ExitStack()

    def din(name, shape, dt=F32):
        return nc.dram_tensor(name, list(shape), dt, kind="ExternalInput").ap()

    def dout(name, shape):
        return nc.dram_tensor(name, list(shape), F32, kind="ExternalOutput").ap()

    def dint(name, shape, dt=F32):
        return nc.dram_tensor(name, list(shape), dt, kind="Internal").ap()

    x_in = din("x", [T, D])
    cond_in = din("cond", [1, D])
    ctxk_in = din("ctxk", [DEPTH, PAST, KW])
    ctxv_in = din("ctxv", [DEPTH, PAST, KW])
    s0_in = din("s0", [DEPTH, 2, 2 * RH, 128, 128])
    keep_in = din("keep", [128, 1])
    maskb_in = din("maskb", [128, NSEG * (2 + NB)])
    cos_in = din("cosT", [128, T])
    sin_in = din("sinT", [128, T])
    cst_in = din("cst", [128, 7 * 128])
    lgam_in = din("lgam", [128, 2 * RH])
    wsh = {
        "mod": din("w_mod", [DEPTH, D, 6 * D]), "in": din("w_in", [DEPTH, D, NIN]),
        "out": din("w_out", [DEPTH, D, D]), "gu": din("w_gu", [DEPTH, D, 2 * FF]),
        "down": din("w_down", [DEPTH, FF, D]),
    }
    wshape = {"mod": (D, 6 * D), "in": (D, NIN), "out": (D, D), "gu": (D, 2 * FF), "down": (FF, D)}
    bmod_in = din("b_mod", [DEPTH, 6 * D])
    gvec_in = din("gvec", [DEPTH, 4, D])
    gqk_in = din("gqk", [DEPTH, 2, 128])
    ghead_in = din("ghead", [DEPTH, 2 * RH * 128])
    hglb_in = din("hg_lb", [DEPTH, 2, HH * 128])

    y_out = dout("y", [T, D])
    ck_out = dout("ck", [NSEG, DEPTH, SEG, KW])
    cv_out = dout("cv", [NSEG, DEPTH, SEG, KW])
    st_out = dout("st", [NSEG, DEPTH, 2, 2 * RH, 128, 128])

    wb = {k: [dint("wb_%s%d" % (k, l), list(wshape[k]), BF16) for l in range(DEPTH)] for k in wsh}
    PFT = [dint("pft%d" % l, [cfg.NFM, 128, T]) for l in range(DEPTH)]
    PTM = [dint("ptm%d" % l, [T, cfg.NTM]) for l in range(DEPTH)]
    OFW = [dint("ofw%d" % l, [2 * RH, 128, T]) for l in range(DEPTH)]
    MIXT = [dint("mixt%d" % l, [NS, 128, T], BF16) for l in range(DEPTH)]
    XMID = dint("xmid", [T, D])
    GROW = dint("grow", [DEPTH, 2, D])

    S = Sched(nc, es)

    names = []

    def sb(name, shape, dt=F32):
        return es2.enter_context(nc.sbuf_tensor("sb_" + name + "_%d" % len(names), list(shape), dt)) if not names.append(name) else None

    es2 = es
    cst = sb("cst", [128, 7 * 128])
    ident_f = cst[:, 0:128]
    ones_f = cst[:, 512:640]
    cstb = sb("cstb", [128, 7 * 128], BF16)
    ident_b, perm_b, maskf_b, maskb_b, ones_b = (cstb[:, i * 128:(i + 1) * 128] for i in range(5))
    maskf_f, maskbk_f = cst[:, 256:384], cst[:, 384:512]
    maskxf_f, maskxb_f = cst[:, 640:768], cst[:, 768:896]
    lgam = sb("lgam", [128, 2 * RH])
    keep = sb("keep", [128, 1])
    epsc = sb("epsc", [128, 1])
    maskb = sb("maskb", [128, NSEG * (2 + NB)])
    modc = sb("modc", [128, DEPTH, 4, KC])
    gqkc = sb("gqkc", [128, DEPTH, 2])
    gheadc = sb("gheadc", [128, DEPTH, 2 * RH])
    lbc = sb("lbc", [128, DEPTH, 2, HH])
    omlbc = sb("omlbc", [128, DEPTH, 2, HH])
    nomlbc = sb("nomlbc", [128, DEPTH, 2, HH])
    ps = [es.enter_context(nc.psum_tensor("ps%d" % i, [128, 512], F32)) for i in range(8)]
    PK = lambda i: ("ps", i)

    def mm(out, lhsT, rhs, start, stop):
        return lambda e: e.matmul(out, lhsT, rhs, start=start, stop=stop)

    def tcopy(out, in_):
        return lambda e: e.tensor_copy(out=out, in_=in_)

    def dmaf(out, in_, **kw):
        return lambda e: e.dma_start(out=out, in_=in_, **kw)

    def act(out, in_, func, bias=None, scale=None, accum_out=None):
        kw = {}
        if bias is not None:
            kw["bias"] = bias
        if scale is not None:
            kw["scale"] = scale
        if accum_out is not None:
            kw["accum_out"] = accum_out
        return lambda e: e.activation(out=out, in_=in_, func=func, **kw)

    def tscal(out, in0, s1, s2, op0, op1=None):
        if op1 is None:
            return lambda e: e.tensor_scalar(out=out, in0=in0, scalar1=s1, scalar2=None, op0=op0)
        return lambda e: e.tensor_scalar(out=out, in0=in0, scalar1=s1, scalar2=s2, op0=op0, op1=op1)

    def tt(out, in0, in1, op):
        return lambda e: e.tensor_tensor(out=out, in0=in0, in1=in1, op=op)

    def stt(out, in0, scalar, in1, op0, op1):
        return lambda e: e.scalar_tensor_tensor(out=out, in0=in0, scalar=scalar, in1=in1, op0=op0, op1=op1)

    def memset(ap, v):
        return lambda e: e.memset(ap, v)

    def rstd_ops(dst, src, n, rk, wk):
        S.op("act", act(dst, src, AF.Ln, bias=epsc[:, 0:1], scale=1.0 / n), reads=list(rk) + ["epsc"], writes=wk)
        S.op("act", act(dst, dst, AF.Exp, scale=-0.5), reads=wk, writes=wk)

    with ExitStack() as es2:
        order = [(k, l) for l in range(DEPTH) for k in ("mod", "in", "out", "gu", "down")]
        for k, l in order:
            rows, cols = wshape[k]
            for r in range(0, rows, 128):
                n = min(128, rows - r)
                for c0_ in range(0, cols, 8192):
                    cn = min(8192, cols - c0_)
                    S.dma("pool", dmaf(wb[k][l][r:r + n, c0_:c0_ + cn], wsh[k][l, r:r + n, c0_:c0_ + cn]),
                          writes=[("W", k, l)], skey="cast", persist=True)

        S.dma("sp", dmaf(cst[:], cst_in[:, :]), writes=["cst"], skey="c0")
        S.dma("sp", dmaf(lgam[:], lgam_in[:, :]), writes=["lgam"], skey="c1")
        S.dma("sp", dmaf(keep[:], keep_in[:, :]), writes=["keep"], skey="c2")
        S.dma("sp", dmaf(maskb[:], maskb_in[:, :]), writes=["maskb"], skey="c3")
        S.op("dve", tcopy(cstb[:], cst[:]), reads=["cst"], writes=["cstb"])
        S.op("dve", memset(epsc[:], EPS), writes=["epsc"])

        rows = sb("rows", [1, D])
        one11 = cst[0:1, 512:513]

        def row_to_cols(src_ap, n, dst, tag):
            S.dma("sp", dmaf(rows[0:1, 0:n * 128], src_ap), writes=["rows"], skey="rows")
            for j in range(n):
                S.op("pe", mm(ps[0][:, j:j + 1], rows[0:1, j * 128:(j + 1) * 128], one11, True, True),
                     reads=["rows", "cst"], writes=[PK(0)])
            S.op("dve", tcopy(dst, ps[0][:, 0:n]), reads=[PK(0)], writes=[tag])

        condc = sb("condc", [128, KC])
        row_to_cols(cond_in[0:1, :], KC, condc[:], "condc")
        scb = sb("scb", [128, KC], BF16)
        S.op("act", act(scb[:], condc[:], AF.Silu), reads=["condc"], writes=["scb"])
        screp = sb("screp", [128, KC, 128], BF16)
        for c in range(KC):
            S.op("dve", tcopy(screp[:, c, :], scb[:, c:c + 1].to_broadcast([128, 128])), reads=["scb"], writes=["screp"])
        gcol = sb("gcol", [128, 4, KC])
        tmpc = sb("tmpc", [128, 2 * max(KC, 2 * RH)])
        bmodb = sb("bmodb", [1, D], BF16)
        onesrow_b = cstb[0:1, 512:640]
        one11b = cstb[0:1, 512:513]
        wmod_ring = Ring("wmod", [sb("wmod%d" % i, [128, KC, 512], BF16) for i in range(2)])
        grep = sb("grep", [128, D])
        growst = sb("growst", [128, 512])
        for l in range(DEPTH):
            for i in range(4):
                row_to_cols(gvec_in[l, i:i + 1, :], KC, gcol[:, i, :], ("gcol", i))
            row_to_cols(gqk_in[l, 0:1, :], 1, gqkc[:, l, 0:1], "gqkc")
            row_to_cols(gqk_in[l, 1:2, :], 1, gqkc[:, l, 1:2], "gqkc")
            row_to_cols(ghead_in[l:l + 1, :], 2 * RH, gheadc[:, l, :], "gheadc")
            for part in range(6):
                S.dma("sp", dmaf(rows[0:1, :], bmod_in[l:l + 1, part * D:(part + 1) * D]), writes=["rows"], skey="rows")
                S.op("dve", tcopy(bmodb[:], rows[:]), reads=["rows"], writes=["bmodb"])
                for cg in range(D // 512 if D >= 512 else 1):
                    c0 = part * D + cg * 512
                    b0_ = cg * 512
                    wt, wk = wmod_ring.next()
                    S.dma("sp", dmaf(wt[:], wb["mod"][l][:, c0:c0 + 512].rearrange("(kc p) n -> p kc n", p=128)),
                          reads=[("W", "mod", l)], writes=[wk], skey=wk)
                    if part in (2, 5):
                        for kc in range(KC):
                            S.op("pe", mm(ps[1][:, :], screp[:, kc, :], wt[:, kc, :], kc == 0, False),
                                 reads=[wk, "screp"], writes=[PK(1)])
                        S.op("pe", mm(ps[1][:, :], onesrow_b, bmodb[0:1, b0_:b0_ + 512], False, True),
                             reads=["bmodb", "cstb"], writes=[PK(1)])
                        gi = 1 if part == 2 else 3
                        if cg == 0:
                            S.dma("sp", dmaf(grep[:], gvec_in[l, gi:gi + 1, :].partition_broadcast(128)),
                                  writes=["grep"], skey="grep")
                        S.op("dve", tt(growst[:], ps[1][:, :], grep[:, cg * 512:(cg + 1) * 512], ALU.mult),
                             reads=[PK(1), "grep"], writes=["growst"])
                        S.dma("pool", dmaf(GROW[l, (0 if part == 2 else 1):(1 if part == 2 else 2), cg * 512:(cg + 1) * 512],
                                           growst[0:1, :]), reads=["growst"], writes=[("grow", l)], skey="growst")
                    else:
                        for j in range(4):
                            for kc in range(KC):
                                S.op("pe", mm(ps[2][:, j:j + 1], wt[:, kc, j * 128:(j + 1) * 128], scb[:, kc:kc + 1], kc == 0, False),
                                     reads=[wk, "scb"], writes=[PK(2)])
                            S.op("pe", mm(ps[2][:, j:j + 1], bmodb[0:1, b0_ + j * 128:b0_ + (j + 1) * 128], one11b, False, True),
                                 reads=["bmodb", "cstb"], writes=[PK(2)])
                        cs = slice(cg * 4, cg * 4 + 4)
                        if part in (0, 3):
                            S.op("dve", tcopy(modc[:, l, 1 if part == 0 else 3, cs], ps[2][:, 0:4]), reads=[PK(2)], writes=["modc"])
                        else:
                            gi = 0 if part == 1 else 2
                            S.op("dve", stt(modc[:, l, 0 if part == 1 else 2, cs], ps[2][:, 0:4], 1.0, gcol[:, gi, cs], ALU.add, ALU.mult),
                                 reads=[PK(2), ("gcol", gi)], writes=["modc"])
        for d in range(2):
            row_to_cols(hglb_in[0, d:d + 1, :], HH, tmpc[:, 0:HH], "tmpc0")
            row_to_cols(hglb_in[1, d:d + 1, :], HH, tmpc[:, HH:2 * HH], "tmpc1")
            S.op("dve", memset(lbc[:, 0, d, :], 0.0), writes=["lbc"])
            S.op("dve", tt(tmpc[:, 0:HH], tmpc[:, HH:2 * HH], tmpc[:, 0:HH], ALU.subtract), reads=["tmpc0", "tmpc1"], writes=["tmpc0"])
            S.op("act", act(lbc[:, 1, d, :], tmpc[:, 0:HH], AF.Sigmoid), reads=["tmpc0", "lbc"], writes=["lbc"])
        S.op("dve", tscal(omlbc[:], lbc[:], -1.0, 1.0, ALU.mult, ALU.add), reads=["lbc"], writes=["omlbc"])
        S.op("dve", tscal(nomlbc[:], omlbc[:], -1.0, None, ALU.mult), reads=["omlbc"], writes=["nomlbc"])
        S.fence(final=(cfg.stop == 0))
    if cfg.stop == 0:
        es.close()
        return nc, S

    def prenorm_tile(x_src, r0, nblk, xring, hT, A, B, ssq, rstd):
        for b in range(nblk):
            xt, xk = xring.next()
            S.dma("sp", dmaf(xt[:], x_src[r0 + b * 128:r0 + (b + 1) * 128, :]), writes=[xk], skey=xk)
            norm_transpose(xt, xk, b, hT, A, B, ssq, rstd)

    jk = {"i": 0}

    def norm_transpose(xt, xk, b, hT, A, B, ssq, rstd, dst=None, dk=None):
        dst = xt if dst is None else dst
        dk = xk if dk is None else dk
        S.op("dve", memset(ssq[:], 0.0), writes=["ssq"])
        S.op("act", act(junk_t[:, 0:D], xt[:], AF.Square, accum_out=ssq[:, 0:1]), reads=[xk, "ssq"], writes=["junk", "ssq"])
        rstd_ops(rstd[:, 0:1], ssq[:, 0:1], D, ["ssq"], ["rstd"])
        S.op("dve", tscal(dst[:], xt[:], rstd[:, 0:1], None, ALU.mult), reads=[xk, "rstd"], writes=[dk])
        for c in range(KC):
            s = jk["i"] % 32
            jk["i"] += 1
            pt = ps[s // 4][:, (s % 4) * 128:(s % 4 + 1) * 128]
            S.op("pe", lambda e, pt=pt, c=c: e.transpose(pt, dst[:, c * 128:(c + 1) * 128], ident_f),
                 reads=[dk, "cst"], writes=[PK(s // 4)])
            S.op("act", act(hT[:, c, b * 128:(b + 1) * 128], pt, AF.Identity, bias=B[:, c:c + 1], scale=A[:, c:c + 1]),
                 reads=[PK(s // 4), "modc"], writes=[("hT", b)])

    def load_w(wring, name, l, k0, kn, c0, ncol):
        wt, wk = wring.next()
        src = wb[name][l][k0 * 128:(k0 + kn) * 128, c0:c0 + ncol].rearrange("(kc p) n -> p kc n", p=128)
        S.dma("sp", dmaf(wt[:, 0:kn, 0:ncol], src), reads=[("W", name, l)], writes=[wk], skey=wk)
        return wt, wk

    KT = 16

    for l in range(DEPTH):
        x_src = x_in if l == 0 else XMID
        x_dst = XMID if l == 0 else y_out
        with ExitStack() as es2:
            TT = 512 if T >= 512 else T
            nblk = TT // 128
            hT = sb("hT", [128, KC, TT], BF16)
            xring = Ring("xin", [sb("xin%d" % i, [128, D]) for i in range(2)])
            junk_t = sb("junk", [128, D], BF16)
            wring = Ring("w", [sb("w%d" % i, [128, KT, 512], BF16) for i in range(3)])
            stg = Ring("stg", [sb("stg%d" % i, [128, 512]) for i in range(6)])
            ssq = sb("ssq", [128, 1])
            rstd = sb("rstd", [128, 1])
            pieces = []
            for nm in ["rq", "rk", "rv", "rg", "hq", "hzf", "hzb", "hi", "hg", "aq", "ak", "av", "ak_tm"]:
                base = nm[:2] if nm.endswith("_tm") else nm
                c0, w = cfg.col[base]
                kind = "tm" if nm in ("rv", "hi", "av", "ak_tm") else "fm"
                o = 0
                while o < w:
                    n = min(512, w - o)
                    pieces.append((kind, base, c0 + o, n, o))
                    o += n
            pset = 0
            for ti in range(T // TT):
                r0 = ti * TT
                prenorm_tile(x_src, r0, nblk, xring, hT, modc[:, l, 0, :], modc[:, l, 1, :], ssq, rstd)
                for kind, base, c0, n, o in pieces:
                    banks = [pset * 4 + i for i in range(4)]
                    pset ^= 1
                    nch = n // 128
                    for k0 in range(0, KC, KT):
                        kn = min(KT, KC - k0)
                        wt, wk = load_w(wring, "in", l, k0, kn, c0, n)
                        for kk in range(kn):
                            kc = k0 + kk
                            st, sp_ = kc == 0, kc == KC - 1
                            if kind == "fm":
                                for j in range(nch):
                                    S.op("pe", mm(ps[banks[j]][:, 0:TT], wt[:, kk, j * 128:(j + 1) * 128], hT[:, kc, :], st, sp_),
                                         reads=[wk] + [("hT", b) for b in range(nblk)], writes=[PK(banks[j])])
                            else:
                                for b in range(nblk):
                                    S.op("pe", mm(ps[banks[b]][:, 0:n], hT[:, kc, b * 128:(b + 1) * 128], wt[:, kk, 0:n], st, sp_),
                                         reads=[wk, ("hT", b)], writes=[PK(banks[b])])
                    nout = nch if kind == "fm" else nblk
                    for j in range(nout):
                        sg, sk = stg.next()
                        eng = "act" if j % 2 == 0 else "dve"
                        if kind == "fm":
                            src = ps[banks[j]][:, 0:TT]
                            fn = act(sg[:, 0:TT], src, AF.Copy) if eng == "act" else tcopy(sg[:, 0:TT], src)
                            S.op(eng, fn, reads=[PK(banks[j])], writes=[sk])
                            fi = cfg.fm[base] + (o // 128) + j
                            S.dma("pool", dmaf(PFT[l][fi, :, r0:r0 + TT], sg[:, 0:TT]), reads=[sk],
                                  writes=[("pft", fi, ti)], skey=sk)
                        else:
                            src = ps[banks[j]][:, 0:n]
                            fn = act(sg[:, 0:n], src, AF.Copy) if eng == "act" else tcopy(sg[:, 0:n], src)
                            S.op(eng, fn, reads=[PK(banks[j])], writes=[sk])
                            tc0 = cfg.tm[base] + o
                            S.dma("pool", dmaf(PTM[l][r0 + j * 128:r0 + (j + 1) * 128, tc0:tc0 + n], sg[:, 0:n]), reads=[sk],
                                  writes=[("ptm", base, ti)], skey=sk)
            S.fence(final=(cfg.stop == 1))
        if cfg.stop == 1:
            es.close()
            return nc, S

        with ExitStack() as es2:
            NK = 2 + NB
            TQ = 512 if T >= 512 else T
            KTall = sb("KTall", [128, KVH, PAST + T], BF16)
            Vall = sb("Vall", [128, NK, KW], BF16)
            ldr = Ring("ld", [sb("ld%d" % i, [128, 512]) for i in range(3)])
            tb_r = Ring("tb", [sb("tb%d" % i, [128, 512], BF16) for i in range(3)])
            f1 = Ring("f1", [sb("f1_%d" % i, [128, 512]) for i in range(3)])
            f2 = Ring("f2", [sb("f2_%d" % i, [128, 512]) for i in range(3)])
            cosr = Ring("cos", [sb("cos%d" % i, [128, 512]) for i in range(2)])
            sinr = Ring("sin", [sb("sin%d" % i, [128, 512]) for i in range(2)])
            qr_r = Ring("qr", [sb("qr%d" % i, [128, 512], BF16) for i in range(2)])
            pT_r = Ring("pT", [sb("pT%d" % i, [128, 512], BF16) for i in range(4)])
            ost = Ring("ost", [sb("ost%d" % i, [128, 512], BF16) for i in range(2)])
            kvld = Ring("kvld", [sb("kvld%d" % i, [128, KW]) for i in range(2)])
            kvo = Ring("kvo", [sb("kvo%d" % i, [128, KW]) for i in range(2)])
            gkrep = sb("gkrep", [128, 128])
            ssqh = sb("ssqh", [128, KVH])
            junk2 = sb("junk2", [128, 128], BF16)
            psi = {"i": 0}

            def psn(lo, hi):
                i = lo + psi["i"] % (hi - lo)
                psi["i"] += 1
                return i

            def norm_rope(src_dram, t0, n, gcol_ap, dst, dk, cs_t, cs_k, sn_t, sn_k):
                ld, lk = ldr.next()
                S.dma("sp", dmaf(ld[:, 0:n], src_dram), reads=[], writes=[lk], skey=lk)
                a1, k1 = f1.next()
                S.op("act", act(a1[:, 0:n], ld[:, 0:n], AF.Square), reads=[lk], writes=[k1])
                b0 = psn(0, 2)
                S.op("pe", mm(ps[b0][:, 0:n], ones_f, a1[:, 0:n], True, True), reads=[k1, "cst"], writes=[PK(b0)])
                a2, k2 = f2.next()
                rstd_ops(a2[:, 0:n], ps[b0][:, 0:n], 128, [PK(b0)], [k2])
                S.op("dve", stt(a1[:, 0:n], ld[:, 0:n], gcol_ap, a2[:, 0:n], ALU.mult, ALU.mult), reads=[lk, k2, "gqkc"], writes=[k1])
                tb, tk = tb_r.next()
                S.op("act", act(tb[:, 0:n], a1[:, 0:n], AF.Copy), reads=[k1], writes=[tk])
                S.op("pe", mm(ps[b0][:, 0:n], perm_b, tb[:, 0:n], True, True), reads=[tk, "cstb"], writes=[PK(b0)])
                S.op("dve", tt(a2[:, 0:n], ps[b0][:, 0:n], sn_t[:, 0:n], ALU.mult), reads=[PK(b0), sn_k], writes=[k2])
                S.op("pool", tt(a1[:, 0:n], a1[:, 0:n], cs_t[:, 0:n], ALU.mult), reads=[k1, cs_k], writes=[k1])
                S.op("dve", tt(dst, a1[:, 0:n], a2[:, 0:n], ALU.add), reads=[k1, k2], writes=[dk])

            for blk in range(2):
                kt_, kk_ = kvld.next()
                S.dma("sp", dmaf(kt_[:], ctxk_in[l, blk * 128:(blk + 1) * 128, :]), writes=[kk_], skey=kk_)
                for h in range(KVH):
                    b0 = psn(0, 2)
                    S.op("pe", lambda e, b0=b0, kt_=kt_, h=h: e.transpose(ps[b0][:, 0:128], kt_[:, h * 128:(h + 1) * 128], ident_f),
                         reads=[kk_, "cst"], writes=[PK(b0)])
                    S.op("dve", tcopy(KTall[:, h, blk * 128:(blk + 1) * 128], ps[b0][:, 0:128]), reads=[PK(b0)], writes=["KT"])
                vt_, vk_ = kvld.next()
                S.dma("sp", dmaf(vt_[:], ctxv_in[l, blk * 128:(blk + 1) * 128, :]), writes=[vk_], skey=vk_)
                S.op("act", act(Vall[:, blk, :], vt_[:], AF.Copy), reads=[vk_], writes=["V"])
            S.dma("sp", dmaf(gkrep[:], gqk_in[l, 1:2, :].partition_broadcast(128)), writes=["gkrep"], skey="gkrep")
            tmo = cfg.tm
            for b in range(NB):
                vt_, vk_ = kvld.next()
                S.dma("sp", dmaf(vt_[:], PTM[l][b * 128:(b + 1) * 128, tmo["av"]:tmo["av"] + KW]), writes=[vk_], skey=vk_)
                S.op("act", act(Vall[:, 2 + b, :], vt_[:], AF.Copy), reads=[vk_], writes=["V"])
                seg, ro = (b * 128) // SEG, (b * 128) % SEG
                S.dma("pool", dmaf(cv_out[seg, l, ro:ro + 128, :], vt_[:]), reads=[vk_], writes=[("cvo", b)], skey=("cvs", vk_))
                kt_, kk_ = kvld.next()
                S.dma("sp", dmaf(kt_[:], PTM[l][b * 128:(b + 1) * 128, tmo["ak"]:tmo["ak"] + KW]), writes=[kk_], skey=kk_)
                S.op("dve", memset(ssqh[:], 0.0), writes=["ssqh"])
                for h in range(KVH):
                    S.op("act", act(junk2[:], kt_[:, h * 128:(h + 1) * 128], AF.Square, accum_out=ssqh[:, h:h + 1]),
                         reads=[kk_, "ssqh"], writes=["ssqh", "junk2"])
                rstd_ops(ssqh[:], ssqh[:], 128, ["ssqh"], ["ssqh"])
                ko, kok = kvo.next()
                for h in range(KVH):
                    S.op("dve", stt(ko[:, h * 128:(h + 1) * 128], kt_[:, h * 128:(h + 1) * 128], ssqh[:, h:h + 1], gkrep[:], ALU.mult, ALU.mult),
                         reads=[kk_, "ssqh", "gkrep"], writes=[kok])
                S.dma("pool", dmaf(ck_out[seg, l, ro:ro + 128, :], ko[:]), reads=[kok], writes=[("cko", b)], skey=kok)
            for ti in range(T // TQ):
                t0 = ti * TQ
                ct, ck_ = cosr.next()
                S.dma("sp", dmaf(ct[:, 0:TQ], cos_in[:, t0:t0 + TQ]), writes=[ck_], skey=ck_)
                st_, sk_ = sinr.next()
                S.dma("sp", dmaf(st_[:, 0:TQ], sin_in[:, t0:t0 + TQ]), writes=[sk_], skey=sk_)
                for h in range(KVH):
                    norm_rope(PFT[l][cfg.fm["ak"] + h, :, t0:t0 + TQ], t0, TQ, gqkc[:, l, 1:2],
                              KTall[:, h, PAST + t0:PAST + t0 + TQ], "KT", ct, ck_, st_, sk_)
            scale = 1.0 / math.sqrt(128.0)
            for ti in range(T // TQ):
                t0 = ti * TQ
                ct, ck_ = cosr.next()
                S.dma("sp", dmaf(ct[:, 0:TQ], cos_in[:, t0:t0 + TQ]), writes=[ck_], skey=ck_)
                st_, sk_ = sinr.next()
                S.dma("sp", dmaf(st_[:, 0:TQ], sin_in[:, t0:t0 + TQ]), writes=[sk_], skey=sk_)
                for h in range(AH):
                    kvh = h // 4
                    qr, qk = qr_r.next()
                    norm_rope(PFT[l][cfg.fm["aq"] + h, :, t0:t0 + TQ], t0, TQ, gqkc[:, l, 0:1], qr[:, 0:TQ], qk, ct, ck_, st_, sk_)
                    bo, bd = psn(2, 4), psn(4, 6)
                    for kb in range(NK):
                        bs_ = psn(6, 8)
                        S.op("pe", mm(ps[bs_][:, 0:TQ], KTall[:, kvh, kb * 128:(kb + 1) * 128], qr[:, 0:TQ], True, True),
                             reads=["KT", qk], writes=[PK(bs_)])
                        pT, pk = pT_r.next()
                        for qh in range(TQ // SEG):
                            gq = (t0 // SEG) + qh
                            S.op("act", act(pT[:, qh * SEG:(qh + 1) * SEG], ps[bs_][:, qh * SEG:(qh + 1) * SEG], AF.Exp,
                                            bias=maskb[:, gq * NK + kb:gq * NK + kb + 1], scale=scale),
                                 reads=[PK(bs_), "maskb"], writes=[pk])
                        S.op("pe", mm(ps[bo][:, 0:TQ], Vall[:, kb, kvh * 128:(kvh + 1) * 128], pT[:, 0:TQ], kb == 0, kb == NK - 1),
                             reads=["V", pk], writes=[PK(bo)])
                        S.op("pe", mm(ps[bd][:, 0:TQ], ones_b, pT[:, 0:TQ], kb == 0, kb == NK - 1),
                             reads=["cstb", pk], writes=[PK(bd)])
                    a2, k2 = f2.next()
                    S.op("dve", lambda e, a2=a2, bd=bd: e.reciprocal(out=a2[:, 0:TQ], in_=ps[bd][:, 0:TQ]), reads=[PK(bd)], writes=[k2])
                    og, ok_ = ost.next()
                    S.op("dve", tt(og[:, 0:TQ], ps[bo][:, 0:TQ], a2[:, 0:TQ], ALU.mult), reads=[PK(bo), k2], writes=[ok_])
                    S.dma("pool", dmaf(MIXT[l][2 * RH + h, :, t0:t0 + TQ], og[:, 0:TQ]), reads=[ok_], writes=[("mixt", 2 * RH + h, ti)], skey=ok_)
            S.fence(final=(cfg.stop == 2))
        if cfg.stop == 2:
            es.close()
            return nc, S

        with ExitStack() as es2:
            NMH = 2 * RH
            G = RH
            Sst = sb("Sst", [128, 2, NMH, 128])
            Sbf = sb("Sbf", [128, 2, NMH, 128], BF16)

            def mk(name, n, shape, dt=F32):
                return Ring(name, [sb("%s%d" % (name, i), shape, dt) for i in range(n)])
            ldq, ldk, ldz, ldg, ldv, ldo = (mk(nm, G, [128, 128]) for nm in ("ldq", "ldk", "ldz", "ldg", "ldv", "ldo"))
            csr = mk("csr", G, [128, 256])
            qf_r, kf_r, gf_r = (mk(nm, G, [128, 128]) for nm in ("qf", "kf", "gf"))
            Bc_r = mk("Bc", G, [128, 132])
            ar_r = mk("ar", 8 * G, [128, 128])
            ex_r = mk("ex", 6 * G, [128, 128])
            dec_r = mk("dec", G, [128, 4])
            qd_r, kd_r, qs_r, qx_r, kx_r, kut_r, vb_r, am_r, am2_r, mx_r = (
                mk(nm, G, [128, 128], BF16) for nm in ("qd", "kd", "qs", "qx", "kx", "kut", "vb", "am", "am2", "mx"))
            tb2 = mk("tb2", 2 * G, [128, 128], BF16)
            ku_r = mk("ku", G, [128, 128])
            of_r, o2_r, o3_r = (mk(nm, G, [128, 128]) for nm in ("of", "o2", "o3"))
            zer = sb("zer", [128, 128])
            one_t = sb("one_t", [128, 128])
            S.op("dve", memset(zer[:], 0.0), writes=["zer"])
            S.op("dve", memset(one_t[:], 1.0), writes=["one_t"])
            psj = {"i": 0}

            pqc = {}

            def pq(lo, hi):
                n = (hi - lo) * 4
                i = pqc["u"] % n
                bank, q = lo + i // 4, i % 4
                return ps[bank][:, q * 128:(q + 1) * 128], PK(bank)

            for d in range(2):
                for mh in range(NMH):
                    S.dma("sp", dmaf(Sst[:, d, mh, :], s0_in[l, d, mh, :, :]), writes=[("S", d, mh)], skey=("sld", d, mh % 4))
                    S.op("act", act(Sbf[:, d, mh, :], Sst[:, d, mh, :], AF.Copy), reads=[("S", d, mh)], writes=[("Sb", d, mh)])

            def gla_block(d, mh, blk):
                isret = mh < RH
                h = mh if isret else mh - RH
                t0 = blk * 128
                fmq = cfg.fm["rq" if isret else "hq"] + h
                ql, qlk = ldq.next()
                S.dma("sp", dmaf(ql[:], PFT[l][fmq, :, t0:t0 + 128]), writes=[qlk], skey=(qlk[0], qlk[1] % 6))
                vl, vlk = ldv.next()
                vc0 = cfg.tm["rv" if isret else "hi"] + h * 128
                S.dma("sp", dmaf(vl[:], PTM[l][t0:t0 + 128, vc0:vc0 + 128]), writes=[vlk], skey=(vlk[0], vlk[1] % 6))
                vb, vbk = vb_r.next()
                S.op("act", act(vb[:], vl[:], AF.Copy), reads=[vlk], writes=[vbk])
                qf, qfk = qf_r.next()
                kf, kfk = kf_r.next()
                gf, gfk = gf_r.next()
                if isret:
                    kl, klk = ldk.next()
                    S.dma("sp", dmaf(kl[:], PFT[l][cfg.fm["rk"] + h, :, t0:t0 + 128]), writes=[klk], skey=(klk[0], klk[1] % 6))
                    cs, csk = csr.next()
                    S.dma("sp", dmaf(cs[:, 0:128], cos_in[:, t0:t0 + 128]), writes=[csk], skey=(csk[0], csk[1] % 6))
                    S.dma("sp", dmaf(cs[:, 128:256], sin_in[:, t0:t0 + 128]), writes=[csk], skey=(csk[0], csk[1] % 6))
                    for (src, sk_, dst, dk_, sc) in ((ql, qlk, qf, qfk, 1.0), (kl, klk, kf, kfk, 128.0 ** -0.5)):
                        tb, tbk = tb2.next()
                        S.op("act", act(tb[:], src[:], AF.Copy), reads=[sk_], writes=[tbk])
                        b0, b0k = pq(0, 2)
                        S.op("pe", mm(b0, perm_b, tb[:], True, True), reads=[tbk, "cstb"], writes=[b0k])
                        ar, ark = ar_r.next()
                        S.op("dve", tt(ar[:], b0, cs[:, 128:256], ALU.mult), reads=[b0k, csk], writes=[ark])
                        S.op("pool", tt(dst[:], src[:], cs[:, 0:128], ALU.mult), reads=[sk_, csk], writes=[dk_])
                        S.op("dve", tt(dst[:], dst[:], ar[:], ALU.add), reads=[dk_, ark], writes=[dk_])
                        if sc != 1.0:
                            S.op("dve", tscal(dst[:], dst[:], sc, None, ALU.mult), reads=[dk_], writes=[dk_])
                    S.op("dve", tscal(gf[:], one_t[:], lgam[:, d * RH + h:d * RH + h + 1], None, ALU.mult),
                         reads=["one_t", "lgam"], writes=[gfk])
                else:
                    S.op("act", act(qf[:], ql[:], AF.Silu), reads=[qlk], writes=[qfk])
                    zl, zlk = ldz.next()
                    S.dma("sp", dmaf(zl[:], PFT[l][cfg.fm["hzf" if d == 0 else "hzb"] + h, :, t0:t0 + 128]), writes=[zlk], skey=(zlk[0], zlk[1] % 6))
                    ar, ark = ar_r.next()
                    S.op("act", act(ar[:], zl[:], AF.Sigmoid), reads=[zlk], writes=[ark])
                    S.op("act", act(gf[:], ar[:], AF.Ln, bias=lbc[:, l, d, h:h + 1], scale=omlbc[:, l, d, h:h + 1]),
                         reads=[ark, "lbc", "omlbc"], writes=[gfk])
                    S.op("act", act(kf[:], zl[:], AF.Sigmoid, scale=-1.0), reads=[zlk], writes=[kfk])
                    S.op("dve", tscal(kf[:], kf[:], omlbc[:, l, d, h:h + 1], None, ALU.mult), reads=[kfk, "omlbc"], writes=[kfk])
                Bc, Bck = Bc_r.next()
                S.op("dve", memset(Bc[:, 0:1], 0.0), writes=[Bck])
                S.op("dve", lambda e, Bc=Bc, gf=gf: e.tensor_tensor_scan(out=Bc[:, 1:129], data0=one_t[:], data1=gf[:], initial=0.0,
                                                                        op0=ALU.mult, op1=ALU.add),
                     reads=[gfk, "one_t", Bck], writes=[Bck])
                Binc, Eexc = Bc[:, 1:129], Bc[:, 0:128]

                def bc4(col_ap):
                    return col_ap.unsqueeze(2).to_broadcast([128, 4, 32])

                def bc8(col_ap):
                    return col_ap.unsqueeze(2).to_broadcast([128, 8, 16])

                def v3(ap):
                    return ap.rearrange("p (a b) -> p a b", b=32)

                def v16(ap):
                    return ap.rearrange("p (a b) -> p a b", b=16)
                if d == 0:
                    X = Binc
                    m16 = v16(Binc)[:, :, 8]
                    rho = v3(Eexc)[:, :, 0]
                    rho_n = v3(Binc)[:, :, 31]
                    rhox = v3(Binc)[:, :, 15]
                    specs = [("qd", qf, qfk, X, m16, 1.0, True, False), ("kd", kf, kfk, X, m16, -1.0, True, False),
                             ("qx", qf, qfk, X, rhox, 1.0, False, True), ("kx", kf, kfk, X, rhox, -1.0, False, True),
                             ("qs", qf, qfk, X, rho, 1.0, False, True), ("ku", kf, kfk, X, rho_n, -1.0, False, True)]
                else:
                    X = Eexc
                    m16 = v16(Eexc)[:, :, 8]
                    rho_e = v3(Binc)[:, :, 31]
                    rho_s = v3(Eexc)[:, :, 0]
                    rhox = v3(Eexc)[:, :, 16]
                    specs = [("qd", qf, qfk, X, m16, -1.0, True, False), ("kd", kf, kfk, X, m16, 1.0, True, False),
                             ("qx", qf, qfk, X, rhox, -1.0, False, True), ("kx", kf, kfk, X, rhox, 1.0, False, True),
                             ("qs", qf, qfk, X, rho_e, -1.0, False, True), ("ku", kf, kfk, X, rho_s, 1.0, False, True)]
                outs = {}
                for nm, base, bk, Xa, ref, sgn, six, clamp in specs:
                    ar, ark = ar_r.next()
                    if six:
                        S.op("dve", lambda e, ar=ar, Xa=Xa, ref=ref: e.tensor_tensor(out=v16(ar[:]), in0=v16(Xa), in1=bc8(ref), op=ALU.subtract),
                             reads=[Bck], writes=[ark])
                    else:
                        S.op("dve", lambda e, ar=ar, Xa=Xa, ref=ref: e.tensor_tensor(out=v3(ar[:]), in0=v3(Xa), in1=bc4(ref), op=ALU.subtract),
                             reads=[Bck], writes=[ark])
                    ex, exk = ex_r.next()
                    if clamp:
                        S.op("pool", tscal(ar[:], ar[:], sgn, 0.0, ALU.mult, ALU.min), reads=[ark], writes=[ark])
                        S.op("act", act(ex[:], ar[:], AF.Exp), reads=[ark], writes=[exk])
                    else:
                        S.op("act", act(ex[:], ar[:], AF.Exp, scale=sgn), reads=[ark], writes=[exk])
                    ring = {"qd": qd_r, "kd": kd_r, "qx": qx_r, "kx": kx_r, "qs": qs_r, "ku": ku_r}[nm]
                    ot, otk = ring.next()
                    S.op("pool" if nm in ("qs", "ku", "qx") else "dve", tt(ot[:], base[:], ex[:], ALU.mult), reads=[bk, exk], writes=[otk])
                    outs[nm] = (ot, otk)
                dec, deck = dec_r.next()
                S.op("dve", tt(dec[:], v3(Binc)[:, :, 31], v3(Eexc)[:, :, 0], ALU.subtract), reads=[Bck], writes=[deck])
                S.op("act", act(dec[:], dec[:], AF.Exp), reads=[deck], writes=[deck])
                ku, kuk = outs["ku"]
                bt, btk = pq(0, 2)
                S.op("pe", lambda e, bt=bt, ku=ku: e.transpose(bt, ku[:], ident_f), reads=[kuk, "cst"], writes=[btk])
                kut, kutk = kut_r.next()
                S.op("dve", tcopy(kut[:], bt), reads=[btk], writes=[kutk])
                qd, qdk = outs["qd"]
                kd, kdk = outs["kd"]
                qs, qsk = outs["qs"]
                ba, bak = pq(2, 4)
                S.op("pe", mm(ba, kd[:], qd[:], True, True), reads=[kdk, qdk], writes=[bak])
                am, amk = am_r.next()
                S.op("dve", tt(am[:], ba, maskf_f if d == 0 else maskbk_f, ALU.mult), reads=[bak, "cst"], writes=[amk])
                qx, qxk = outs["qx"]
                kx, kxk = outs["kx"]
                ba2, ba2k = pq(2, 4)
                S.op("pe", mm(ba2, kx[:], qx[:], True, True), reads=[kxk, qxk], writes=[ba2k])
                am2, am2k = am2_r.next()
                S.op("dve", tt(am2[:], ba2, maskxf_f if d == 0 else maskxb_f, ALU.mult), reads=[ba2k, "cst"], writes=[am2k])
                S.op("dve", tt(am[:], am[:], am2[:], ALU.add), reads=[amk, am2k], writes=[amk])
                bo, bok = pq(4, 6)
                S.op("pe", mm(bo, vb[:], am[:], True, True), reads=[vbk, amk], writes=[bok])
                of, ofk = of_r.next()
                S.op("dve", tcopy(of[:], bo), reads=[bok], writes=[ofk])
                bi, bik = pq(4, 6)
                subs = range(4) if d == 0 else range(3, -1, -1)
                for n_i, I in enumerate(subs):
                    last = n_i == 3
                    S.op("pe", mm(bi[:, I * 32:(I + 1) * 32], Sbf[:, d, mh, :], qs[:, I * 32:(I + 1) * 32], True, True),
                         reads=[("Sb", d, mh), qsk], writes=[bik])
                    bs_, bsk = pq(6, 8)
                    S.op("pe", lambda e, bs_=bs_, I=I, kut=kut, vb=vb: e.matmul(bs_, kut[I * 32:(I + 1) * 32, :], vb[I * 32:(I + 1) * 32, :],
                                                                           start=True, stop=True, tile_position=((I * 32), 0)),
                         reads=[kutk, vbk], writes=[bsk])
                    S.op("dve", stt(Sst[:, d, mh, :], Sst[:, d, mh, :], dec[:, I:I + 1], bs_, ALU.mult, ALU.add),
                         reads=[("S", d, mh), deck, bsk], writes=[("S", d, mh)])
                    S.op("act", act(Sbf[:, d, mh, :], Sst[:, d, mh, :], AF.Copy), reads=[("S", d, mh)], writes=[("Sb", d, mh)])
                endseg = (d == 0 and (t0 + 128) % SEG == 0) or (d == 1 and t0 % SEG == 0)
                if endseg:
                    seg = t0 // SEG
                    S.dma("pool", dmaf(st_out[seg, l, d, mh, :, :], Sst[:, d, mh, :]), reads=[("S", d, mh)], writes=[("sto", seg, d, mh)],
                          skey=("sst", d, mh % 4))
                    S.op("dve", tscal(Sst[:, d, mh, :], Sst[:, d, mh, :], keep[:, 0:1], None, ALU.mult), reads=[("S", d, mh), "keep"], writes=[("S", d, mh)])
                    S.op("act", act(Sbf[:, d, mh, :], Sst[:, d, mh, :], AF.Copy), reads=[("S", d, mh)], writes=[("Sb", d, mh)])
                if d == 0:
                    S.op("dve", tt(of[:], of[:], bi, ALU.add), reads=[ofk, bik], writes=[ofk])
                    S.dma("pool", dmaf(OFW[l][mh, :, t0:t0 + 128], of[:]), reads=[ofk], writes=[("ofw", mh, blk)], skey=ofk)
                else:
                    lo, lok = ldo.next()
                    S.dma("sp", dmaf(lo[:], OFW[l][mh, :, t0:t0 + 128]), reads=[("ofw", mh, blk)], writes=[lok], skey=(lok[0], lok[1] % 6))
                    o2, o2k = o2_r.next()
                    S.op("dve", tt(o2[:], of[:], bi, ALU.add), reads=[ofk, bik], writes=[o2k])
                    S.op("dve", tt(o2[:], o2[:], lo[:], ALU.add), reads=[o2k, lok], writes=[o2k])
                    o3, o3k = o3_r.next()
                    bn, bnk = pq(0, 2)
                    if isret:
                        S.op("pe", mm(bn, ones_f, o2[:], True, True), reads=[o2k, "cst"], writes=[bnk])
                        S.op("dve", stt(o2[:], bn, -1.0 / 128.0, o2[:], ALU.mult, ALU.add), reads=[bnk, o2k], writes=[o2k])
                    S.op("act", act(o3[:], o2[:], AF.Square), reads=[o2k], writes=[o3k])
                    S.op("pe", mm(bn, ones_f, o3[:], True, True), reads=[o3k, "cst"], writes=[bnk])
                    S.op("dve", tcopy(o3[:], bn), reads=[bnk], writes=[o3k])
                    rstd_ops(o3[:], o3[:], 128, [o3k], [o3k])
                    S.op("dve", stt(o2[:], o2[:], gheadc[:, l, mh:mh + 1], o3[:], ALU.mult, ALU.mult), reads=[o2k, o3k, "gheadc"], writes=[o2k])
                    gl, glk = ldg.next()
                    S.dma("sp", dmaf(gl[:], PFT[l][cfg.fm["rg" if isret else "hg"] + h, :, t0:t0 + 128]), writes=[glk], skey=(glk[0], glk[1] % 6))
                    S.op("act", act(gl[:], gl[:], AF.Silu if isret else AF.Sigmoid), reads=[glk], writes=[glk])
                    mx, mxk = mx_r.next()
                    S.op("dve", tt(mx[:], o2[:], gl[:], ALU.mult), reads=[o2k, glk], writes=[mxk])
                    S.dma("pool", dmaf(MIXT[l][mh, :, t0:t0 + 128], mx[:]), reads=[mxk], writes=[("mixt", mh, blk)], skey=mxk)

            def run_group(d, blk, heads):
                lists = []
                for ui, mh in enumerate(heads):
                    pqc["u"] = ui
                    S.cap = []
                    gla_block(d, mh, blk)
                    lists.append(S.cap)
                    S.cap = None
                S.replay_interleaved(lists)

            for blk in range(NB):
                for g0 in range(0, NMH, G):
                    run_group(0, blk, range(g0, g0 + G))
            for blk in range(NB - 1, -1, -1):
                for g0 in range(0, NMH, G):
                    run_group(1, blk, range(g0, g0 + G))
            S.fence(final=(cfg.stop == 3))
        if cfg.stop == 3:
            es.close()
            return nc, S

        with ExitStack() as es2:
            TT = 256
            nblk = 2
            hT = sb("hT3", [128, max(KC, NS), TT], BF16)
            actT = sb("actT", [128, FC, TT], BF16)
            wring = Ring("w3", [sb("w3_%d" % i, [128, KT, 512], BF16) for i in range(3)])
            xb = [sb("xb%d" % i, [128, D]) for i in range(2)]
            zb = [sb("zb%d" % i, [128, D]) for i in range(2)]
            junk_t = sb("junk3", [128, D], BF16)
            gr_r = Ring("gr", [sb("gr%d" % i, [128, 512]) for i in range(2)])
            sl_r = Ring("sl", [sb("sl%d" % i, [128, 256]) for i in range(3)])
            ssq = sb("ssq3", [128, 1])
            rstd = sb("rstd3", [128, 1])
            ssqp = sb("ssqp", [128, 2, 16])
            NCG = D // 512 if D >= 512 else 1
            CW = min(512, D)

            def epilogue(which, cg, b, bank):
                gr, grk = gr_r.next()
                S.dma("sp", dmaf(gr[:, 0:CW], GROW[l, which:which + 1, cg * CW:(cg + 1) * CW].partition_broadcast(128)),
                      reads=[("grow", l)], writes=[grk], skey=grk)
                S.op("act", act(junk_t[:, 0:CW], ps[bank][:, 0:CW], AF.Square, accum_out=ssqp[:, b, cg:cg + 1]),
                     reads=[PK(bank), ("ssqp", b)], writes=[("ssqp", b), "junk"])
                S.op("dve", tt(zb[b][:, cg * CW:(cg + 1) * CW], ps[bank][:, 0:CW], gr[:, 0:CW], ALU.mult),
                     reads=[PK(bank), grk, ("ssqp", b)], writes=[("zb", b)])

            def finish(b):
                S.op("dve", tcopy(ssq[:, 0:1], ssqp[:, b, 0:1]), reads=[("ssqp", b)], writes=["ssq"])
                for cg_ in range(1, NCG):
                    S.op("dve", tt(ssq[:, 0:1], ssq[:, 0:1], ssqp[:, b, cg_:cg_ + 1], ALU.add), reads=[("ssqp", b), "ssq"], writes=["ssq"])
                rstd_ops(rstd[:, 0:1], ssq[:, 0:1], D, ["ssq"], ["rstd"])
                S.op("dve", stt(xb[b][:], zb[b][:], rstd[:, 0:1], xb[b][:], ALU.mult, ALU.add), reads=[("zb", b), "rstd", ("xb", b)], writes=[("xb", b)])

            pset = 0
            for ti in range(T // TT):
                r0 = ti * TT
                for s_ in range(NS):
                    S.dma("sp", dmaf(hT[:, s_, :], MIXT[l][s_, :, r0:r0 + TT]),
                          reads=[], writes=[("hT", 0), ("hT", 1)], skey=("mixld", s_ % 4))
                for b in range(nblk):
                    S.dma("sp", dmaf(xb[b][:], x_src[r0 + b * 128:r0 + (b + 1) * 128, :]), writes=[("xb", b)], skey=("xb", b))
                    S.op("dve", memset(ssqp[:, b, :], 0.0), writes=[("ssqp", b)])
                for cg in range(NCG):
                    banks = [pset * 4 + i for i in range(4)]
                    pset ^= 1
                    for k0 in range(0, NS, KT):
                        kn = min(KT, NS - k0)
                        wt, wk = load_w(wring, "out", l, k0, kn, cg * CW, CW)
                        for kk in range(kn):
                            kc = k0 + kk
                            for b in range(nblk):
                                S.op("pe", mm(ps[banks[b]][:, 0:CW], hT[:, kc, b * 128:(b + 1) * 128], wt[:, kk, 0:CW], kc == 0, kc == NS - 1),
                                     reads=[wk, ("hT", b)], writes=[PK(banks[b])])
                    for b in range(nblk):
                        epilogue(0, cg, b, banks[b])
                for b in range(nblk):
                    finish(b)
                    S.op("dve", memset(ssqp[:, b, :], 0.0), writes=[("ssqp", b)])
                    norm_transpose(xb[b], ("xb", b), b, hT, modc[:, l, 2, :], modc[:, l, 3, :], ssq, rstd, dst=zb[b], dk=("zb", b))
                if cfg.stop == 5:
                    continue
                for f0 in range(0, FC, 4):
                    nf = min(4, FC - f0)
                    banks = [pset * 4 + i for i in range(4)]
                    pset ^= 1
                    for half, coff in ((0, 0), (1, FF)):
                        for k0 in range(0, KC, KT):
                            kn = min(KT, KC - k0)
                            wt, wk = load_w(wring, "gu", l, k0, kn, coff + f0 * 128, nf * 128)
                            for kk in range(kn):
                                kc = k0 + kk
                                for j in range(nf):
                                    S.op("pe", mm(ps[banks[j]][:, half * TT:(half + 1) * TT], wt[:, kk, j * 128:(j + 1) * 128], hT[:, kc, :],
                                                  kc == 0, kc == KC - 1),
                                         reads=[wk, ("hT", 0), ("hT", 1)], writes=[PK(banks[j])])
                    for j in range(nf):
                        sl, slk = sl_r.next()
                        S.op("act", act(sl[:, 0:TT], ps[banks[j]][:, 0:TT], AF.Silu), reads=[PK(banks[j])], writes=[slk])
                        S.op("dve", tt(actT[:, f0 + j, :], sl[:, 0:TT], ps[banks[j]][:, TT:2 * TT], ALU.mult),
                             reads=[slk, PK(banks[j])], writes=[("actT", f0 + j)])
                if cfg.stop == 6:
                    continue
                for cg in range(NCG):
                    banks = [pset * 4 + i for i in range(4)]
                    pset ^= 1
                    for k0 in range(0, FC, KT):
                        kn = min(KT, FC - k0)
                        wt, wk = load_w(wring, "down", l, k0, kn, cg * CW, CW)
                        for kk in range(kn):
                            kc = k0 + kk
                            for b in range(nblk):
                                S.op("pe", mm(ps[banks[b]][:, 0:CW], actT[:, kc, b * 128:(b + 1) * 128], wt[:, kk, 0:CW], kc == 0, kc == FC - 1),
                                     reads=[wk, ("actT", kc)], writes=[PK(banks[b])])
                    for b in range(nblk):
                        epilogue(1, cg, b, banks[b])
                for b in range(nblk):
                    finish(b)
                    S.dma("pool", dmaf(x_dst[r0 + b * 128:r0 + (b + 1) * 128, :], xb[b][:]), reads=[("xb", b)],
                          writes=[("xdst", ti, b)], skey=("xst", b))
            S.fence(final=(l == DEPTH - 1 or cfg.stop in (5, 6, 9)))
        if cfg.stop in (5, 6, 9):
            break
    es.close()
    return nc, S


def _consts(cfg, is_sample):
    T, NB, NSEG, RH = cfg.T, cfg.NB, cfg.NSEG, cfg.RH
    ident = np.eye(128, dtype=np.float32)
    perm = np.zeros((128, 128), np.float32)
    for m in range(128):
        blk = m // 32
        partner = m + 32 if blk % 2 == 0 else m - 32
        perm[partner, m] = 1.0
    j = np.arange(128)[:, None]
    i = np.arange(128)[None, :]
    same = (j // 32) == (i // 32)
    same16 = (j // 16) == (i // 16)
    mf = (same16 & (j <= i)).astype(np.float32)
    mb = (same16 & (j >= i)).astype(np.float32)
    mxf = (same & ((j % 32) < 16) & ((i % 32) >= 16)).astype(np.float32)
    mxb = (same & ((j % 32) >= 16) & ((i % 32) < 16)).astype(np.float32)
    ones = np.ones((128, 128), np.float32)
    cst = np.concatenate([ident, perm, mf, mb, ones, mxf, mxb], axis=1)
    lg_f = np.log1p(-np.exp2(-5.0 - np.arange(RH, dtype=np.float32))).astype(np.float32)
    lg = np.concatenate([lg_f, lg_f[::-1]])
    lgam = np.broadcast_to(lg[None, :], (128, 2 * RH)).astype(np.float32).copy()
    NK = 2 + NB
    maskb = np.zeros((NSEG, NK), np.float32)
    NEG = -30000.0
    if not is_sample:
        maskb[:] = NEG
        for q in range(NSEG):
            maskb[q, 2 + 2 * q] = 0.0
            maskb[q, 2 + 2 * q + 1] = 0.0
    maskb = np.broadcast_to(maskb.reshape(1, -1), (128, NSEG * NK)).astype(np.float32).copy()
    cosT = np.ones((128, T), np.float32)
    sinT = np.zeros((128, T), np.float32)
    if is_sample:
        t = np.arange(T)
        row = (t // 64).astype(np.float32)
        colp = (t % 64).astype(np.float32)
        half = 64
        inv = (10000.0 ** (-np.arange(0, half, 2, dtype=np.float32) / half)).astype(np.float32)
        for p in range(128):
            f = p % 32
            ang = (row if p < 64 else colp) * inv[f]
            cosT[p] = np.cos(ang.astype(np.float32))
            s = np.sin(ang.astype(np.float32))
            sinT[p] = -s if (p // 32) % 2 == 0 else s
    return cst, lgam, maskb, cosT, sinT


_CACHE = {}


def _run(cfg, inputs, n_sample, n_prompt_cores):
    D, T, RH = cfg.D, cfg.T, cfg.RH
    key = (D, T)
    if key not in _CACHE:
        _CACHE[key] = build(cfg)
    nc, _ = _CACHE[key]
    print("built: instructions", _CACHE[key][1].nins, flush=True)
    f = lambda a: np.ascontiguousarray(np.asarray(a), dtype=np.float32)
    w = {k: f(inputs[k]) for k in ("w_mod", "w_in", "w_out", "w_gu", "w_down")}
    cs, cp = _consts(cfg, True), _consts(cfg, False)
    gvec = np.stack([f(inputs["g_pre_mix"]), f(inputs["g_post_mix"]), f(inputs["g_pre_ffn"]), f(inputs["g_post_ffn"])], axis=1)
    gqk = np.stack([f(inputs["g_q"]), f(inputs["g_k"])], axis=1)
    ghead = np.concatenate([f(inputs["g_ret"]), f(inputs["g_hgrn"])], axis=1)
    in_maps = []
    ncores = n_sample + n_prompt_cores
    for c in range(ncores):
        m = {}
        for k in ("w_mod", "w_in", "w_out", "w_gu", "w_down"):
            m[k] = w[k]
        m["b_mod"] = f(inputs["b_mod"])
        m["gvec"], m["gqk"], m["ghead"], m["hg_lb"] = gvec, gqk, ghead, f(inputs["hg_lb"])
        if c < n_sample:
            cst, lgam, maskb, cosT, sinT = cs
            m["x"] = f(inputs["x_sample"][c])
            m["cond"] = f(inputs["c"][c]).reshape(1, D)
            m["ctxk"] = f(inputs["cache_k"][c]).reshape(DEPTH, PAST, cfg.KW)
            m["ctxv"] = f(inputs["cache_v"][c]).reshape(DEPTH, PAST, cfg.KW)
            sr, sh = f(inputs["state_ret"][c]), f(inputs["state_hgrn"][c])
            m["s0"] = np.ascontiguousarray(np.concatenate([sr, sh], axis=2))
            m["keep"] = np.ones((128, 1), np.float32)
        else:
            cst, lgam, maskb, cosT, sinT = cp
            pc = c - n_sample
            if pc < n_prompt_cores:
                m["x"] = f(inputs["x_prompt"][pc * cfg.NSEG:(pc + 1) * cfg.NSEG]).reshape(T, D)
            else:
                m["x"] = np.zeros((T, D), np.float32)
            m["cond"] = f(inputs["c_ctx"]).reshape(1, D)
            m["ctxk"] = np.zeros((DEPTH, PAST, cfg.KW), np.float32)
            m["ctxv"] = np.zeros((DEPTH, PAST, cfg.KW), np.float32)
            m["s0"] = np.zeros((DEPTH, 2, 2 * RH, 128, 128), np.float32)
            m["keep"] = np.zeros((128, 1), np.float32)
        m["cst"], m["lgam"], m["maskb"], m["cosT"], m["sinT"] = cst, lgam, maskb, cosT, sinT
        in_maps.append(m)
    res = run_bass_kernel_spmd(nc, in_maps, core_ids=list(range(ncores)))
    R = res.results
    y_sample = np.stack([R[c]["y"] for c in range(n_sample)], axis=0)
    pcs = range(n_sample, n_sample + n_prompt_cores)
    y_prompt = np.concatenate([R[c]["y"].reshape(cfg.NSEG, SEG, D) for c in pcs], axis=0)
    ck = np.concatenate([R[c]["ck"] for c in pcs], axis=0).reshape(-1, DEPTH, SEG, cfg.KVH, 128)
    cv = np.concatenate([R[c]["cv"] for c in pcs], axis=0).reshape(-1, DEPTH, SEG, cfg.KVH, 128)
    st = np.concatenate([R[c]["st"] for c in pcs], axis=0)
    return (y_prompt, y_sample, ck, cv, np.ascontiguousarray(st[:, :, :, :RH]), np.ascontiguousarray(st[:, :, :, RH:]))


def kernel(**inputs):
    cfg = Cfg(4096, 4096)
    return _run(cfg, inputs, 4, 2)
```

```python
import math
from contextlib import ExitStack
import numpy as np
import concourse.bass as bass
import concourse.mybir as mybir
from concourse.bass_utils import run_bass_kernel_spmd

F32 = mybir.dt.float32
BF16 = mybir.dt.bfloat16
ALU = mybir.AluOpType
AF = mybir.ActivationFunctionType
EPS = 1e-6
NCORES = 8
DEPTH = 2
SEG = 256
PAST = 256


class Cfg:
    def __init__(self, D=4096, T=4096, stop=None):
        self.D, self.T, self.stop = D, T, stop
        self.KC = D // 128
        self.NS = D // 128
        self.RH = self.HH = self.NS // 4
        self.AH = self.NS // 2
        self.KVH = self.AH // 4
        self.RW, self.HW, self.AW, self.KW = self.RH * 128, self.HH * 128, self.AH * 128, self.KVH * 128
        self.NIN = 4 * self.RW + 5 * self.HW + self.AW + 2 * self.KW
        self.FF = 256 * (-(-8 * D // (3 * 256)))
        self.FC = self.FF // 128
        self.NSEG = T // SEG
        self.NB = T // 128
        o = 0
        self.col = {}
        for nm, w in [("rq", self.RW), ("rk", self.RW), ("rv", self.RW), ("rg", self.RW), ("hq", self.HW),
                      ("hzf", self.HW), ("hzb", self.HW), ("hi", self.HW), ("hg", self.HW), ("aq", self.AW),
                      ("ak", self.KW), ("av", self.KW)]:
            self.col[nm] = (o, w)
            o += w
        self.fm = {}
        i = 0
        for nm in ["rq", "rk", "rg", "hq", "hzf", "hzb", "hg", "aq", "ak"]:
            self.fm[nm] = i
            i += self.col[nm][1] // 128
        self.NFM = i
        self.tm = {}
        o = 0
        for nm in ["rv", "hi", "av", "ak"]:
            self.tm[nm] = o
            o += self.col[nm][1]
        self.NTM = o


class _Op:
    __slots__ = ("eng", "fn", "waits", "signal", "val", "epoch", "dsem")

    def __init__(self, eng, fn, epoch):
        self.eng, self.fn, self.waits, self.signal, self.val, self.epoch, self.dsem = eng, fn, [], False, 0, epoch, None


class Sched:
    CE = ("pe", "act", "dve", "pool")
    ALL = ("pe", "act", "dve", "pool", "sp")
    NDS = 96

    def __init__(self, nc, es):
        self.nc = nc
        self.sem = {e: es.enter_context(nc.semaphore("se_" + e)) for e in self.CE}
        self.dsem = [es.enter_context(nc.semaphore("sd%d" % i)) for i in range(self.NDS)]
        self.dcnt = [0] * self.NDS
        self.dpersist = set()
        self.cnt = {e: 0 for e in self.CE}
        self.ops = {e: [] for e in self.ALL}
        self.lastw, self.rd = {}, {}
        self.k2s = {}
        self.epoch = 0
        self.waited = {e: {} for e in self.ALL}
        self.nins = 0

    def _dep(self, o, ev, is_dma):
        if ev is None:
            return
        if ev[0] == "op":
            p = ev[1]
            if p.epoch < self.epoch:
                return
            if p.eng == o.eng and p.eng == "pe" and not is_dma:
                return
            p.signal = True
            o.waits.append(ev)
        else:
            if ev[3] < self.epoch and ev[1] not in self.dpersist:
                return
            o.waits.append(ev)

    def _track(self, o, reads, writes, ev, is_dma):
        for k in reads:
            self._dep(o, self.lastw.get(k), is_dma)
        for k in writes:
            self._dep(o, self.lastw.get(k), is_dma)
            for e2 in self.rd.get(k, {}).values():
                self._dep(o, e2, is_dma)
        rk = ("e", o.eng) if ev[0] == "op" else ("d", ev[1])
        for k in reads:
            self.rd.setdefault(k, {})[rk] = ev
        for k in writes:
            self.lastw[k] = ev
            self.rd[k] = {}

    cap = None

    def op(self, eng, fn, reads=(), writes=()):
        if self.cap is not None:
            self.cap.append((0, eng, fn, tuple(reads), tuple(writes), None))
            return None
        o = _Op(eng, fn, self.epoch)
        self._track(o, reads, writes, ("op", o), False)
        self.ops[eng].append(o)
        return o

    def replay_interleaved(self, lists):
        n = max(len(L) for L in lists)
        for i in range(n):
            for L in lists:
                if i < len(L):
                    k, eng, fn, rd, wr, sk = L[i]
                    if k == 0:
                        self.op(eng, fn, rd, wr)
                    else:
                        self.dma(eng, fn, rd, wr, skey=sk)

    def dma(self, q, fn, reads=(), writes=(), skey=None, persist=False):
        if self.cap is not None:
            self.cap.append((1, q, fn, tuple(reads), tuple(writes), skey))
            return None
        o = _Op(q, fn, self.epoch)
        if skey not in self.k2s:
            used = set(self.k2s.values()) | self.dpersist
            rng = range(0, 60) if q == "sp" else range(60, self.NDS)
            si = next(i for i in rng if i not in used)
            self.k2s[skey] = si
            if persist:
                self.dpersist.add(si)
        si = self.k2s[skey]
        if self.dcnt[si]:
            o.waits.append(("dma", si, self.dcnt[si], self.epoch))
        self.dcnt[si] += 16
        o.dsem = si
        self._track(o, reads, writes, ("dma", si, self.dcnt[si], self.epoch), True)
        self.ops[q].append(o)
        return o

    def fence(self, final=False):
        lasts = {}
        for e in self.CE:
            if self.ops[e]:
                cands = [o for o in self.ops[e] if o.fn is not None and o.dsem is None]
                if cands:
                    cands[-1].signal = True
                    lasts[e] = cands[-1]
        for e in self.ALL:
            o = _Op(e, None, self.epoch)
            for f, p in lasts.items():
                if f != e:
                    o.waits.append(("op", p))
            for si in range(self.NDS):
                if self.dcnt[si] and (final or si not in self.dpersist):
                    o.waits.append(("dma", si, self.dcnt[si], self.epoch))
            self.ops[e].append(o)
        self.emit()
        self.epoch += 1
        self.lastw = {k: v for k, v in self.lastw.items() if v[0] == "dma" and v[1] in self.dpersist}
        self.rd = {}
        self.k2s = {k: v for k, v in self.k2s.items() if v in self.dpersist}

    def emit(self):
        nc = self.nc
        for e in self.CE:
            for o in self.ops[e]:
                if o.signal:
                    self.cnt[e] += 1
                    o.val = self.cnt[e]

        def run(ename, eng):
            wd = self.waited[ename]
            for o in self.ops[ename]:
                for ev in o.waits:
                    if ev[0] == "op":
                        s, v = self.sem[ev[1].eng], ev[1].val
                        key = ev[1].eng
                    else:
                        s, v = self.dsem[ev[1]], ev[2]
                        key = ev[1]
                    if wd.get(key, 0) >= v:
                        continue
                    wd[key] = v
                    eng.wait_ge(s, v)
                if o.fn is None:
                    continue
                ins = o.fn(eng)
                self.nins += 1
                if o.dsem is not None:
                    ins.then_inc(self.dsem[o.dsem], 16)
                elif o.signal:
                    ins.then_inc(self.sem[ename], 1)

        with nc.Block() as block:
            @block.tensor
            def _(t):
                run("pe", t)

            @block.scalar
            def _(a):
                run("act", a)

            @block.vector
            def _(v):
                run("dve", v)

            @block.gpsimd
            def _(g):
                run("pool", g)

            @block.sync
            def _(s):
                run("sp", s)
        self.ops = {e: [] for e in self.ALL}


class Ring:
    def __init__(self, name, tiles):
        self.name, self.tiles, self.i = name, tiles, 0

    def next(self):
        i = self.i % len(self.tiles)
        self.i += 1
        return self.tiles[i], (self.name, i)


def build(cfg):
    D, T, KC, NS, RH, HH, AH, KVH = cfg.D, cfg.T, cfg.KC, cfg.NS, cfg.RH, cfg.HH, cfg.AH, cfg.KVH
    FF, FC, NIN, KW, NB, NSEG = cfg.FF, cfg.FC, cfg.NIN, cfg.KW, cfg.NB, cfg.NSEG
    nc = bass.Bass("TRN2", target_bir_lowering=False)
    es = ExitStack()

    def din(name, shape, dt=F32):
        return nc.dram_tensor(name, list(shape), dt, kind="ExternalInput").ap()

    def dout(name, shape):
        return nc.dram_tensor(name, list(shape), F32, kind="ExternalOutput").ap()

    def dint(name, shape, dt=F32):
        return nc.dram_tensor(name, list(shape), dt, kind="Internal").ap()

    x_in = din("x", [T, D])
    cond_in = din("cond", [1, D])
    ctxk_in = din("ctxk", [DEPTH, PAST, KW])
    ctxv_in = din("ctxv", [DEPTH, PAST, KW])
    s0_in = din("s0", [DEPTH, 2, 2 * RH, 128, 128])
    keep_in = din("keep", [128, 1])
    maskb_in = din("maskb", [128, NSEG * (2 + NB)])
    cos_in = din("cosT", [128, T])
    sin_in = din("sinT", [128, T])
    cst_in = din("cst", [128, 7 * 128])
    lgam_in = din("lgam", [128, 2 * RH])
    wsh = {
        "mod": din("w_mod", [DEPTH, D, 6 * D]), "in": din("w_in", [DEPTH, D, NIN]),
        "out": din("w_out", [DEPTH, D, D]), "gu": din("w_gu", [DEPTH, D, 2 * FF]),
        "down": din("w_down", [DEPTH, FF, D]),
    }
    wshape = {"mod": (D, 6 * D), "in": (D, NIN), "out": (D, D), "gu": (D, 2 * FF), "down": (FF, D)}
    bmod_in = din("b_mod", [DEPTH, 6 * D])
    gvec_in = din("gvec", [DEPTH, 4, D])
    gqk_in = din("gqk", [DEPTH, 2, 128])
    ghead_in = din("ghead", [DEPTH, 2 * RH * 128])
    hglb_in = din("hg_lb", [DEPTH, 2, HH * 128])

    y_out = dout("y", [T, D])
    ck_out = dout("ck", [NSEG, DEPTH, SEG, KW])
    cv_out = dout("cv", [NSEG, DEPTH, SEG, KW])
    st_out = dout("st", [NSEG, DEPTH, 2, 2 * RH, 128, 128])

    wb = {k: [dint("wb_%s%d" % (k, l), list(wshape[k]), BF16) for l in range(DEPTH)] for k in wsh}
    PFT = [dint("pft%d" % l, [cfg.NFM, 128, T]) for l in range(DEPTH)]
    PTM = [dint("ptm%d" % l, [T, cfg.NTM]) for l in range(DEPTH)]
    OFW = [dint("ofw%d" % l, [2 * RH, 128, T]) for l in range(DEPTH)]
    MIXT = [dint("mixt%d" % l, [NS, 128, T], BF16) for l in range(DEPTH)]
    XMID = dint("xmid", [T, D])
    GROW = dint("grow", [DEPTH, 2, D])

    S = Sched(nc, es)
    cast_jobs = []

    def emit_casts(count):
        for _ in range(min(count, len(cast_jobs))):
            k, l_, r, n, c0_, cn = cast_jobs.pop(0)
            S.dma("pool", dmaf(wb[k][l_][r:r + n, c0_:c0_ + cn], wsh[k][l_, r:r + n, c0_:c0_ + cn]),
                  writes=[("W", k, l_)], skey="cast", persist=True)

    names = []

    def sb(name, shape, dt=F32):
        return es2.enter_context(nc.sbuf_tensor("sb_" + name + "_%d" % len(names), list(shape), dt)) if not names.append(name) else None

    es2 = es
    cst = sb("cst", [128, 7 * 128])
    ident_f = cst[:, 0:128]
    ones_f = cst[:, 512:640]
    cstb = sb("cstb", [128, 7 * 128], BF16)
    ident_b, perm_b, maskf_b, maskb_b, ones_b = (cstb[:, i * 128:(i + 1) * 128] for i in range(5))
    maskf_f, maskbk_f = cst[:, 256:384], cst[:, 384:512]
    maskxf_f, maskxb_f = cst[:, 640:768], cst[:, 768:896]
    lgam = sb("lgam", [128, 2 * RH])
    keep = sb("keep", [128, 1])
    epsc = sb("epsc", [128, 1])
    maskb = sb("maskb", [128, NSEG * (2 + NB)])
    modc = sb("modc", [128, DEPTH, 4, KC])
    gqkc = sb("gqkc", [128, DEPTH, 2])
    gheadc = sb("gheadc", [128, DEPTH, 2 * RH])
    lbc = sb("lbc", [128, DEPTH, 2, HH])
    omlbc = sb("omlbc", [128, DEPTH, 2, HH])
    nomlbc = sb("nomlbc", [128, DEPTH, 2, HH])
    ps = [es.enter_context(nc.psum_tensor("ps%d" % i, [128, 512], F32)) for i in range(8)]
    PK = lambda i: ("ps", i)

    def mm(out, lhsT, rhs, start, stop):
        return lambda e: e.matmul(out, lhsT, rhs, start=start, stop=stop)

    def tcopy(out, in_):
        return lambda e: e.tensor_copy(out=out, in_=in_)

    def dmaf(out, in_, **kw):
        return lambda e: e.dma_start(out=out, in_=in_, **kw)

    def act(out, in_, func, bias=None, scale=None, accum_out=None):
        kw = {}
        if bias is not None:
            kw["bias"] = bias
        if scale is not None:
            kw["scale"] = scale
        if accum_out is not None:
            kw["accum_out"] = accum_out
        return lambda e: e.activation(out=out, in_=in_, func=func, **kw)

    def tscal(out, in0, s1, s2, op0, op1=None):
        if op1 is None:
            return lambda e: e.tensor_scalar(out=out, in0=in0, scalar1=s1, scalar2=None, op0=op0)
        return lambda e: e.tensor_scalar(out=out, in0=in0, scalar1=s1, scalar2=s2, op0=op0, op1=op1)

    def tt(out, in0, in1, op):
        return lambda e: e.tensor_tensor(out=out, in0=in0, in1=in1, op=op)

    def stt(out, in0, scalar, in1, op0, op1):
        return lambda e: e.scalar_tensor_tensor(out=out, in0=in0, scalar=scalar, in1=in1, op0=op0, op1=op1)

    def memset(ap, v):
        return lambda e: e.memset(ap, v)

    def rstd_ops(dst, src, n, rk, wk):
        S.op("act", act(dst, src, AF.Ln, bias=epsc[:, 0:1], scale=1.0 / n), reads=list(rk) + ["epsc"], writes=wk)
        S.op("act", act(dst, dst, AF.Exp, scale=-0.5), reads=wk, writes=wk)

    with ExitStack() as es2:
        order = [(k, l) for l in range(DEPTH) for k in ("mod", "in", "out", "gu", "down")]
        for k, l in order:
            rows, cols = wshape[k]
            for r in range(0, rows, 128):
                n = min(128, rows - r)
                for c0_ in range(0, cols, 8192):
                    cn = min(8192, cols - c0_)
                    cast_jobs.append((k, l, r, n, c0_, cn))
        first = [j for j in cast_jobs if j[0] == "mod" or (j[0] == "in" and j[1] == 0)]
        rest = [j for j in cast_jobs if not (j[0] == "mod" or (j[0] == "in" and j[1] == 0))]
        cast_jobs[:] = first + rest
        emit_casts(len(first))

        S.dma("sp", dmaf(cst[:], cst_in[:, :]), writes=["cst"], skey="c0")
        S.dma("sp", dmaf(lgam[:], lgam_in[:, :]), writes=["lgam"], skey="c1")
        S.dma("sp", dmaf(keep[:], keep_in[:, :]), writes=["keep"], skey="c2")
        S.dma("sp", dmaf(maskb[:], maskb_in[:, :]), writes=["maskb"], skey="c3")
        S.op("dve", tcopy(cstb[:], cst[:]), reads=["cst"], writes=["cstb"])
        S.op("dve", memset(epsc[:], EPS), writes=["epsc"])

        rows = sb("rows", [1, D])
        one11 = cst[0:1, 512:513]

        def row_to_cols(src_ap, n, dst, tag):
            S.dma("sp", dmaf(rows[0:1, 0:n * 128], src_ap), writes=["rows"], skey="rows")
            for j in range(n):
                S.op("pe", mm(ps[0][:, j:j + 1], rows[0:1, j * 128:(j + 1) * 128], one11, True, True),
                     reads=["rows", "cst"], writes=[PK(0)])
            S.op("dve", tcopy(dst, ps[0][:, 0:n]), reads=[PK(0)], writes=[tag])

        condc = sb("condc", [128, KC])
        row_to_cols(cond_in[0:1, :], KC, condc[:], "condc")
        scb = sb("scb", [128, KC], BF16)
        S.op("act", act(scb[:], condc[:], AF.Silu), reads=["condc"], writes=["scb"])
        screp = sb("screp", [128, KC, 128], BF16)
        for c in range(KC):
            S.op("dve", tcopy(screp[:, c, :], scb[:, c:c + 1].to_broadcast([128, 128])), reads=["scb"], writes=["screp"])
        gcol = sb("gcol", [128, 4, KC])
        tmpc = sb("tmpc", [128, 2 * max(KC, 2 * RH)])
        bmodb = sb("bmodb", [1, D], BF16)
        onesrow_b = cstb[0:1, 512:640]
        one11b = cstb[0:1, 512:513]
        wmod_ring = Ring("wmod", [sb("wmod%d" % i, [128, KC, 512], BF16) for i in range(2)])
        grep = sb("grep", [128, D])
        growst = sb("growst", [128, 512])
        for l in range(DEPTH):
            for i in range(4):
                row_to_cols(gvec_in[l, i:i + 1, :], KC, gcol[:, i, :], ("gcol", i))
            row_to_cols(gqk_in[l, 0:1, :], 1, gqkc[:, l, 0:1], "gqkc")
            row_to_cols(gqk_in[l, 1:2, :], 1, gqkc[:, l, 1:2], "gqkc")
            row_to_cols(ghead_in[l:l + 1, :], 2 * RH, gheadc[:, l, :], "gheadc")
            for part in range(6):
                S.dma("sp", dmaf(rows[0:1, :], bmod_in[l:l + 1, part * D:(part + 1) * D]), writes=["rows"], skey="rows")
                S.op("dve", tcopy(bmodb[:], rows[:]), reads=["rows"], writes=["bmodb"])
                for cg in range(D // 512 if D >= 512 else 1):
                    c0 = part * D + cg * 512
                    b0_ = cg * 512
                    wt, wk = wmod_ring.next()
                    S.dma("sp", dmaf(wt[:], wb["mod"][l][:, c0:c0 + 512].rearrange("(kc p) n -> p kc n", p=128)),
                          reads=[("W", "mod", l)], writes=[wk], skey=wk)
                    if part in (2, 5):
                        for kc in range(KC):
                            S.op("pe", mm(ps[1][:, :], screp[:, kc, :], wt[:, kc, :], kc == 0, False),
                                 reads=[wk, "screp"], writes=[PK(1)])
                        S.op("pe", mm(ps[1][:, :], onesrow_b, bmodb[0:1, b0_:b0_ + 512], False, True),
                             reads=["bmodb", "cstb"], writes=[PK(1)])
                        gi = 1 if part == 2 else 3
                        if cg == 0:
                            S.dma("sp", dmaf(grep[:], gvec_in[l, gi:gi + 1, :].partition_broadcast(128)),
                                  writes=["grep"], skey="grep")
                        S.op("dve", tt(growst[:], ps[1][:, :], grep[:, cg * 512:(cg + 1) * 512], ALU.mult),
                             reads=[PK(1), "grep"], writes=["growst"])
                        S.dma("pool", dmaf(GROW[l, (0 if part == 2 else 1):(1 if part == 2 else 2), cg * 512:(cg + 1) * 512],
                                           growst[0:1, :]), reads=["growst"], writes=[("grow", l)], skey="growst")
                    else:
                        for j in range(4):
                            for kc in range(KC):
                                S.op("pe", mm(ps[2][:, j:j + 1], wt[:, kc, j * 128:(j + 1) * 128], scb[:, kc:kc + 1], kc == 0, False),
                                     reads=[wk, "scb"], writes=[PK(2)])
                            S.op("pe", mm(ps[2][:, j:j + 1], bmodb[0:1, b0_ + j * 128:b0_ + (j + 1) * 128], one11b, False, True),
                                 reads=["bmodb", "cstb"], writes=[PK(2)])
                        cs = slice(cg * 4, cg * 4 + 4)
                        if part in (0, 3):
                            S.op("dve", tcopy(modc[:, l, 1 if part == 0 else 3, cs], ps[2][:, 0:4]), reads=[PK(2)], writes=["modc"])
                        else:
                            gi = 0 if part == 1 else 2
                            S.op("dve", stt(modc[:, l, 0 if part == 1 else 2, cs], ps[2][:, 0:4], 1.0, gcol[:, gi, cs], ALU.add, ALU.mult),
                                 reads=[PK(2), ("gcol", gi)], writes=["modc"])
        for d in range(2):
            row_to_cols(hglb_in[0, d:d + 1, :], HH, tmpc[:, 0:HH], "tmpc0")
            row_to_cols(hglb_in[1, d:d + 1, :], HH, tmpc[:, HH:2 * HH], "tmpc1")
            S.op("dve", memset(lbc[:, 0, d, :], 0.0), writes=["lbc"])
            S.op("dve", tt(tmpc[:, 0:HH], tmpc[:, HH:2 * HH], tmpc[:, 0:HH], ALU.subtract), reads=["tmpc0", "tmpc1"], writes=["tmpc0"])
            S.op("act", act(lbc[:, 1, d, :], tmpc[:, 0:HH], AF.Sigmoid), reads=["tmpc0", "lbc"], writes=["lbc"])
        S.op("dve", tscal(omlbc[:], lbc[:], -1.0, 1.0, ALU.mult, ALU.add), reads=["lbc"], writes=["omlbc"])
        S.op("dve", tscal(nomlbc[:], omlbc[:], -1.0, None, ALU.mult), reads=["omlbc"], writes=["nomlbc"])
        S.fence(final=(cfg.stop == 0))
    if cfg.stop == 0:
        es.close()
        return nc, S

    def prenorm_tile(x_src, r0, nblk, xring, hT, A, B, ssq, rstd):
        for b in range(nblk):
            xt, xk = xring.next()
            S.dma("sp", dmaf(xt[:], x_src[r0 + b * 128:r0 + (b + 1) * 128, :]), writes=[xk], skey=xk)
            norm_transpose(xt, xk, b, hT, A, B, ssq, rstd)

    jk = {"i": 0}

    def norm_transpose(xt, xk, b, hT, A, B, ssq, rstd, dst=None, dk=None):
        dst = xt if dst is None else dst
        dk = xk if dk is None else dk
        S.op("dve", memset(ssq[:], 0.0), writes=["ssq"])
        S.op("act", act(junk_t[:, 0:D], xt[:], AF.Square, accum_out=ssq[:, 0:1]), reads=[xk, "ssq"], writes=["junk", "ssq"])
        rstd_ops(rstd[:, 0:1], ssq[:, 0:1], D, ["ssq"], ["rstd"])
        S.op("dve", tscal(dst[:], xt[:], rstd[:, 0:1], None, ALU.mult), reads=[xk, "rstd"], writes=[dk])
        for c in range(KC):
            s = jk["i"] % 32
            jk["i"] += 1
            pt = ps[s // 4][:, (s % 4) * 128:(s % 4 + 1) * 128]
            S.op("pe", lambda e, pt=pt, c=c: e.transpose(pt, dst[:, c * 128:(c + 1) * 128], ident_f),
                 reads=[dk, "cst"], writes=[PK(s // 4)])
            S.op("act", act(hT[:, c, b * 128:(b + 1) * 128], pt, AF.Identity, bias=B[:, c:c + 1], scale=A[:, c:c + 1]),
                 reads=[PK(s // 4), "modc"], writes=[("hT", b)])

    def load_w(wring, name, l, k0, kn, c0, ncol):
        wt, wk = wring.next()
        src = wb[name][l][k0 * 128:(k0 + kn) * 128, c0:c0 + ncol].rearrange("(kc p) n -> p kc n", p=128)
        S.dma("sp", dmaf(wt[:, 0:kn, 0:ncol], src), reads=[("W", name, l)], writes=[wk], skey=wk)
        return wt, wk

    KT = 16

    for l in range(DEPTH):
        x_src = x_in if l == 0 else XMID
        x_dst = XMID if l == 0 else y_out
        with ExitStack() as es2:
            TT = 512 if T >= 512 else T
            nblk = TT // 128
            hT = sb("hT", [128, KC, TT], BF16)
            xring = Ring("xin", [sb("xin%d" % i, [128, D]) for i in range(2)])
            junk_t = sb("junk", [128, D], BF16)
            wring = Ring("w", [sb("w%d" % i, [128, KT, 512], BF16) for i in range(3)])
            stg = Ring("stg", [sb("stg%d" % i, [128, 512]) for i in range(6)])
            ssq = sb("ssq", [128, 1])
            rstd = sb("rstd", [128, 1])
            pieces = []
            for nm in ["rq", "rk", "rv", "rg", "hq", "hzf", "hzb", "hi", "hg", "aq", "ak", "av", "ak_tm"]:
                base = nm[:2] if nm.endswith("_tm") else nm
                c0, w = cfg.col[base]
                kind = "tm" if nm in ("rv", "hi", "av", "ak_tm") else "fm"
                o = 0
                while o < w:
                    n = min(512, w - o)
                    pieces.append((kind, base, c0 + o, n, o))
                    o += n
            pset = 0
            for ti in range(T // TT):
                r0 = ti * TT
                prenorm_tile(x_src, r0, nblk, xring, hT, modc[:, l, 0, :], modc[:, l, 1, :], ssq, rstd)
                for kind, base, c0, n, o in pieces:
                    banks = [pset * 4 + i for i in range(4)]
                    pset ^= 1
                    nch = n // 128
                    for k0 in range(0, KC, KT):
                        kn = min(KT, KC - k0)
                        wt, wk = load_w(wring, "in", l, k0, kn, c0, n)
                        for kk in range(kn):
                            kc = k0 + kk
                            st, sp_ = kc == 0, kc == KC - 1
                            if kind == "fm":
                                for j in range(nch):
                                    S.op("pe", mm(ps[banks[j]][:, 0:TT], wt[:, kk, j * 128:(j + 1) * 128], hT[:, kc, :], st, sp_),
                                         reads=[wk] + [("hT", b) for b in range(nblk)], writes=[PK(banks[j])])
                            else:
                                for b in range(nblk):
                                    S.op("pe", mm(ps[banks[b]][:, 0:n], hT[:, kc, b * 128:(b + 1) * 128], wt[:, kk, 0:n], st, sp_),
                                         reads=[wk, ("hT", b)], writes=[PK(banks[b])])
                    nout = nch if kind == "fm" else nblk
                    for j in range(nout):
                        sg, sk = stg.next()
                        eng = "act" if j % 2 == 0 else "dve"
                        if kind == "fm":
                            src = ps[banks[j]][:, 0:TT]
                            fn = act(sg[:, 0:TT], src, AF.Copy) if eng == "act" else tcopy(sg[:, 0:TT], src)
                            S.op(eng, fn, reads=[PK(banks[j])], writes=[sk])
                            fi = cfg.fm[base] + (o // 128) + j
                            S.dma("pool", dmaf(PFT[l][fi, :, r0:r0 + TT], sg[:, 0:TT]), reads=[sk],
                                  writes=[("pft", fi, ti)], skey=sk)
                        else:
                            src = ps[banks[j]][:, 0:n]
                            fn = act(sg[:, 0:n], src, AF.Copy) if eng == "act" else tcopy(sg[:, 0:n], src)
                            S.op(eng, fn, reads=[PK(banks[j])], writes=[sk])
                            tc0 = cfg.tm[base] + o
                            S.dma("pool", dmaf(PTM[l][r0 + j * 128:r0 + (j + 1) * 128, tc0:tc0 + n], sg[:, 0:n]), reads=[sk],
                                  writes=[("ptm", base, ti)], skey=sk)
            S.fence(final=(cfg.stop == 1))
        if cfg.stop == 1:
            es.close()
            return nc, S

        with ExitStack() as es2:
            NK = 2 + NB
            TQ = 512 if T >= 512 else T
            KTall = sb("KTall", [128, KVH, PAST + T], BF16)
            Vall = sb("Vall", [128, NK, KW], BF16)
            ldr = Ring("ld", [sb("ld%d" % i, [128, 512]) for i in range(3)])
            tb_r = Ring("tb", [sb("tb%d" % i, [128, 512], BF16) for i in range(3)])
            f1 = Ring("f1", [sb("f1_%d" % i, [128, 512]) for i in range(3)])
            f2 = Ring("f2", [sb("f2_%d" % i, [128, 512]) for i in range(3)])
            cosr = Ring("cos", [sb("cos%d" % i, [128, 512]) for i in range(2)])
            sinr = Ring("sin", [sb("sin%d" % i, [128, 512]) for i in range(2)])
            qr_r = Ring("qr", [sb("qr%d" % i, [128, 512], BF16) for i in range(2)])
            pT_r = Ring("pT", [sb("pT%d" % i, [128, 512], BF16) for i in range(4)])
            ost = Ring("ost", [sb("ost%d" % i, [128, 512], BF16) for i in range(2)])
            kvld = Ring("kvld", [sb("kvld%d" % i, [128, KW]) for i in range(2)])
            kvo = Ring("kvo", [sb("kvo%d" % i, [128, KW]) for i in range(2)])
            gkrep = sb("gkrep", [128, 128])
            ssqh = sb("ssqh", [128, KVH])
            junk2 = sb("junk2", [128, 128], BF16)
            psi = {"i": 0}

            def psn(lo, hi):
                i = lo + psi["i"] % (hi - lo)
                psi["i"] += 1
                return i

            def norm_rope(src_dram, t0, n, gcol_ap, dst, dk, cs_t, cs_k, sn_t, sn_k):
                ld, lk = ldr.next()
                S.dma("sp", dmaf(ld[:, 0:n], src_dram), reads=[], writes=[lk], skey=lk)
                a1, k1 = f1.next()
                S.op("act", act(a1[:, 0:n], ld[:, 0:n], AF.Square), reads=[lk], writes=[k1])
                b0 = psn(0, 2)
                S.op("pe", mm(ps[b0][:, 0:n], ones_f, a1[:, 0:n], True, True), reads=[k1, "cst"], writes=[PK(b0)])
                a2, k2 = f2.next()
                rstd_ops(a2[:, 0:n], ps[b0][:, 0:n], 128, [PK(b0)], [k2])
                S.op("dve", stt(a1[:, 0:n], ld[:, 0:n], gcol_ap, a2[:, 0:n], ALU.mult, ALU.mult), reads=[lk, k2, "gqkc"], writes=[k1])
                tb, tk = tb_r.next()
                S.op("act", act(tb[:, 0:n], a1[:, 0:n], AF.Copy), reads=[k1], writes=[tk])
                S.op("pe", mm(ps[b0][:, 0:n], perm_b, tb[:, 0:n], True, True), reads=[tk, "cstb"], writes=[PK(b0)])
                S.op("dve", tt(a2[:, 0:n], ps[b0][:, 0:n], sn_t[:, 0:n], ALU.mult), reads=[PK(b0), sn_k], writes=[k2])
                S.op("pool", tt(a1[:, 0:n], a1[:, 0:n], cs_t[:, 0:n], ALU.mult), reads=[k1, cs_k], writes=[k1])
                S.op("dve", tt(dst, a1[:, 0:n], a2[:, 0:n], ALU.add), reads=[k1, k2], writes=[dk])

            for blk in range(2):
                kt_, kk_ = kvld.next()
                S.dma("sp", dmaf(kt_[:], ctxk_in[l, blk * 128:(blk + 1) * 128, :]), writes=[kk_], skey=kk_)
                for h in range(KVH):
                    b0 = psn(0, 2)
                    S.op("pe", lambda e, b0=b0, kt_=kt_, h=h: e.transpose(ps[b0][:, 0:128], kt_[:, h * 128:(h + 1) * 128], ident_f),
                         reads=[kk_, "cst"], writes=[PK(b0)])
                    S.op("dve", tcopy(KTall[:, h, blk * 128:(blk + 1) * 128], ps[b0][:, 0:128]), reads=[PK(b0)], writes=["KT"])
                vt_, vk_ = kvld.next()
                S.dma("sp", dmaf(vt_[:], ctxv_in[l, blk * 128:(blk + 1) * 128, :]), writes=[vk_], skey=vk_)
                S.op("act", act(Vall[:, blk, :], vt_[:], AF.Copy), reads=[vk_], writes=["V"])
            S.dma("sp", dmaf(gkrep[:], gqk_in[l, 1:2, :].partition_broadcast(128)), writes=["gkrep"], skey="gkrep")
            tmo = cfg.tm
            for b in range(NB):
                vt_, vk_ = kvld.next()
                S.dma("sp", dmaf(vt_[:], PTM[l][b * 128:(b + 1) * 128, tmo["av"]:tmo["av"] + KW]), writes=[vk_], skey=vk_)
                S.op("act", act(Vall[:, 2 + b, :], vt_[:], AF.Copy), reads=[vk_], writes=["V"])
                seg, ro = (b * 128) // SEG, (b * 128) % SEG
                S.dma("pool", dmaf(cv_out[seg, l, ro:ro + 128, :], vt_[:]), reads=[vk_], writes=[("cvo", b)], skey=("cvs", vk_))
                kt_, kk_ = kvld.next()
                S.dma("sp", dmaf(kt_[:], PTM[l][b * 128:(b + 1) * 128, tmo["ak"]:tmo["ak"] + KW]), writes=[kk_], skey=kk_)
                S.op("dve", memset(ssqh[:], 0.0), writes=["ssqh"])
                for h in range(KVH):
                    S.op("act", act(junk2[:], kt_[:, h * 128:(h + 1) * 128], AF.Square, accum_out=ssqh[:, h:h + 1]),
                         reads=[kk_, "ssqh"], writes=["ssqh", "junk2"])
                rstd_ops(ssqh[:], ssqh[:], 128, ["ssqh"], ["ssqh"])
                ko, kok = kvo.next()
                for h in range(KVH):
                    S.op("dve", stt(ko[:, h * 128:(h + 1) * 128], kt_[:, h * 128:(h + 1) * 128], ssqh[:, h:h + 1], gkrep[:], ALU.mult, ALU.mult),
                         reads=[kk_, "ssqh", "gkrep"], writes=[kok])
                S.dma("pool", dmaf(ck_out[seg, l, ro:ro + 128, :], ko[:]), reads=[kok], writes=[("cko", b)], skey=kok)
            for ti in range(T // TQ):
                t0 = ti * TQ
                ct, ck_ = cosr.next()
                S.dma("sp", dmaf(ct[:, 0:TQ], cos_in[:, t0:t0 + TQ]), writes=[ck_], skey=ck_)
                st_, sk_ = sinr.next()
                S.dma("sp", dmaf(st_[:, 0:TQ], sin_in[:, t0:t0 + TQ]), writes=[sk_], skey=sk_)
                for h in range(KVH):
                    norm_rope(PFT[l][cfg.fm["ak"] + h, :, t0:t0 + TQ], t0, TQ, gqkc[:, l, 1:2],
                              KTall[:, h, PAST + t0:PAST + t0 + TQ], "KT", ct, ck_, st_, sk_)
            scale = 1.0 / math.sqrt(128.0)
            for ti in range(T // TQ):
                t0 = ti * TQ
                ct, ck_ = cosr.next()
                S.dma("sp", dmaf(ct[:, 0:TQ], cos_in[:, t0:t0 + TQ]), writes=[ck_], skey=ck_)
                st_, sk_ = sinr.next()
                S.dma("sp", dmaf(st_[:, 0:TQ], sin_in[:, t0:t0 + TQ]), writes=[sk_], skey=sk_)
                for h in range(AH):
                    emit_casts(3)
                    kvh = h // 4
                    qr, qk = qr_r.next()
                    norm_rope(PFT[l][cfg.fm["aq"] + h, :, t0:t0 + TQ], t0, TQ, gqkc[:, l, 0:1], qr[:, 0:TQ], qk, ct, ck_, st_, sk_)
                    bo, bd = psn(2, 4), psn(4, 6)
                    for kb in range(NK):
                        bs_ = psn(6, 8)
                        S.op("pe", mm(ps[bs_][:, 0:TQ], KTall[:, kvh, kb * 128:(kb + 1) * 128], qr[:, 0:TQ], True, True),
                             reads=["KT", qk], writes=[PK(bs_)])
                        pT, pk = pT_r.next()
                        for qh in range(TQ // SEG):
                            gq = (t0 // SEG) + qh
                            S.op("act", act(pT[:, qh * SEG:(qh + 1) * SEG], ps[bs_][:, qh * SEG:(qh + 1) * SEG], AF.Exp,
                                            bias=maskb[:, gq * NK + kb:gq * NK + kb + 1], scale=scale),
                                 reads=[PK(bs_), "maskb"], writes=[pk])
                        S.op("pe", mm(ps[bo][:, 0:TQ], Vall[:, kb, kvh * 128:(kvh + 1) * 128], pT[:, 0:TQ], kb == 0, kb == NK - 1),
                             reads=["V", pk], writes=[PK(bo)])
                        S.op("pe", mm(ps[bd][:, 0:TQ], ones_b, pT[:, 0:TQ], kb == 0, kb == NK - 1),
                             reads=["cstb", pk], writes=[PK(bd)])
                    a2, k2 = f2.next()
                    S.op("dve", lambda e, a2=a2, bd=bd: e.reciprocal(out=a2[:, 0:TQ], in_=ps[bd][:, 0:TQ]), reads=[PK(bd)], writes=[k2])
                    og, ok_ = ost.next()
                    S.op("dve", tt(og[:, 0:TQ], ps[bo][:, 0:TQ], a2[:, 0:TQ], ALU.mult), reads=[PK(bo), k2], writes=[ok_])
                    S.dma("pool", dmaf(MIXT[l][2 * RH + h, :, t0:t0 + TQ], og[:, 0:TQ]), reads=[ok_], writes=[("mixt", 2 * RH + h, ti)], skey=ok_)
            S.fence(final=(cfg.stop == 2))
        if cfg.stop == 2:
            es.close()
            return nc, S

        with ExitStack() as es2:
            NMH = 2 * RH
            G = RH
            Sst = sb("Sst", [128, 2, NMH, 128])
            Sbf = sb("Sbf", [128, 2, NMH, 128], BF16)

            def mk(name, n, shape, dt=F32):
                return Ring(name, [sb("%s%d" % (name, i), shape, dt) for i in range(n)])
            ldq, ldk, ldz, ldg, ldv, ldo = (mk(nm, G, [128, 128]) for nm in ("ldq", "ldk", "ldz", "ldg", "ldv", "ldo"))
            csr = mk("csr", G, [128, 256])
            qf_r, kf_r, gf_r = (mk(nm, G, [128, 128]) for nm in ("qf", "kf", "gf"))
            Bc_r = mk("Bc", G, [128, 132])
            ar_r = mk("ar", 8 * G, [128, 128])
            ex_r = mk("ex", 6 * G, [128, 128])
            dec_r = mk("dec", G, [128, 4])
            qd_r, kd_r, qs_r, qx_r, kx_r, kut_r, vb_r, am_r, am2_r, mx_r = (
                mk(nm, G, [128, 128], BF16) for nm in ("qd", "kd", "qs", "qx", "kx", "kut", "vb", "am", "am2", "mx"))
            tb2 = mk("tb2", 2 * G, [128, 128], BF16)
            ku_r = mk("ku", G, [128, 128])
            of_r, o2_r, o3_r = (mk(nm, G, [128, 128]) for nm in ("of", "o2", "o3"))
            zer = sb("zer", [128, 128])
            one_t = sb("one_t", [128, 128])
            S.op("dve", memset(zer[:], 0.0), writes=["zer"])
            S.op("dve", memset(one_t[:], 1.0), writes=["one_t"])
            psj = {"i": 0}

            pqc = {}

            def pq(lo, hi):
                n = (hi - lo) * 4
                i = pqc["u"] % n
                bank, q = lo + i // 4, i % 4
                return ps[bank][:, q * 128:(q + 1) * 128], PK(bank)

            for d in range(2):
                for mh in range(NMH):
                    S.dma("sp", dmaf(Sst[:, d, mh, :], s0_in[l, d, mh, :, :]), writes=[("S", d, mh)], skey=("sld", d, mh % 4))
                    S.op("act", act(Sbf[:, d, mh, :], Sst[:, d, mh, :], AF.Copy), reads=[("S", d, mh)], writes=[("Sb", d, mh)])

            def gla_block(d, mh, blk):
                isret = mh < RH
                h = mh if isret else mh - RH
                t0 = blk * 128
                fmq = cfg.fm["rq" if isret else "hq"] + h
                ql, qlk = ldq.next()
                S.dma("sp", dmaf(ql[:], PFT[l][fmq, :, t0:t0 + 128]), writes=[qlk], skey=(qlk[0], qlk[1] % 6))
                vl, vlk = ldv.next()
                vc0 = cfg.tm["rv" if isret else "hi"] + h * 128
                S.dma("sp", dmaf(vl[:], PTM[l][t0:t0 + 128, vc0:vc0 + 128]), writes=[vlk], skey=(vlk[0], vlk[1] % 6))
                vb, vbk = vb_r.next()
                S.op("act", act(vb[:], vl[:], AF.Copy), reads=[vlk], writes=[vbk])
                qf, qfk = qf_r.next()
                kf, kfk = kf_r.next()
                gf, gfk = gf_r.next()
                if isret:
                    kl, klk = ldk.next()
                    S.dma("sp", dmaf(kl[:], PFT[l][cfg.fm["rk"] + h, :, t0:t0 + 128]), writes=[klk], skey=(klk[0], klk[1] % 6))
                    cs, csk = csr.next()
                    S.dma("sp", dmaf(cs[:, 0:128], cos_in[:, t0:t0 + 128]), writes=[csk], skey=(csk[0], csk[1] % 6))
                    S.dma("sp", dmaf(cs[:, 128:256], sin_in[:, t0:t0 + 128]), writes=[csk], skey=(csk[0], csk[1] % 6))
                    for (src, sk_, dst, dk_, sc) in ((ql, qlk, qf, qfk, 1.0), (kl, klk, kf, kfk, 128.0 ** -0.5)):
                        tb, tbk = tb2.next()
                        S.op("act", act(tb[:], src[:], AF.Copy), reads=[sk_], writes=[tbk])
                        b0, b0k = pq(0, 2)
                        S.op("pe", mm(b0, perm_b, tb[:], True, True), reads=[tbk, "cstb"], writes=[b0k])
                        ar, ark = ar_r.next()
                        S.op("dve", tt(ar[:], b0, cs[:, 128:256], ALU.mult), reads=[b0k, csk], writes=[ark])
                        S.op("pool", tt(dst[:], src[:], cs[:, 0:128], ALU.mult), reads=[sk_, csk], writes=[dk_])
                        S.op("dve", tt(dst[:], dst[:], ar[:], ALU.add), reads=[dk_, ark], writes=[dk_])
                        if sc != 1.0:
                            S.op("dve", tscal(dst[:], dst[:], sc, None, ALU.mult), reads=[dk_], writes=[dk_])
                    S.op("dve", tscal(gf[:], one_t[:], lgam[:, d * RH + h:d * RH + h + 1], None, ALU.mult),
                         reads=["one_t", "lgam"], writes=[gfk])
                else:
                    S.op("act", act(qf[:], ql[:], AF.Silu), reads=[qlk], writes=[qfk])
                    zl, zlk = ldz.next()
                    S.dma("sp", dmaf(zl[:], PFT[l][cfg.fm["hzf" if d == 0 else "hzb"] + h, :, t0:t0 + 128]), writes=[zlk], skey=(zlk[0], zlk[1] % 6))
                    ar, ark = ar_r.next()
                    S.op("act", act(ar[:], zl[:], AF.Sigmoid), reads=[zlk], writes=[ark])
                    S.op("act", act(gf[:], ar[:], AF.Ln, bias=lbc[:, l, d, h:h + 1], scale=omlbc[:, l, d, h:h + 1]),
                         reads=[ark, "lbc", "omlbc"], writes=[gfk])
                    S.op("act", act(kf[:], zl[:], AF.Sigmoid, scale=-1.0), reads=[zlk], writes=[kfk])
                    S.op("dve", tscal(kf[:], kf[:], omlbc[:, l, d, h:h + 1], None, ALU.mult), reads=[kfk, "omlbc"], writes=[kfk])
                Bc, Bck = Bc_r.next()
                S.op("dve", memset(Bc[:, 0:1], 0.0), writes=[Bck])
                S.op("dve", lambda e, Bc=Bc, gf=gf: e.tensor_tensor_scan(out=Bc[:, 1:129], data0=one_t[:], data1=gf[:], initial=0.0,
                                                                        op0=ALU.mult, op1=ALU.add),
                     reads=[gfk, "one_t", Bck], writes=[Bck])
                Binc, Eexc = Bc[:, 1:129], Bc[:, 0:128]

                def bc4(col_ap):
                    return col_ap.unsqueeze(2).to_broadcast([128, 4, 32])

                def bc8(col_ap):
                    return col_ap.unsqueeze(2).to_broadcast([128, 8, 16])

                def v3(ap):
                    return ap.rearrange("p (a b) -> p a b", b=32)

                def v16(ap):
                    return ap.rearrange("p (a b) -> p a b", b=16)
                if d == 0:
                    X = Binc
                    m16 = v16(Binc)[:, :, 8]
                    rho = v3(Eexc)[:, :, 0]
                    rho_n = v3(Binc)[:, :, 31]
                    rhox = v3(Binc)[:, :, 15]
                    specs = [("qd", qf, qfk, X, m16, 1.0, True, False), ("kd", kf, kfk, X, m16, -1.0, True, False),
                             ("qx", qf, qfk, X, rhox, 1.0, False, True), ("kx", kf, kfk, X, rhox, -1.0, False, True),
                             ("qs", qf, qfk, X, rho, 1.0, False, True), ("ku", kf, kfk, X, rho_n, -1.0, False, True)]
                else:
                    X = Eexc
                    m16 = v16(Eexc)[:, :, 8]
                    rho_e = v3(Binc)[:, :, 31]
                    rho_s = v3(Eexc)[:, :, 0]
                    rhox = v3(Eexc)[:, :, 16]
                    specs = [("qd", qf, qfk, X, m16, -1.0, True, False), ("kd", kf, kfk, X, m16, 1.0, True, False),
                             ("qx", qf, qfk, X, rhox, -1.0, False, True), ("kx", kf, kfk, X, rhox, 1.0, False, True),
                             ("qs", qf, qfk, X, rho_e, -1.0, False, True), ("ku", kf, kfk, X, rho_s, 1.0, False, True)]
                outs = {}
                for nm, base, bk, Xa, ref, sgn, six, clamp in specs:
                    ar, ark = ar_r.next()
                    if six:
                        S.op("dve", lambda e, ar=ar, Xa=Xa, ref=ref: e.tensor_tensor(out=v16(ar[:]), in0=v16(Xa), in1=bc8(ref), op=ALU.subtract),
                             reads=[Bck], writes=[ark])
                    else:
                        S.op("dve", lambda e, ar=ar, Xa=Xa, ref=ref: e.tensor_tensor(out=v3(ar[:]), in0=v3(Xa), in1=bc4(ref), op=ALU.subtract),
                             reads=[Bck], writes=[ark])
                    ex, exk = ex_r.next()
                    if clamp:
                        S.op("pool", tscal(ar[:], ar[:], sgn, 0.0, ALU.mult, ALU.min), reads=[ark], writes=[ark])
                        S.op("act", act(ex[:], ar[:], AF.Exp), reads=[ark], writes=[exk])
                    else:
                        S.op("act", act(ex[:], ar[:], AF.Exp, scale=sgn), reads=[ark], writes=[exk])
                    ring = {"qd": qd_r, "kd": kd_r, "qx": qx_r, "kx": kx_r, "qs": qs_r, "ku": ku_r}[nm]
                    ot, otk = ring.next()
                    S.op("pool" if nm in ("qs", "ku", "qx") else "dve", tt(ot[:], base[:], ex[:], ALU.mult), reads=[bk, exk], writes=[otk])
                    outs[nm] = (ot, otk)
                dec, deck = dec_r.next()
                S.op("dve", tt(dec[:], v3(Binc)[:, :, 31], v3(Eexc)[:, :, 0], ALU.subtract), reads=[Bck], writes=[deck])
                S.op("act", act(dec[:], dec[:], AF.Exp), reads=[deck], writes=[deck])
                ku, kuk = outs["ku"]
                bt, btk = pq(0, 2)
                S.op("pe", lambda e, bt=bt, ku=ku: e.transpose(bt, ku[:], ident_f), reads=[kuk, "cst"], writes=[btk])
                kut, kutk = kut_r.next()
                S.op("dve", tcopy(kut[:], bt), reads=[btk], writes=[kutk])
                qd, qdk = outs["qd"]
                kd, kdk = outs["kd"]
                qs, qsk = outs["qs"]
                ba, bak = pq(2, 4)
                S.op("pe", mm(ba, kd[:], qd[:], True, True), reads=[kdk, qdk], writes=[bak])
                am, amk = am_r.next()
                S.op("dve", tt(am[:], ba, maskf_f if d == 0 else maskbk_f, ALU.mult), reads=[bak, "cst"], writes=[amk])
                qx, qxk = outs["qx"]
                kx, kxk = outs["kx"]
                ba2, ba2k = pq(2, 4)
                S.op("pe", mm(ba2, kx[:], qx[:], True, True), reads=[kxk, qxk], writes=[ba2k])
                am2, am2k = am2_r.next()
                S.op("dve", tt(am2[:], ba2, maskxf_f if d == 0 else maskxb_f, ALU.mult), reads=[ba2k, "cst"], writes=[am2k])
                S.op("dve", tt(am[:], am[:], am2[:], ALU.add), reads=[amk, am2k], writes=[amk])
                bo, bok = pq(4, 6)
                S.op("pe", mm(bo, vb[:], am[:], True, True), reads=[vbk, amk], writes=[bok])
                of, ofk = of_r.next()
                S.op("dve", tcopy(of[:], bo), reads=[bok], writes=[ofk])
                bi, bik = pq(4, 6)
                subs = range(4) if d == 0 else range(3, -1, -1)
                for n_i, I in enumerate(subs):
                    last = n_i == 3
                    S.op("pe", mm(bi[:, I * 32:(I + 1) * 32], Sbf[:, d, mh, :], qs[:, I * 32:(I + 1) * 32], True, True),
                         reads=[("Sb", d, mh), qsk], writes=[bik])
                    bs_, bsk = pq(6, 8)
                    S.op("pe", lambda e, bs_=bs_, I=I, kut=kut, vb=vb: e.matmul(bs_, kut[I * 32:(I + 1) * 32, :], vb[I * 32:(I + 1) * 32, :],
                                                                           start=True, stop=True, tile_position=((I * 32), 0)),
                         reads=[kutk, vbk], writes=[bsk])
                    S.op("dve", stt(Sst[:, d, mh, :], Sst[:, d, mh, :], dec[:, I:I + 1], bs_, ALU.mult, ALU.add),
                         reads=[("S", d, mh), deck, bsk], writes=[("S", d, mh)])
                    S.op("act", act(Sbf[:, d, mh, :], Sst[:, d, mh, :], AF.Copy), reads=[("S", d, mh)], writes=[("Sb", d, mh)])
                endseg = (d == 0 and (t0 + 128) % SEG == 0) or (d == 1 and t0 % SEG == 0)
                if endseg:
                    seg = t0 // SEG
                    S.dma("pool", dmaf(st_out[seg, l, d, mh, :, :], Sst[:, d, mh, :]), reads=[("S", d, mh)], writes=[("sto", seg, d, mh)],
                          skey=("sst", d, mh % 4))
                    S.op("dve", tscal(Sst[:, d, mh, :], Sst[:, d, mh, :], keep[:, 0:1], None, ALU.mult), reads=[("S", d, mh), "keep"], writes=[("S", d, mh)])
                    S.op("act", act(Sbf[:, d, mh, :], Sst[:, d, mh, :], AF.Copy), reads=[("S", d, mh)], writes=[("Sb", d, mh)])
                if d == 0:
                    S.op("dve", tt(of[:], of[:], bi, ALU.add), reads=[ofk, bik], writes=[ofk])
                    S.dma("pool", dmaf(OFW[l][mh, :, t0:t0 + 128], of[:]), reads=[ofk], writes=[("ofw", mh, blk)], skey=ofk)
                else:
                    lo, lok = ldo.next()
                    S.dma("sp", dmaf(lo[:], OFW[l][mh, :, t0:t0 + 128]), reads=[("ofw", mh, blk)], writes=[lok], skey=(lok[0], lok[1] % 6))
                    o2, o2k = o2_r.next()
                    S.op("dve", tt(o2[:], of[:], bi, ALU.add), reads=[ofk, bik], writes=[o2k])
                    S.op("dve", tt(o2[:], o2[:], lo[:], ALU.add), reads=[o2k, lok], writes=[o2k])
                    o3, o3k = o3_r.next()
                    bn, bnk = pq(0, 2)
                    if isret:
                        S.op("pe", mm(bn, ones_f, o2[:], True, True), reads=[o2k, "cst"], writes=[bnk])
                        S.op("dve", stt(o2[:], bn, -1.0 / 128.0, o2[:], ALU.mult, ALU.add), reads=[bnk, o2k], writes=[o2k])
                    S.op("act", act(o3[:], o2[:], AF.Square), reads=[o2k], writes=[o3k])
                    S.op("pe", mm(bn, ones_f, o3[:], True, True), reads=[o3k, "cst"], writes=[bnk])
                    S.op("dve", tcopy(o3[:], bn), reads=[bnk], writes=[o3k])
                    rstd_ops(o3[:], o3[:], 128, [o3k], [o3k])
                    S.op("dve", stt(o2[:], o2[:], gheadc[:, l, mh:mh + 1], o3[:], ALU.mult, ALU.mult), reads=[o2k, o3k, "gheadc"], writes=[o2k])
                    gl, glk = ldg.next()
                    S.dma("sp", dmaf(gl[:], PFT[l][cfg.fm["rg" if isret else "hg"] + h, :, t0:t0 + 128]), writes=[glk], skey=(glk[0], glk[1] % 6))
                    S.op("act", act(gl[:], gl[:], AF.Silu if isret else AF.Sigmoid), reads=[glk], writes=[glk])
                    mx, mxk = mx_r.next()
                    S.op("dve", tt(mx[:], o2[:], gl[:], ALU.mult), reads=[o2k, glk], writes=[mxk])
                    S.dma("pool", dmaf(MIXT[l][mh, :, t0:t0 + 128], mx[:]), reads=[mxk], writes=[("mixt", mh, blk)], skey=mxk)

            def run_group(d, blk, heads):
                lists = []
                for ui, mh in enumerate(heads):
                    pqc["u"] = ui
                    S.cap = []
                    gla_block(d, mh, blk)
                    lists.append(S.cap)
                    S.cap = None
                S.replay_interleaved(lists)
                emit_casts(3)

            for blk in range(NB):
                for g0 in range(0, NMH, G):
                    run_group(0, blk, range(g0, g0 + G))
            for blk in range(NB - 1, -1, -1):
                for g0 in range(0, NMH, G):
                    run_group(1, blk, range(g0, g0 + G))
            emit_casts(len(cast_jobs))
            S.fence(final=(cfg.stop == 3))
        if cfg.stop == 3:
            es.close()
            return nc, S

        with ExitStack() as es2:
            TT = 256
            nblk = 2
            hT = sb("hT3", [128, max(KC, NS), TT], BF16)
            actT = sb("actT", [128, FC, TT], BF16)
            wring = Ring("w3", [sb("w3_%d" % i, [128, KT, 512], BF16) for i in range(3)])
            xb = [sb("xb%d" % i, [128, D]) for i in range(2)]
            zb = [sb("zb%d" % i, [128, D]) for i in range(2)]
            junk_t = sb("junk3", [128, D], BF16)
            gr_r = Ring("gr", [sb("gr%d" % i, [128, 512]) for i in range(2)])
            sl_r = Ring("sl", [sb("sl%d" % i, [128, 256]) for i in range(3)])
            ssq = sb("ssq3", [128, 1])
            rstd = sb("rstd3", [128, 1])
            ssqp = sb("ssqp", [128, 2, 16])
            NCG = D // 512 if D >= 512 else 1
            CW = min(512, D)

            def epilogue(which, cg, b, bank):
                gr, grk = gr_r.next()
                S.dma("sp", dmaf(gr[:, 0:CW], GROW[l, which:which + 1, cg * CW:(cg + 1) * CW].partition_broadcast(128)),
                      reads=[("grow", l)], writes=[grk], skey=grk)
                S.op("act", act(junk_t[:, 0:CW], ps[bank][:, 0:CW], AF.Square, accum_out=ssqp[:, b, cg:cg + 1]),
                     reads=[PK(bank), ("ssqp", b)], writes=[("ssqp", b), "junk"])
                S.op("dve", tt(zb[b][:, cg * CW:(cg + 1) * CW], ps[bank][:, 0:CW], gr[:, 0:CW], ALU.mult),
                     reads=[PK(bank), grk, ("ssqp", b)], writes=[("zb", b)])

            def finish(b):
                S.op("dve", tcopy(ssq[:, 0:1], ssqp[:, b, 0:1]), reads=[("ssqp", b)], writes=["ssq"])
                for cg_ in range(1, NCG):
                    S.op("dve", tt(ssq[:, 0:1], ssq[:, 0:1], ssqp[:, b, cg_:cg_ + 1], ALU.add), reads=[("ssqp", b), "ssq"], writes=["ssq"])
                rstd_ops(rstd[:, 0:1], ssq[:, 0:1], D, ["ssq"], ["rstd"])
                S.op("dve", stt(xb[b][:], zb[b][:], rstd[:, 0:1], xb[b][:], ALU.mult, ALU.add), reads=[("zb", b), "rstd", ("xb", b)], writes=[("xb", b)])

            pset = 0
            for ti in range(T // TT):
                r0 = ti * TT
                for s_ in range(NS):
                    S.dma("sp", dmaf(hT[:, s_, :], MIXT[l][s_, :, r0:r0 + TT]),
                          reads=[], writes=[("hT", 0), ("hT", 1)], skey=("mixld", s_ % 4))
                for b in range(nblk):
                    S.dma("sp", dmaf(xb[b][:], x_src[r0 + b * 128:r0 + (b + 1) * 128, :]), writes=[("xb", b)], skey=("xb", b))
                    S.op("dve", memset(ssqp[:, b, :], 0.0), writes=[("ssqp", b)])
                for cg in range(NCG):
                    banks = [pset * 4 + i for i in range(4)]
                    pset ^= 1
                    for k0 in range(0, NS, KT):
                        kn = min(KT, NS - k0)
                        wt, wk = load_w(wring, "out", l, k0, kn, cg * CW, CW)
                        for kk in range(kn):
                            kc = k0 + kk
                            for b in range(nblk):
                                S.op("pe", mm(ps[banks[b]][:, 0:CW], hT[:, kc, b * 128:(b + 1) * 128], wt[:, kk, 0:CW], kc == 0, kc == NS - 1),
                                     reads=[wk, ("hT", b)], writes=[PK(banks[b])])
                    for b in range(nblk):
                        epilogue(0, cg, b, banks[b])
                for b in range(nblk):
                    finish(b)
                    S.op("dve", memset(ssqp[:, b, :], 0.0), writes=[("ssqp", b)])
                    norm_transpose(xb[b], ("xb", b), b, hT, modc[:, l, 2, :], modc[:, l, 3, :], ssq, rstd, dst=zb[b], dk=("zb", b))
                if cfg.stop == 5:
                    continue
                for f0 in range(0, FC, 4):
                    nf = min(4, FC - f0)
                    banks = [pset * 4 + i for i in range(4)]
                    pset ^= 1
                    for half, coff in ((0, 0), (1, FF)):
                        for k0 in range(0, KC, KT):
                            kn = min(KT, KC - k0)
                            wt, wk = load_w(wring, "gu", l, k0, kn, coff + f0 * 128, nf * 128)
                            for kk in range(kn):
                                kc = k0 + kk
                                for j in range(nf):
                                    S.op("pe", mm(ps[banks[j]][:, half * TT:(half + 1) * TT], wt[:, kk, j * 128:(j + 1) * 128], hT[:, kc, :],
                                                  kc == 0, kc == KC - 1),
                                         reads=[wk, ("hT", 0), ("hT", 1)], writes=[PK(banks[j])])
                    for j in range(nf):
                        sl, slk = sl_r.next()
                        S.op("act", act(sl[:, 0:TT], ps[banks[j]][:, 0:TT], AF.Silu), reads=[PK(banks[j])], writes=[slk])
                        S.op("dve", tt(actT[:, f0 + j, :], sl[:, 0:TT], ps[banks[j]][:, TT:2 * TT], ALU.mult),
                             reads=[slk, PK(banks[j])], writes=[("actT", f0 + j)])
                if cfg.stop == 6:
                    continue
                for cg in range(NCG):
                    banks = [pset * 4 + i for i in range(4)]
                    pset ^= 1
                    for k0 in range(0, FC, KT):
                        kn = min(KT, FC - k0)
                        wt, wk = load_w(wring, "down", l, k0, kn, cg * CW, CW)
                        for kk in range(kn):
                            kc = k0 + kk
                            for b in range(nblk):
                                S.op("pe", mm(ps[banks[b]][:, 0:CW], actT[:, kc, b * 128:(b + 1) * 128], wt[:, kk, 0:CW], kc == 0, kc == FC - 1),
                                     reads=[wk, ("actT", kc)], writes=[PK(banks[b])])
                    for b in range(nblk):
                        epilogue(1, cg, b, banks[b])
                for b in range(nblk):
                    finish(b)
                    S.dma("pool", dmaf(x_dst[r0 + b * 128:r0 + (b + 1) * 128, :], xb[b][:]), reads=[("xb", b)],
                          writes=[("xdst", ti, b)], skey=("xst", b))
            S.fence(final=(l == DEPTH - 1 or cfg.stop in (5, 6, 9)))
        if cfg.stop in (5, 6, 9):
            break
    es.close()
    return nc, S


def _consts(cfg, is_sample):
    T, NB, NSEG, RH = cfg.T, cfg.NB, cfg.NSEG, cfg.RH
    ident = np.eye(128, dtype=np.float32)
    perm = np.zeros((128, 128), np.float32)
    for m in range(128):
        blk = m // 32
        partner = m + 32 if blk % 2 == 0 else m - 32
        perm[partner, m] = 1.0
    j = np.arange(128)[:, None]
    i = np.arange(128)[None, :]
    same = (j // 32) == (i // 32)
    same16 = (j // 16) == (i // 16)
    mf = (same16 & (j <= i)).astype(np.float32)
    mb = (same16 & (j >= i)).astype(np.float32)
    mxf = (same & ((j % 32) < 16) & ((i % 32) >= 16)).astype(np.float32)
    mxb = (same & ((j % 32) >= 16) & ((i % 32) < 16)).astype(np.float32)
    ones = np.ones((128, 128), np.float32)
    cst = np.concatenate([ident, perm, mf, mb, ones, mxf, mxb], axis=1)
    lg_f = np.log1p(-np.exp2(-5.0 - np.arange(RH, dtype=np.float32))).astype(np.float32)
    lg = np.concatenate([lg_f, lg_f[::-1]])
    lgam = np.broadcast_to(lg[None, :], (128, 2 * RH)).astype(np.float32).copy()
    NK = 2 + NB
    maskb = np.zeros((NSEG, NK), np.float32)
    NEG = -30000.0
    if not is_sample:
        maskb[:] = NEG
        for q in range(NSEG):
            maskb[q, 2 + 2 * q] = 0.0
            maskb[q, 2 + 2 * q + 1] = 0.0
    maskb = np.broadcast_to(maskb.reshape(1, -1), (128, NSEG * NK)).astype(np.float32).copy()
    cosT = np.ones((128, T), np.float32)
    sinT = np.zeros((128, T), np.float32)
    if is_sample:
        t = np.arange(T)
        row = (t // 64).astype(np.float32)
        colp = (t % 64).astype(np.float32)
        half = 64
        inv = (10000.0 ** (-np.arange(0, half, 2, dtype=np.float32) / half)).astype(np.float32)
        for p in range(128):
            f = p % 32
            ang = (row if p < 64 else colp) * inv[f]
            cosT[p] = np.cos(ang.astype(np.float32))
            s = np.sin(ang.astype(np.float32))
            sinT[p] = -s if (p // 32) % 2 == 0 else s
    return cst, lgam, maskb, cosT, sinT


_CACHE = {}


def _run(cfg, inputs, n_sample, n_prompt_cores):
    D, T, RH = cfg.D, cfg.T, cfg.RH
    key = (D, T)
    if key not in _CACHE:
        _CACHE[key] = build(cfg)
    nc, _ = _CACHE[key]
    print("built: instructions", _CACHE[key][1].nins, flush=True)
    f = lambda a: np.ascontiguousarray(np.asarray(a), dtype=np.float32)
    w = {k: f(inputs[k]) for k in ("w_mod", "w_in", "w_out", "w_gu", "w_down")}
    cs, cp = _consts(cfg, True), _consts(cfg, False)
    gvec = np.stack([f(inputs["g_pre_mix"]), f(inputs["g_post_mix"]), f(inputs["g_pre_ffn"]), f(inputs["g_post_ffn"])], axis=1)
    gqk = np.stack([f(inputs["g_q"]), f(inputs["g_k"])], axis=1)
    ghead = np.concatenate([f(inputs["g_ret"]), f(inputs["g_hgrn"])], axis=1)
    in_maps = []
    ncores = n_sample + n_prompt_cores
    for c in range(ncores):
        m = {}
        for k in ("w_mod", "w_in", "w_out", "w_gu", "w_down"):
            m[k] = w[k]
        m["b_mod"] = f(inputs["b_mod"])
        m["gvec"], m["gqk"], m["ghead"], m["hg_lb"] = gvec, gqk, ghead, f(inputs["hg_lb"])
        if c < n_sample:
            cst, lgam, maskb, cosT, sinT = cs
            m["x"] = f(inputs["x_sample"][c])
            m["cond"] = f(inputs["c"][c]).reshape(1, D)
            m["ctxk"] = f(inputs["cache_k"][c]).reshape(DEPTH, PAST, cfg.KW)
            m["ctxv"] = f(inputs["cache_v"][c]).reshape(DEPTH, PAST, cfg.KW)
            sr, sh = f(inputs["state_ret"][c]), f(inputs["state_hgrn"][c])
            m["s0"] = np.ascontiguousarray(np.concatenate([sr, sh], axis=2))
            m["keep"] = np.ones((128, 1), np.float32)
        else:
            cst, lgam, maskb, cosT, sinT = cp
            pc = c - n_sample
            if pc < n_prompt_cores:
                m["x"] = f(inputs["x_prompt"][pc * cfg.NSEG:(pc + 1) * cfg.NSEG]).reshape(T, D)
            else:
                m["x"] = np.zeros((T, D), np.float32)
            m["cond"] = f(inputs["c_ctx"]).reshape(1, D)
            m["ctxk"] = np.zeros((DEPTH, PAST, cfg.KW), np.float32)
            m["ctxv"] = np.zeros((DEPTH, PAST, cfg.KW), np.float32)
            m["s0"] = np.zeros((DEPTH, 2, 2 * RH, 128, 128), np.float32)
            m["keep"] = np.zeros((128, 1), np.float32)
        m["cst"], m["lgam"], m["maskb"], m["cosT"], m["sinT"] = cst, lgam, maskb, cosT, sinT
        in_maps.append(m)
    res = run_bass_kernel_spmd(nc, in_maps, core_ids=list(range(ncores)))
    R = res.results
    y_sample = np.stack([R[c]["y"] for c in range(n_sample)], axis=0)
    pcs = range(n_sample, n_sample + n_prompt_cores)
    y_prompt = np.concatenate([R[c]["y"].reshape(cfg.NSEG, SEG, D) for c in pcs], axis=0)
    ck = np.concatenate([R[c]["ck"] for c in pcs], axis=0).reshape(-1, DEPTH, SEG, cfg.KVH, 128)
    cv = np.concatenate([R[c]["cv"] for c in pcs], axis=0).reshape(-1, DEPTH, SEG, cfg.KVH, 128)
    st = np.concatenate([R[c]["st"] for c in pcs], axis=0)
    return (y_prompt, y_sample, ck, cv, np.ascontiguousarray(st[:, :, :, :RH]), np.ascontiguousarray(st[:, :, :, RH:]))


def kernel(**inputs):
    cfg = Cfg(4096, 4096)
    return _run(cfg, inputs, 4, 2)
```
